# Optimizing a Trainium2 kernel written in Bass

```python
import jax, jax.numpy as jnp
from jax import lax
import numpy as np

D_MODEL = 1024
BATCH = 8
SEQ = 2048
DEPTH = 2
DEC_BATCH = 128
DEC_SEQ = 1
PAST_LEN = 2048
PAGE_SIZE = 128

HEAD_DIM = 64
H_MOBA = 6
H_SB = 6
H_ATT = H_MOBA + H_SB
D_MOBA = H_MOBA * HEAD_DIM
D_SB = H_SB * HEAD_DIM
D_CONV = D_MODEL - D_MOBA - D_SB
D_IN = 3 * D_MOBA + 2 * D_CONV + 3 * D_SB
CONV_WIDTH = 31
MOBA_BLOCK = 256
MOBA_TOPK = 3
MOBA_Q_BLOCK = 16
SB_Q_BLOCK = 128
ROPE_THETA = 500000.0
ROT_DIM = HEAD_DIM // 4
D_FF = 4 * D_MODEL
RMS_EPS = 1e-6

kernel_name = "hymba_moba_conformer_stickbreak_decode"

SPLITS = list(np.cumsum([D_MOBA, D_MOBA, D_MOBA, D_CONV, D_CONV, D_SB, D_SB]))


def rmsnorm(x, g):
    xf = x.astype(jnp.float32)
    y = xf * lax.rsqrt(jnp.mean(xf * xf, axis=-1, keepdims=True) + RMS_EPS)
    return (y * g.astype(jnp.float32)).astype(x.dtype)


def rope_partial(x, pos):
    half = ROT_DIM // 2
    inv_freq = ROPE_THETA ** (-jnp.arange(half, dtype=jnp.float32) / half)
    ang = pos.astype(jnp.float32)[:, None] * inv_freq[None, :]
    cos = jnp.cos(ang)[None, :, None, :].astype(x.dtype)
    sin = jnp.sin(ang)[None, :, None, :].astype(x.dtype)
    x1, x2, rest = x[..., :half], x[..., half:ROT_DIM], x[..., ROT_DIM:]
    return jnp.concatenate([x1 * cos - x2 * sin, x2 * cos + x1 * sin, rest], axis=-1)


def sweep_queries(fn, q, q_pos, block):
    b, sq, h, d = q.shape
    if sq <= block:
        return fn(q, q_pos)
    nq = sq // block
    qs = q.reshape(b, nq, block, h, d).transpose(1, 0, 2, 3, 4)
    ps = q_pos.reshape(nq, block)
    out = lax.map(lambda a: fn(a[0], a[1]), (qs, ps))
    return out.transpose(1, 0, 2, 3, 4).reshape(b, sq, h, out.shape[-1])


def moba_attention(q, q_pos, k, v):
    b, sk, h, d = k.shape
    nb = -(-sk // MOBA_BLOCK)
    pad = nb * MOBA_BLOCK - sk
    k_bh = jnp.pad(k, ((0, 0), (0, pad), (0, 0), (0, 0))).reshape(b, nb, MOBA_BLOCK, h, d).transpose(0, 3, 1, 2, 4)
    v_bh = jnp.pad(v, ((0, 0), (0, pad), (0, 0), (0, 0))).reshape(b, nb, MOBA_BLOCK, h, d).transpose(0, 3, 1, 2, 4)
    k_mean = jnp.mean(k_bh.astype(jnp.float32), axis=3)
    n_sel = min(MOBA_TOPK, nb)
    bi = jnp.arange(b)[:, None, None, None]
    hi = jnp.arange(h)[None, None, :, None]
    blk_ids = jnp.arange(nb)
    offs = jnp.arange(MOBA_BLOCK)

    def block_fn(qb, pb):
        qblk = pb // MOBA_BLOCK
        gate = jnp.einsum('bqhd,bhnd->bqhn', qb.astype(jnp.float32), k_mean)
        is_past = blk_ids[None, None, None, :] < qblk[None, :, None, None]
        gate = jnp.where(is_past, gate, -jnp.inf)
        _, sel = lax.top_k(gate, n_sel)
        sel_valid = sel < qblk[None, :, None, None]
        own = jnp.broadcast_to(qblk[None, :, None, None], sel.shape[:3] + (1,)).astype(sel.dtype)
        blocks = jnp.concatenate([sel, own], axis=-1)
        valid = jnp.concatenate([sel_valid, jnp.ones(own.shape, dtype=bool)], axis=-1)
        kg = k_bh[bi, hi, blocks]
        vg = v_bh[bi, hi, blocks]
        s = jnp.einsum('bqhd,bqhnkd->bqhnk', qb, kg).astype(jnp.float32) * (HEAD_DIM ** -0.5)
        kpos = blocks[..., None] * MOBA_BLOCK + offs
        mask = valid[..., None] & (kpos <= pb[None, :, None, None, None])
        s = jnp.where(mask, s, -jnp.inf)
        bq = s.shape[1]
        p = jax.nn.softmax(s.reshape(b, bq, h, -1), axis=-1).reshape(s.shape)
        return jnp.einsum('bqhnk,bqhnkd->bqhd', p.astype(vg.dtype), vg)

    return sweep_queries(block_fn, q, q_pos, MOBA_Q_BLOCK)


def stick_breaking_attention(q, q_pos, k, v):
    k_pos = jnp.arange(k.shape[1])

    def block_fn(qb, pb):
        z = jnp.einsum('bqhd,bkhd->bhqk', qb, k).astype(jnp.float32) * (HEAD_DIM ** -0.5)
        causal = (k_pos[None, :] < pb[:, None])[None, None]
        log_stay = jnp.where(causal, jax.nn.log_sigmoid(-z), 0.0)
        later = lax.cumsum(log_stay, axis=3, reverse=True) - log_stay
        a = jnp.where(causal, jnp.exp(jax.nn.log_sigmoid(z) + later), 0.0)
        return jnp.einsum('bhqk,bkhd->bqhd', a.astype(v.dtype), v)

    return sweep_queries(block_fn, q, q_pos, SB_Q_BLOCK)


def conformer_conv(u_val, u_gate, conv_state, w_dw, b_dw, g_norm):
    u = u_val * jax.nn.sigmoid(u_gate)
    u_ext = jnp.concatenate([conv_state, u], axis=1)
    y = lax.conv_general_dilated(u_ext, w_dw[:, None, :], window_strides=(1,), padding='VALID',
                                 dimension_numbers=('NWC', 'WIO', 'NWC'),
                                 feature_group_count=u.shape[-1]) + b_dw
    y = jax.nn.silu(rmsnorm(y, g_norm))
    return y, u_ext[:, -(CONV_WIDTH - 1):]


def run_group(x, pos, past_k, past_v, conv_state, w_in, w_out, conv_w, conv_b, conv_g,
              w_up, w_down, g_pre_mix, g_post_mix, g_pre_mlp, g_post_mlp):
    b, s, _ = x.shape
    h = x
    new_k, new_v, new_conv = [], [], []
    for l in range(DEPTH):
        a = rmsnorm(h, g_pre_mix[l])
        proj = a @ w_in[l]
        q_a, k_a, v_a, u_val, u_gate, q_c, k_c, v_c = jnp.split(proj, SPLITS, axis=-1)
        q_a = rope_partial(q_a.reshape(b, s, H_MOBA, HEAD_DIM), pos)
        k_a = rope_partial(k_a.reshape(b, s, H_MOBA, HEAD_DIM), pos)
        v_a = v_a.reshape(b, s, H_MOBA, HEAD_DIM)
        q_c = q_c.reshape(b, s, H_SB, HEAD_DIM)
        k_c = k_c.reshape(b, s, H_SB, HEAD_DIM)
        v_c = v_c.reshape(b, s, H_SB, HEAD_DIM)
        k_rows = jnp.concatenate([k_a, k_c], axis=2)
        v_rows = jnp.concatenate([v_a, v_c], axis=2)
        new_k.append(k_rows)
        new_v.append(v_rows)
        if past_k is None:
            k_all, v_all = k_rows, v_rows
        else:
            k_all = jnp.concatenate([past_k[l], k_rows], axis=1)
            v_all = jnp.concatenate([past_v[l], v_rows], axis=1)
        o_a = moba_attention(q_a, pos, k_all[:, :, :H_MOBA], v_all[:, :, :H_MOBA])
        o_c = stick_breaking_attention(q_c, pos, k_all[:, :, H_MOBA:], v_all[:, :, H_MOBA:])
        o_b, cs = conformer_conv(u_val, u_gate, conv_state[l], conv_w[l], conv_b[l], conv_g[l])
        new_conv.append(cs)
        mix = jnp.concatenate([o_a.reshape(b, s, D_MOBA), o_b, o_c.reshape(b, s, D_SB)], axis=-1) @ w_out[l]
        h = h + rmsnorm(mix, g_post_mix[l])
        m = rmsnorm(h, g_pre_mlp[l])
        f = jnp.square(jax.nn.relu(m @ w_up[l])) @ w_down[l]
        h = h + rmsnorm(f, g_post_mlp[l])
    return h, jnp.stack(new_k), jnp.stack(new_v), jnp.stack(new_conv)


def setup_inputs(seed: int = 0) -> dict:
    key = jax.random.key(seed)
    ks = jax.random.split(key, 20)
    n_pages = PAST_LEN // PAGE_SIZE
    n_pool = (DEC_BATCH * n_pages * 5) // 4
    nrm = jax.random.normal
    x_prompt = nrm(ks[0], (BATCH, SEQ, D_MODEL), jnp.float32)
    x_sample = nrm(ks[1], (DEC_BATCH, DEC_SEQ, D_MODEL), jnp.float32)
    cache_k = nrm(ks[2], (DEPTH, n_pool, PAGE_SIZE, H_ATT, HEAD_DIM), jnp.float32)
    cache_v = nrm(ks[3], (DEPTH, n_pool, PAGE_SIZE, H_ATT, HEAD_DIM), jnp.float32)
    state_conv = 0.5 * nrm(ks[4], (DEPTH, DEC_BATCH, CONV_WIDTH - 1, D_CONV), jnp.float32)
    page_table = jax.random.permutation(ks[5], n_pool)[:DEC_BATCH * n_pages].reshape(DEC_BATCH, n_pages).astype(jnp.int32)
    w_in = nrm(ks[6], (DEPTH, D_MODEL, D_IN), jnp.float32) * D_MODEL ** -0.5
    w_out = nrm(ks[7], (DEPTH, D_MODEL, D_MODEL), jnp.float32) * D_MODEL ** -0.5
    conv_w = nrm(ks[8], (DEPTH, CONV_WIDTH, D_CONV), jnp.float32) * CONV_WIDTH ** -0.5
    conv_b = 0.01 * nrm(ks[9], (DEPTH, D_CONV), jnp.float32)
    conv_g = 1.0 + 0.05 * nrm(ks[10], (DEPTH, D_CONV), jnp.float32)
    w_up = nrm(ks[11], (DEPTH, D_MODEL, D_FF), jnp.float32) * D_MODEL ** -0.5
    w_down = nrm(ks[12], (DEPTH, D_FF, D_MODEL), jnp.float32) * D_FF ** -0.5
    g_pre_mix = 1.0 + 0.05 * nrm(ks[13], (DEPTH, D_MODEL), jnp.float32)
    g_post_mix = 1.0 + 0.05 * nrm(ks[14], (DEPTH, D_MODEL), jnp.float32)
    g_pre_mlp = 1.0 + 0.05 * nrm(ks[15], (DEPTH, D_MODEL), jnp.float32)
    g_post_mlp = 1.0 + 0.05 * nrm(ks[16], (DEPTH, D_MODEL), jnp.float32)
    return {"x_prompt": x_prompt, "x_sample": x_sample, "cache_k": cache_k, "cache_v": cache_v,
            "state_conv": state_conv, "page_table": page_table, "w_in": w_in, "w_out": w_out,
            "conv_w": conv_w, "conv_b": conv_b, "conv_g": conv_g, "w_up": w_up, "w_down": w_down,
            "g_pre_mix": g_pre_mix, "g_post_mix": g_post_mix, "g_pre_mlp": g_pre_mlp, "g_post_mlp": g_post_mlp}


def reference(x_prompt, x_sample, cache_k, cache_v, state_conv, page_table, w_in, w_out,
              conv_w, conv_b, conv_g, w_up, w_down, g_pre_mix, g_post_mix, g_pre_mlp, g_post_mlp):
    weights = (w_in, w_out, conv_w, conv_b, conv_g, w_up, w_down, g_pre_mix, g_post_mix, g_pre_mlp, g_post_mlp)
    b_p, s_p, _ = x_prompt.shape
    pos_p = jnp.arange(s_p, dtype=jnp.int32)
    conv0 = jnp.zeros((DEPTH, b_p, CONV_WIDTH - 1, D_CONV), x_prompt.dtype)
    y_prompt, k_rows_prompt, v_rows_prompt, conv_prompt = run_group(x_prompt, pos_p, None, None, conv0, *weights)
    b_s, n_pages = page_table.shape
    past_len = n_pages * PAGE_SIZE
    past_k = [cache_k[l][page_table].reshape(b_s, past_len, H_ATT, HEAD_DIM) for l in range(DEPTH)]
    past_v = [cache_v[l][page_table].reshape(b_s, past_len, H_ATT, HEAD_DIM) for l in range(DEPTH)]
    pos_s = past_len + jnp.arange(x_sample.shape[1], dtype=jnp.int32)
    y_sample, k_rows_sample, v_rows_sample, conv_sample = run_group(x_sample, pos_s, past_k, past_v, state_conv, *weights)
    return (y_prompt, y_sample, k_rows_prompt, v_rows_prompt, conv_prompt, k_rows_sample, v_rows_sample, conv_sample)
```

```python
import contextlib
import numpy as np
import concourse.bass as bass
import concourse.mybir as mybir
from concourse.bass_utils import run_bass_kernel_spmd

F32 = mybir.dt.float32
BF16 = mybir.dt.bfloat16
I32 = mybir.dt.int32
AF = mybir.ActivationFunctionType
ALU = mybir.AluOpType
AX = mybir.AxisListType

D = 1024
DIN = 2816
DFF = 4096
HD = 64
NEG = -30000.0
EPS = 1e-6
QA, KA, VA, UV, UG, QC, KC, VC = 0, 384, 768, 1152, 1408, 1664, 2048, 2432


class Lane:
    __slots__ = ("sem", "val", "inc")

    def __init__(self, sem, inc):
        self.sem, self.val, self.inc = sem, 0, inc


class Buf:
    __slots__ = ("w", "r", "excl")

    def __init__(self, excl=False):
        self.w = None
        self.r = {}
        self.excl = excl


class FW:
    NDMA = 8

    def __init__(self, nc, es):
        self.nc = nc
        self.engs = {"pe": nc.tensor, "act": nc.scalar, "dve": nc.vector, "pool": nc.gpsimd, "sp": nc.sync}
        self.lanes = {k: Lane(es.enter_context(nc.semaphore("s_" + k)), 1) for k in ("pe", "act", "dve", "pool")}
        self.dlanes = {q: [Lane(es.enter_context(nc.semaphore(f"d_{q}{i}")), 16) for i in range(self.NDMA)]
                       for q in ("sp", "pool", "act")}
        self.rr = {"sp": 0, "pool": 0, "act": 0}
        self.known = {k: {} for k in self.engs}
        self.n_inst = 0

    def _wait(self, issuer, lane, val):
        k = self.known[issuer]
        if k.get(lane, 0) < val:
            self.engs[issuer].wait_ge(lane.sem, val)
            k[lane] = val

    def _deps(self, issuer, reads, writes, own=None):
        for b in reads:
            if b.w is not None and b.w[0] is not own:
                self._wait(issuer, b.w[0], b.w[1])
        for b in writes:
            if b.w is not None and b.w[0] is not own:
                self._wait(issuer, b.w[0], b.w[1])
            for lane, val in b.r.items():
                if lane is not own:
                    self._wait(issuer, lane, val)

    @staticmethod
    def _commit(lane, val, reads, writes):
        for b in writes:
            b.w = (lane, val)
            b.r = {}
        for b in reads:
            if b.r.get(lane, 0) < val:
                b.r[lane] = val

    def op(self, eng, fn, reads=(), writes=(), inc=True):
        lane = self.lanes[eng]
        if any(b.excl for b in reads):
            writes = list(writes) + [b for b in reads if b.excl]
            reads = [b for b in reads if not b.excl]
        self._deps(eng, reads, writes, own=lane if eng == "pe" else None)
        inst = fn()
        self.n_inst += 1
        if inc:
            lane.val += 1
            inst.then_inc(lane.sem, 1)
            self._commit(lane, lane.val, reads, writes)
        else:
            self._commit(lane, lane.val + 1, reads, writes)
        return inst

    def dma(self, q, out, in_, reads=(), writes=(), **kw):
        lanes = self.dlanes[q]
        lane = lanes[self.rr[q] % self.NDMA]
        self.rr[q] += 1
        self._deps(q, reads, writes)
        self._wait(q, lane, lane.val)
        lane.val += 16
        self.engs[q].dma_start(out=out, in_=in_, **kw).then_inc(lane.sem, 16)
        self.n_inst += 1
        self._commit(lane, lane.val, reads, writes)

    def gather(self, out, in_, idx_ap, reads=(), writes=(), element_offset=0):
        q = "pool"
        lanes = self.dlanes[q]
        lane = lanes[self.rr[q] % self.NDMA]
        self.rr[q] += 1
        self._deps(q, reads, writes)
        self._wait(q, lane, lane.val)
        lane.val += 16
        self.nc.gpsimd.indirect_dma_start(out=out, out_offset=None, in_=in_,
                                          in_offset=bass.IndirectOffsetOnAxis(ap=idx_ap, axis=0),
                                          element_offset=element_offset).then_inc(lane.sem, 16)
        self.n_inst += 1
        self._commit(lane, lane.val, reads, writes)

    def barrier(self):
        all_l = list(self.lanes.values()) + [l for ls in self.dlanes.values() for l in ls]
        for issuer in ("pe", "act", "dve", "pool", "sp"):
            for l in all_l:
                if l.val > 0 and not (issuer in self.lanes and self.lanes[issuer] is l):
                    self._wait(issuer, l, l.val)

    def finish(self):
        for ls in self.dlanes.values():
            for l in ls:
                if l.val > 0:
                    self._wait("sp", l, l.val)
        for l in self.lanes.values():
            if l.val > 0:
                self._wait("sp", l, l.val)


def emit_pipelined(items, skew=1):
    n = len(items)
    K = max(len(it) for it in items) if items else 0
    for slot in range(n + (K - 1) * skew):
        for k in range(K):
            i = slot - k * skew
            if 0 <= i < n and k < len(items[i]):
                items[i][k]()


class T:
    def __init__(self, ap, nb=1, excl=False):
        self.ap = ap
        self.b = [Buf(excl) for _ in range(nb)]
        self.b0 = self.b[0]

    def __getitem__(self, k):
        return self.ap[k]


def make_consts(NT, past_len):
    c = {}
    c["ident"] = np.eye(128, dtype=np.float32)
    j = np.arange(128)
    c["trineg"] = -(j[:, None] >= j[None, :]).astype(np.float32)
    seln = np.zeros((128, 16, 128), np.float32)
    for kj in range(16):
        seln[:, kj, kj] = -1.0
    c["selneg"] = seln.reshape(128, 2048)
    c["su16"] = (np.arange(16)[:, None] > np.arange(16)[None, :]).astype(np.float32)
    selb = np.zeros((128, 16, 128), np.float32)
    for kj in range(16):
        selb[kj, kj, :] = 1.0
    c["selb"] = selb.reshape(128, 2048)
    e48 = np.zeros((128, 48, 128), np.float32)
    for r in range(48):
        e48[r, r, :] = 1.0
    c["e48"] = e48.reshape(128, 48 * 128)
    p = np.arange(128)[:, None]
    f = np.arange(512)[None, :]
    cm = np.zeros((128, 8, 512), np.float32)
    for r in range(4):
        cm[:, r, :] = np.where((128 * r + p) < f, 0.0, NEG)
        cm[:, 4 + r, :] = np.where((128 * r + p) <= f, 0.0, NEG)
    c["cmask"] = cm.reshape(128, 8 * 512)
    half = 8
    inv_freq = (np.float32(500000.0) ** (-np.arange(half, dtype=np.float32) / np.float32(half))).astype(np.float32)
    pos = np.zeros((128, NT + 1), np.float32)
    for t in range(NT):
        pos[:, t] = t * 128 + np.arange(128)
    pos[:, NT] = past_len
    ang = pos[:, :, None].astype(np.float32) * inv_freq[None, None, :]
    cs, sn = np.cos(ang).astype(np.float32), np.sin(ang).astype(np.float32)
    c["ropec"] = np.concatenate([cs, cs], axis=2).reshape(128, (NT + 1) * 16)
    c["ropes"] = np.concatenate([sn, sn], axis=2).reshape(128, (NT + 1) * 16)
    gb = np.zeros((128, 8, 8), np.float32)
    for qb in range(8):
        for n in range(8):
            gb[:, qb, n] = 0.0 if n < qb else (1e30 if n == qb else -1e30)
    c["gbias"] = gb.reshape(128, 64)
    c["piota"] = np.arange(128, dtype=np.float32).reshape(128, 1)
    hs = np.zeros((128, 128), np.float32)
    hs[0:64, 0:64] = 1.0
    hs[64:128, 64:128] = 1.0
    c["hsel"] = hs
    return c


CONST_SHAPES = lambda NT: {"ident": [128, 128], "trineg": [128, 128], "selneg": [128, 2048], "su16": [16, 16],
                           "selb": [128, 2048], "e48": [128, 6144], "cmask": [128, 4096],
                           "ropec": [128, (NT + 1) * 16], "ropes": [128, (NT + 1) * 16], "gbias": [128, 64],
                           "piota": [128, 1], "hsel": [128, 128]}


class KB:
    def __init__(self, NT, NS, NPG, NPOOL, depth=2):
        self.NT, self.NS, self.NPG, self.NPOOL, self.depth = NT, NS, NPG, NPOOL, depth
        self.NTOK = NT * 128 + NS
        self.nc = bass.Bass("TRN2", target_bir_lowering=False)
        self.es = contextlib.ExitStack()
        self.fw = FW(self.nc, self.es)
        self.cnt = 0

    def dram(self, name, shape, dt=F32, kind="ExternalInput"):
        return self.nc.dram_tensor(name, list(shape), dt, kind=kind).ap()

    def sb(self, es, name, shape, dt=F32, nb=1):
        self.cnt += 1
        return T(es.enter_context(self.nc.sbuf_tensor(f"{name}_{self.cnt}", list(shape), dt)), nb)

    def ps(self, es, name, shape=(128, 512), dt=F32):
        self.cnt += 1
        return T(es.enter_context(self.nc.psum_tensor(f"{name}_{self.cnt}", list(shape), dt)), 1, excl=True)

    def rows(self, t):
        return 128 if t < self.NT else self.NS


def build(NT=16, NS=16, NPG=16, NPOOL=2560, depth=2, do_sample=True):
    import os
    UPTO = int(os.environ.get('UPTO', '99'))
    kb = KB(NT, NS, NPG, NPOOL, depth)
    nc, fw, es = kb.nc, kb.fw, kb.es
    NTOK = kb.NTOK
    NB = NT // 4
    SP = NT * 128
    x_p = kb.dram("x_p", [SP, D]); x_s = kb.dram("x_s", [NS, D])
    ck = kb.dram("ck", [depth * NPOOL * 128, 768]); cv = kb.dram("cv", [depth * NPOOL * 128, 768])
    sconv = kb.dram("sconv", [depth, NS * 30, 256]); ptab = kb.dram("ptab", [NS * NPG], I32)
    w_in = kb.dram("w_in", [depth, D, DIN]); w_out = kb.dram("w_out", [depth, D, D])
    w_up = kb.dram("w_up", [depth, D, DFF]); w_down = kb.dram("w_down", [depth, DFF, D])
    conv_w = kb.dram("conv_w", [depth, 31, 256]); conv_b = kb.dram("conv_b", [depth, 256]); conv_g = kb.dram("conv_g", [depth, 256])
    gvec = {n: kb.dram(n, [depth, D]) for n in ("g_pre_mix", "g_post_mix", "g_pre_mlp", "g_post_mlp")}
    cdram = {n: kb.dram("c_" + n, shp) for n, shp in CONST_SHAPES(NT).items()}
    y_p = kb.dram("y_p", [SP, D], kind="ExternalOutput"); y_s = kb.dram("y_s", [NS, D], kind="ExternalOutput")
    kr_p = kb.dram("kr_p", [depth, SP, 768], kind="ExternalOutput"); vr_p = kb.dram("vr_p", [depth, SP, 768], kind="ExternalOutput")
    cv_p = kb.dram("cv_p", [depth, 30, 256], kind="ExternalOutput")
    kr_s = kb.dram("kr_s", [depth, NS, 768], kind="ExternalOutput"); vr_s = kb.dram("vr_s", [depth, NS, 768], kind="ExternalOutput")
    cv_s = kb.dram("cv_s", [depth, NS * 30, 256], kind="ExternalOutput")
    hbuf = kb.dram("hbuf", [SP + 128, D], kind="Internal")
    hb = [Buf() for _ in range(NT + 1)]
    outb = Buf()

    def hsrc(l, t):
        R = kb.rows(t)
        if l == 0:
            return x_p[t * 128:(t + 1) * 128, :] if t < NT else x_s[0:NS, :]
        return hbuf[t * 128:t * 128 + R, :]

    def hdst(l, t, final):
        R = kb.rows(t)
        if final:
            return y_p[t * 128:(t + 1) * 128, :] if t < NT else y_s[0:NS, :]
        return hbuf[t * 128:t * 128 + R, :]

    cst = {}
    for n, dt in (("ident", F32), ("su16", F32), ("ropec", F32), ("ropes", F32), ("gbias", F32), ("hsel", F32)):
        cst[n] = kb.sb(es, "c_" + n, CONST_SHAPES(NT)[n], dt)
        fw.dma("sp", cst[n].ap[:], cdram[n][:, :], writes=[cst[n].b0])
    for n in ("ident", "trineg", "selneg", "selb", "e48", "cmask"):
        cst[n + "_b"] = kb.sb(es, "cb_" + n, CONST_SHAPES(NT)[n], BF16)
        fw.dma("pool", cst[n + "_b"].ap[:], cdram[n][:, :], writes=[cst[n + "_b"].b0])
    ident_f, ident_b = cst["ident"], cst["ident_b"]
    ones_b = kb.sb(es, "ones_b", [128, 128], BF16)
    fw.op("dve", lambda: nc.vector.memset(ones_b.ap[:], 1.0), writes=[ones_b.b0])
    ones_f = kb.sb(es, "ones_f", [128, 128], F32)
    fw.op("dve", lambda: nc.vector.memset(ones_f.ap[:], 1.0), writes=[ones_f.b0])
    cvals = kb.sb(es, "cvals", [128, 4], F32)
    fw.op("dve", lambda: nc.vector.memset(cvals.ap[:, 0:1], 1.0), writes=[cvals.b0])
    fw.op("dve", lambda: nc.vector.memset(cvals.ap[:, 1:2], -0.5), writes=[cvals.b0])
    fw.op("dve", lambda: nc.vector.memset(cvals.ap[:, 2:3], EPS), writes=[cvals.b0])
    mhalf = kb.sb(es, "mhalf", [128, 512], F32)
    fw.op("pool", lambda: nc.gpsimd.memset(mhalf.ap[:], -0.5), writes=[mhalf.b0])
    idx = kb.sb(es, "idx", [128, NS * NPG], I32)
    idx_f = kb.sb(es, "idx_f", [128, NS * NPG], F32)
    piota = kb.sb(es, "piota", [128, 1], F32)
    fw.dma("sp", piota.ap[:], cdram["piota"][:, :], writes=[piota.b0])
    fw.dma("sp", idx.ap[:], ptab.partition_broadcast(128), writes=[idx.b0])
    fw.op("dve", lambda: nc.vector.tensor_copy(out=idx_f.ap[:], in_=idx.ap[:]), reads=[idx.b0], writes=[idx_f.b0])
    fw.op("dve", lambda: nc.vector.tensor_scalar(out=idx_f.ap[:], in0=idx_f.ap[:], scalar1=128.0, scalar2=piota.ap[:, 0:1], op0=ALU.mult, op1=ALU.add),
          reads=[idx_f.b0, piota.b0], writes=[idx_f.b0])
    fw.op("dve", lambda: nc.vector.tensor_copy(out=idx.ap[:], in_=idx_f.ap[:]), reads=[idx_f.b0], writes=[idx.b0])
    ev = [0]

    def evac(out, in_, reads, writes, scale=None):
        ev[0] += 1
        if ev[0] % 2 == 0:
            if scale is None:
                fw.op("act", lambda: nc.scalar.copy(out=out, in_=in_), reads=reads, writes=writes)
            else:
                fw.op("act", lambda: nc.scalar.activation(out=out, in_=in_, func=AF.Identity, scale=scale), reads=reads, writes=writes)
        else:
            if scale is None:
                fw.op("dve", lambda: nc.vector.tensor_copy(out=out, in_=in_), reads=reads, writes=writes)
            else:
                fw.op("dve", lambda: nc.vector.tensor_scalar(out=out, in0=in_, scalar1=scale, scalar2=None, op0=ALU.mult), reads=reads, writes=writes)

    def rstd_from_ssq(ssq_ap, ssq_b, R, n, tmp, rs):
        fw.op("dve", lambda: nc.vector.tensor_scalar(out=tmp.ap[0:R, 0:1], in0=ssq_ap, scalar1=1.0 / n, scalar2=EPS, op0=ALU.mult, op1=ALU.add),
              reads=[ssq_b], writes=[tmp.b0])
        fw.op("pool", lambda: nc.gpsimd.tensor_tensor(out=rs.ap[0:R, 0:1], in0=tmp.ap[0:R, 0:1], in1=cvals.ap[0:R, 1:2], op=ALU.pow),
              reads=[tmp.b0, cvals.b0], writes=[rs.b0])

    def norm_to_T(l, t, hT, gbc, XT, trb, pes_tmps):
        R = kb.rows(t)
        junk, ssq, tmp1, rs, abf = pes_tmps
        fw.op("act", lambda: nc.scalar.activation(out=junk.ap[0:R, :], in_=hT.ap[0:R, :], func=AF.Square, accum_out=ssq.ap[0:R, 0:1]),
              reads=[hT.b0], writes=[junk.b0, ssq.b0])
        rstd_from_ssq(ssq.ap[0:R, 0:1], ssq.b0, R, D, tmp1, rs)
        fw.op("dve", lambda: nc.vector.scalar_tensor_tensor(out=abf.ap[0:R, :], in0=hT.ap[0:R, :], scalar=rs.ap[0:R, 0:1], in1=gbc.ap[0:R, :],
                                                            op0=ALU.mult, op1=ALU.mult), reads=[hT.b0, rs.b0, gbc.b0], writes=[abf.b0])
        for c in range(8):
            fw.op("pe", lambda c=c: nc.tensor.transpose(out=trb.ap[:, c * 128:c * 128 + R], in_=abf.ap[0:R, c * 128:(c + 1) * 128],
                                                        identity=ident_b.ap[0:R, 0:R]), reads=[abf.b0, ident_b.b0], writes=[trb.b0], inc=(c == 7))
        evac(XT.ap[:, :, t * 128:t * 128 + R], trb.ap[:, :].rearrange("p (c r) -> p c r", c=8)[:, :, 0:R], [trb.b0], [XT.b[t]])

    for l in range(depth):
        final = (l == depth - 1)
        with contextlib.ExitStack() as les:
            XT = kb.sb(les, "XT", [128, 8, NTOK], BF16, nb=NT + 1)
            QS = kb.sb(les, "QS", [128, 12, NS], BF16)
            VN = kb.sb(les, "VN", [NS, 768], BF16)
            with contextlib.ExitStack() as aes:
                QKT = kb.sb(aes, "QKT", [128, 12, SP], BF16, nb=12 * (NT + 1))
                Vsb = kb.sb(aes, "Vsb", [128, NT, 768], BF16, nb=NT + 1)

                def qkb(ch, t):
                    return QKT.b[ch * (NT + 1) + t]
                with contextlib.ExitStack() as nes:
                    negT = kb.sb(nes, "negT", [128, SP], BF16, nb=NT)
                    fw.op("pool", lambda: nc.gpsimd.memset(negT.ap[:, :], 0.0), writes=list(negT.b))
                    with contextlib.ExitStack() as ues:
                        uT = kb.sb(ues, "uT", [128, 2, 30 + SP], BF16, nb=NT + 1)
                        usT = kb.sb(ues, "usT", [128, 2, NS, 31], F32)
                        fw.op("pool", lambda: nc.gpsimd.memset(uT.ap[:, :, 0:30], 0.0), writes=[uT.b[NT]])
                        with contextlib.ExitStack() as pes:
                            gbc = kb.sb(pes, "gbc", [128, D])
                            fw.dma("sp", gbc.ap[:], gvec["g_pre_mix"][l, :].partition_broadcast(128), writes=[gbc.b0])
                            hr = [kb.sb(pes, "hr", [128, D]) for _ in range(2)]
                            tmps = [(kb.sb(pes, "junk", [128, D], BF16), kb.sb(pes, "ssq", [128, 1]), kb.sb(pes, "tmp1", [128, 1]),
                                     kb.sb(pes, "rs", [128, 1]), kb.sb(pes, "abf", [128, D], BF16)) for _ in range(2)]
                            trbs = [kb.ps(pes, "trb", [128, 1024], BF16) for _ in range(2)]
                            for t in range(NT + 1):
                                R = kb.rows(t)
                                hT = hr[t % 2]
                                fw.dma("sp", hT.ap[0:R, :], hsrc(l, t), reads=[hb[t]], writes=[hT.b0])
                                norm_to_T(l, t, hT, gbc, XT, trbs[t % 2], tmps[t % 2])
                            fw.barrier()
                        with contextlib.ExitStack() as pes:
                          if UPTO >= 1:
                            phase_a1(kb, l, pes, XT, QKT, qkb, Vsb, negT, uT, usT, w_in, cst, ident_f, ident_b, evac,
                                     kr_p, vr_p, kr_s, vr_s, cv_p, cv_s, outb, QS, VN)
                            fw.barrier()
                        with contextlib.ExitStack() as pes:
                          if UPTO >= 2:
                            phase_conv(kb, l, pes, XT, uT, usT, sconv, conv_w, conv_b, conv_g, cv_s, outb, cst, ident_f, ident_b,
                                       ones_f, mhalf, cvals, evac)
                            fw.barrier()
                    with contextlib.ExitStack() as pes:
                      if UPTO >= 3:
                        phase_moba(kb, l, pes, XT, QKT, qkb, Vsb, negT, cst, ident_b, ones_b)
                        fw.barrier()
                with contextlib.ExitStack() as pes:
                  if UPTO >= 4:
                    phase_sb(kb, l, pes, XT, QKT, qkb, Vsb, cst, ident_b, cvals)
                    fw.barrier()
            with contextlib.ExitStack() as pes:
                if do_sample and UPTO >= 5:
                    phase_sample(kb, l, pes, XT, QS, VN, ck, cv, idx, cst, ident_b, ones_b, ones_f, cvals, evac)
                else:
                    for ch in (0, 1, 2, 5, 6, 7):
                        fw.op("pool", lambda ch=ch: nc.gpsimd.memset(XT.ap[:, ch, SP:SP + NS], 0.0), writes=[XT.b[NT]])
                fw.barrier()
            with contextlib.ExitStack() as pes:
              if UPTO >= 5:
                phase_mix(kb, l, pes, XT, w_out, gvec, hsrc, hbuf, hb, ident_b, evac, norm_to_T, rstd_from_ssq)
                fw.barrier()
            with contextlib.ExitStack() as pes:
              if UPTO >= 6:
                phase_mlp(kb, l, pes, XT, w_up, w_down, gvec, hbuf, hb, hdst, outb, final, evac, rstd_from_ssq)
                fw.barrier()
    fw.finish()
    return nc


def phase_a1(kb, l, pes, XT, QKT, qkb, Vsb, negT, uT, usT, w_in, cst, ident_f, ident_b, evac,
             kr_p, vr_p, kr_s, vr_s, cv_p, cv_s, outb, QS, VN):
    nc, fw = kb.nc, kb.fw
    NT, NS = kb.NT, kb.NS
    SP = NT * 128
    wr = [kb.sb(pes, "wg", [128, 8, 512], BF16) for _ in range(2)]
    stgs = [kb.sb(pes, "stg", [128, 512]) for _ in range(3)]
    cb16 = [kb.sb(pes, "cb16", [128, 384], BF16) for _ in range(3)]
    rtmp = kb.sb(pes, "rtmp", [128, 2, 96])
    qaT = kb.sb(pes, "qaT", [128, 3, 128])
    ksum = kb.sb(pes, "ksum", [128, 3, max(NT, 2)])
    kmT = kb.sb(pes, "kmT", [128, 3, 8])
    g1 = kb.sb(pes, "g1", [128, 48]); mx = kb.sb(pes, "mx", [128, 48]); thr = kb.sb(pes, "thr", [128, 6])
    negm = kb.sb(pes, "negm", [128, 48], BF16)
    gtmp = kb.sb(pes, "gtmp", [128, 256])
    mm = [kb.ps(pes, "mm") for _ in range(2)]
    fbs = [kb.ps(pes, "fb") for _ in range(2)]
    trb = [kb.ps(pes, "trb", [128, 1024], BF16) for _ in range(2)]
    gbank = kb.ps(pes, "gbank"); gb2 = kb.ps(pes, "gb2")
    ropec, ropes, gbias = cst["ropec"], cst["ropes"], cst["gbias"]
    cnt = {"mm": 0, "stg": 0, "cb": 0, "tr": 0, "fb": 0}

    def nxt(k, lst):
        cnt[k] += 1
        return lst[cnt[k] % len(lst)]

    fw.op("dve", lambda: nc.vector.memset(kmT.ap[:], 0.0), writes=[kmT.b0])

    def rope(stg, t, R):
        X = stg.ap[0:R, 0:384].rearrange("p (h d) -> p h d", h=6)
        A = rtmp.ap[0:R, 0, :].rearrange("p (h d) -> p h d", h=6)
        B = rtmp.ap[0:R, 1, :].rearrange("p (h d) -> p h d", h=6)
        cc = ropec.ap[0:R, t * 16:(t + 1) * 16].unsqueeze(1).to_broadcast([R, 6, 16])
        ss = ropes.ap[0:R, t * 16:(t + 1) * 16].unsqueeze(1).to_broadcast([R, 6, 16])
        fw.op("dve", lambda: nc.vector.tensor_tensor(out=A, in0=X[:, :, 0:16], in1=cc, op=ALU.mult), reads=[stg.b0, ropec.b0], writes=[rtmp.b0])
        fw.op("dve", lambda: nc.vector.tensor_tensor(out=B, in0=X[:, :, 0:16], in1=ss, op=ALU.mult), reads=[stg.b0, ropes.b0], writes=[rtmp.b0])
        fw.op("dve", lambda: nc.vector.tensor_tensor(out=X[:, :, 0:8], in0=A[:, :, 0:8], in1=B[:, :, 8:16], op=ALU.subtract), reads=[rtmp.b0], writes=[stg.b0])
        fw.op("dve", lambda: nc.vector.tensor_tensor(out=X[:, :, 8:16], in0=A[:, :, 8:16], in1=B[:, :, 0:8], op=ALU.add), reads=[rtmp.b0], writes=[stg.b0])

    def to_T_b(cb, tb, ncol, R, dst, dch0, dcol0, dbufs):
        nch = ncol // 128
        for c in range(nch):
            fw.op("pe", lambda c=c: nc.tensor.transpose(out=tb.ap[:, c * 128:c * 128 + R], in_=cb.ap[0:R, c * 128:(c + 1) * 128],
                                                        identity=ident_b.ap[0:R, 0:R]), reads=[cb.b0, ident_b.b0], writes=[tb.b0], inc=(c == nch - 1))
        evac(dst.ap[:, dch0:dch0 + nch, dcol0:dcol0 + R], tb.ap[:, 0:nch * 128].rearrange("p (c r) -> p c r", c=nch)[:, :, 0:R], [tb.b0], dbufs)

    def fp32_T(stg, fb, n):
        for c in range(n):
            fw.op("pe", lambda c=c: nc.tensor.transpose(out=fb.ap[:, c * 128:(c + 1) * 128], in_=stg.ap[0:128, c * 128:(c + 1) * 128],
                                                        identity=ident_f.ap[:, :]), reads=[stg.b0, ident_f.b0], writes=[fb.b0], inc=(c == n - 1))

    def make_unit(gi, kind, col0, ncol, t, last_of_group):
        R = kb.rows(t)
        prompt = t < NT
        wg = wr[gi % 2]
        bank = nxt("mm", mm)
        stg = nxt("stg", stgs)
        rowsl = slice(t * 128, (t + 1) * 128)
        needs_T = kind in ("ka", "kc", "qa", "qc") or (kind == "u" and prompt)
        cb = nxt("cb", cb16) if needs_T else None
        tb = nxt("tr", trb) if needs_T else None
        fb = nxt("fb", fbs) if ((kind in ("ka", "qa") and prompt) or (kind == "u" and not prompt)) else None
        tb2 = nxt("tr", trb) if (kind == "qa" and prompt) else None

        def st0():
            if t == 0:
                fw.dma("pool", wg.ap[:, :, 0:ncol], w_in[l, :, col0:col0 + ncol].rearrange("(k p) n -> p k n", p=128), writes=[wg.b0])
            for k in range(8):
                fw.op("pe", lambda k=k: nc.tensor.matmul(bank.ap[0:R, 0:ncol], lhsT=XT.ap[:, k, t * 128:t * 128 + R], rhs=wg.ap[:, k, 0:ncol],
                                                         start=(k == 0), stop=(k == 7)), reads=[XT.b[t], wg.b0], writes=[bank.b0], inc=(k == 7))

        def st1():
            evac(stg.ap[0:R, 0:ncol], bank.ap[0:R, 0:ncol], [bank.b0], [stg.b0])
            if kind in ("ka", "kc"):
                off = 0 if kind == "ka" else 384
                if kind == "ka":
                    rope(stg, t, R)
                dst = kr_p[l, rowsl, off:off + 384] if prompt else kr_s[l, 0:NS, off:off + 384]
                fw.dma("sp", dst, stg.ap[0:R, 0:384], reads=[stg.b0], writes=[outb])
                evac(cb.ap[0:R, 0:384], stg.ap[0:R, 0:384], [stg.b0], [cb.b0])
            elif kind in ("qa", "qc"):
                if kind == "qa":
                    rope(stg, t, R)
                evac(cb.ap[0:R, 0:384], stg.ap[0:R, 0:384], [stg.b0], [cb.b0], scale=0.125)
            elif kind in ("va", "vc"):
                off = 0 if kind == "va" else 384
                dst = vr_p[l, rowsl, off:off + 384] if prompt else vr_s[l, 0:NS, off:off + 384]
                fw.dma("sp", dst, stg.ap[0:R, 0:384], reads=[stg.b0], writes=[outb])
                if prompt:
                    evac(Vsb.ap[0:R, t, off:off + 384], stg.ap[0:R, 0:384], [stg.b0], [Vsb.b[t]])
                else:
                    evac(VN.ap[0:R, off:off + 384], stg.ap[0:R, 0:384], [stg.b0], [VN.b0])
            elif kind == "u":
                fw.op("act", lambda: nc.scalar.activation(out=gtmp.ap[0:R, :], in_=stg.ap[0:R, 256:512], func=AF.Exp, scale=-1.0),
                      reads=[stg.b0], writes=[gtmp.b0])
                fw.op("dve", lambda: nc.vector.tensor_scalar(out=gtmp.ap[0:R, :], in0=gtmp.ap[0:R, :], scalar1=1.0, scalar2=None, op0=ALU.add),
                      reads=[gtmp.b0], writes=[gtmp.b0])
                fw.op("dve", lambda: nc.vector.reciprocal(out=gtmp.ap[0:R, :], in_=gtmp.ap[0:R, :]), reads=[gtmp.b0], writes=[gtmp.b0])
                fw.op("dve", lambda: nc.vector.tensor_tensor(out=stg.ap[0:R, 0:256], in0=stg.ap[0:R, 0:256], in1=gtmp.ap[0:R, :], op=ALU.mult),
                      reads=[stg.b0, gtmp.b0], writes=[stg.b0])
                if prompt:
                    if t == NT - 1:
                        fw.dma("sp", cv_p[l, 0:30, :], stg.ap[98:128, 0:256], reads=[stg.b0], writes=[outb])
                    evac(cb.ap[0:R, 0:256], stg.ap[0:R, 0:256], [stg.b0], [cb.b0])
                else:
                    fw.dma("sp", cv_s[l, :, :].rearrange("(s j) c -> s j c", j=30)[:, 29, :], stg.ap[0:NS, 0:256], reads=[stg.b0], writes=[outb])

        def st2():
            if kind in ("ka", "kc"):
                ch0 = 3 if kind == "ka" else 9
                if prompt:
                    to_T_b(cb, tb, 384, R, QKT, ch0, t * 128, [qkb(ch0 + c, t) for c in range(3)])
                else:
                    to_T_b(cb, tb, 384, R, QS, ch0, 0, [QS.b0])
                if kind == "ka" and prompt:
                    fp32_T(stg, fb, 3)
                    fw.op("dve", lambda: nc.vector.tensor_reduce(out=ksum.ap[:, :, t], in_=fb.ap[:, 0:384].rearrange("p (c r) -> p c r", c=3),
                                                                 axis=AX.X, op=ALU.add), reads=[fb.b0], writes=[ksum.b0])
                if kind == "ka" and last_of_group:
                    nb = NT // 2
                    kv = ksum.ap[:, :, 0:2 * nb].rearrange("p c (n two) -> p c n two", two=2)
                    fw.op("dve", lambda: nc.vector.tensor_tensor(out=kmT.ap[:, :, 0:nb].unsqueeze(3), in0=kv[:, :, :, 0:1], in1=kv[:, :, :, 1:2], op=ALU.add),
                          reads=[ksum.b0], writes=[kmT.b0])
                    fw.op("dve", lambda: nc.vector.tensor_scalar(out=kmT.ap[:, :, 0:nb], in0=kmT.ap[:, :, 0:nb], scalar1=1.0 / 256, scalar2=None, op0=ALU.mult),
                          reads=[kmT.b0], writes=[kmT.b0])
            elif kind in ("qa", "qc"):
                ch0 = 0 if kind == "qa" else 6
                if prompt:
                    to_T_b(cb, tb, 384, R, QKT, ch0, t * 128, [qkb(ch0 + c, t) for c in range(3)])
                else:
                    to_T_b(cb, tb, 384, R, QS, ch0, 0, [QS.b0])
                if kind == "qa" and prompt:
                    fp32_T(stg, fb, 3)
                    evac(qaT.ap[:, :, :], fb.ap[:, 0:384].rearrange("p (c r) -> p c r", c=3), [fb.b0], [qaT.b0])
                    for par, gbk in ((0, gbank), (1, gb2)):
                        for h in range(par, 6, 2):
                            c, pb = h // 2, 64 * par
                            fw.op("pe", lambda h=h, c=c, pb=pb, gbk=gbk: nc.tensor.matmul(gbk.ap[0:128, h * 8:(h + 1) * 8], lhsT=qaT.ap[pb:pb + 64, c, 0:128],
                                                                                          rhs=kmT.ap[pb:pb + 64, c, 0:8], start=True, stop=True),
                                  reads=[qaT.b0, kmT.b0], writes=[gbk.b0], inc=(h >= 4))
                    qbk = t // 2
                    for par, gbk in ((0, gbank), (1, gb2)):
                        fw.op("dve", lambda par=par, gbk=gbk: nc.vector.tensor_tensor(out=g1.ap[:, :].rearrange("p (c two e) -> p c two e", c=3, two=2)[:, :, par, :],
                                                                                      in0=gbk.ap[:, 0:48].rearrange("p (c two e) -> p c two e", c=3, two=2)[:, :, par, :],
                                                                                      in1=gbias.ap[:, qbk * 8:(qbk + 1) * 8].unsqueeze(1).to_broadcast([128, 3, 8]), op=ALU.add),
                              reads=[gbk.b0, gbias.b0], writes=[g1.b0])
                    for h in range(6):
                        fw.op("dve", lambda h=h: nc.vector.max(out=mx.ap[:, h * 8:(h + 1) * 8], in_=g1.ap[:, h * 8:(h + 1) * 8]), reads=[g1.b0], writes=[mx.b0])
                    fw.op("dve", lambda: nc.vector.tensor_scalar(out=thr.ap[:, 0:6].unsqueeze(2), in0=mx.ap[:, :].rearrange("p (h e) -> p h e", h=6)[:, :, 3:4],
                                                                 scalar1=-1e29, scalar2=None, op0=ALU.max), reads=[mx.b0], writes=[thr.b0])
                    for h in range(6):
                        fw.op("dve", lambda h=h: nc.vector.tensor_scalar(out=negm.ap[:, h * 8:(h + 1) * 8], in0=g1.ap[:, h * 8:(h + 1) * 8],
                                                                         scalar1=thr.ap[:, h:h + 1], scalar2=NEG, op0=ALU.is_lt, op1=ALU.mult),
                              reads=[g1.b0, thr.b0], writes=[negm.b0])
                    fw.op("pe", lambda: nc.tensor.transpose(out=tb2.ap[0:48, 0:128], in_=negm.ap[0:128, 0:48], identity=ident_b.ap[:, :]),
                          reads=[negm.b0, ident_b.b0], writes=[tb2.b0])
                    evac(negT.ap[0:48, rowsl], tb2.ap[0:48, 0:128], [tb2.b0], [negT.b[t]])
            elif kind == "u":
                if prompt:
                    to_T_b(cb, tb, 256, R, uT, 0, 30 + t * 128, [uT.b[t]])
                else:
                    for c in range(2):
                        fw.op("pe", lambda c=c: nc.tensor.transpose(out=fb.ap[:, c * NS:(c + 1) * NS], in_=stg.ap[0:NS, c * 128:(c + 1) * 128],
                                                                    identity=ident_f.ap[0:NS, 0:NS]), reads=[stg.b0, ident_f.b0], writes=[fb.b0], inc=(c == 1))
                    evac(usT.ap[:, :, :, 30], fb.ap[:, 0:2 * NS].rearrange("p (c s) -> p c s", c=2), [fb.b0], [usT.b0])
        return [st0, st1, st2]

    groups = [("ka", KA, 384), ("qa", QA, 384), ("va", VA, 384), ("u", UV, 512), ("kc", KC, 384), ("qc", QC, 384), ("vc", VC, 384)]
    items = []
    for gi, (kind, col0, ncol) in enumerate(groups):
        for t in range(NT + 1):
            items.append(make_unit(gi, kind, col0, ncol, t, t == NT))
    emit_pipelined(items, 1)


def phase_conv(kb, l, pes, XT, uT, usT, sconv, conv_w, conv_b, conv_g, cv_s, outb, cst, ident_f, ident_b, ones_f, mhalf, cvals, evac):
    nc, fw = kb.nc, kb.fw
    NT, NS = kb.NT, kb.NS
    SP = NT * 128
    NB = NT // 4
    cwn = kb.sb(pes, "cwn", [33, 256])
    cwT = kb.sb(pes, "cwT", [128, 2, 33])
    fw.dma("sp", cwn.ap[0:31, :], conv_w[l, :, :], writes=[cwn.b0])
    fw.dma("sp", cwn.ap[31:32, :], conv_b[l:l + 1, :], writes=[cwn.b0])
    fw.dma("sp", cwn.ap[32:33, :], conv_g[l:l + 1, :], writes=[cwn.b0])
    FB0 = kb.ps(pes, "FB0")
    for c in range(2):
        fw.op("pe", lambda c=c: nc.tensor.transpose(out=FB0.ap[:, c * 33:(c + 1) * 33], in_=cwn.ap[0:33, c * 128:(c + 1) * 128], identity=ident_f.ap[0:33, 0:33]),
              reads=[cwn.b0, ident_f.b0], writes=[FB0.b0], inc=(c == 1))
    evac(cwT.ap[:, :, :], FB0.ap[:, 0:66].rearrange("p (c j) -> p c j", c=2), [FB0.b0], [cwT.b0])

    class _V:
        def __init__(self, ap, b0):
            self.ap, self.b0 = ap, b0
    cw = _V(cwT.ap[:, :, 0:31], cwT.b0)
    cb = _V(cwT.ap[:, :, 31], cwT.b0)
    cg = _V(cwT.ap[:, :, 32], cwT.b0)
    diag = [kb.sb(pes, "diag", [128, 31, 128], BF16) for _ in range(2)]
    for c in range(2):
        for j in range(31):
            fw.op("dve", lambda c=c, j=j: nc.vector.tensor_scalar(out=diag[c].ap[:, j, :], in0=ident_b.ap[:, :], scalar1=cw.ap[:, c, j:j + 1], scalar2=None,
                                                                  op0=ALU.mult), reads=[ident_b.b0, cw.b0], writes=[diag[c].b0])
    yb = [kb.sb(pes, "yb", [128, 512]) for _ in range(2)]
    sq = [kb.sb(pes, "sq", [128, 512]) for _ in range(2)]
    ms = kb.sb(pes, "ms", [128, 512]); rstd = kb.sb(pes, "rstd", [128, 512])
    yn = kb.sb(pes, "yn", [128, 512]); et = kb.sb(pes, "et", [128, 512])
    Y = [kb.ps(pes, "Y") for _ in range(2)]
    SQ = kb.ps(pes, "SQ"); FB = kb.ps(pes, "FB")
    st = [kb.sb(pes, "st", [120, 256]) for _ in range(2)]
    for i in range(NS // 4):
        s_ = st[i % 2]
        fw.dma("sp", s_.ap[:, :], sconv[l, i * 120:(i + 1) * 120, :], writes=[s_.b0])
        for s in range(4):
            fw.dma("sp", cv_s[l, (4 * i + s) * 30:(4 * i + s) * 30 + 29, :], s_.ap[s * 30 + 1:s * 30 + 30, :], reads=[s_.b0], writes=[outb])
        for c in range(2):
            fw.op("pe", lambda c=c: nc.tensor.transpose(out=FB.ap[:, c * 120:(c + 1) * 120], in_=s_.ap[0:120, c * 128:(c + 1) * 128], identity=ident_f.ap[0:120, 0:120]),
                  reads=[s_.b0, ident_f.b0], writes=[FB.b0], inc=(c == 1))
        evac(usT.ap[:, :, 4 * i:4 * i + 4, 0:30], FB.ap[:, 0:240].rearrange("p (c s j) -> p c s j", c=2, s=4), [FB.b0], [usT.b0])
    prod = kb.sb(pes, "prod", [128, NS, 31])
    blocks = [(tb * 512, 512, True) for tb in range(NB)] + [(SP, NS, False)]
    for (c0, N, prompt) in blocks:
        for c in range(2):
            if prompt:
                for j in range(31):
                    fw.op("pe", lambda c=c, j=j: nc.tensor.matmul(Y[c].ap[:, 0:N], lhsT=diag[c].ap[:, j, :], rhs=uT.ap[:, c, c0 + j:c0 + j + N],
                                                                  start=(j == 0), stop=(j == 30)),
                          reads=[diag[c].b0] + [uT.b[t] for t in range(max(0, c0 // 128 - 1), c0 // 128 + 4)] + [uT.b[NT]], writes=[Y[c].b0], inc=(j == 30))
                fw.op("act", lambda c=c: nc.scalar.activation(out=yb[c].ap[:, 0:N], in_=Y[c].ap[:, 0:N], func=AF.Identity, bias=cb.ap[:, c:c + 1]),
                      reads=[Y[c].b0, cb.b0], writes=[yb[c].b0])
            else:
                fw.op("dve", lambda c=c: nc.vector.tensor_tensor(out=prod.ap[:, :, :], in0=usT.ap[:, c, :, :],
                                                                 in1=cw.ap[:, c, :].unsqueeze(1).to_broadcast([128, NS, 31]), op=ALU.mult),
                      reads=[usT.b0, cw.b0], writes=[prod.b0])
                fw.op("dve", lambda c=c: nc.vector.tensor_reduce(out=yb[c].ap[:, 0:N], in_=prod.ap[:, :, :], axis=AX.X, op=ALU.add), reads=[prod.b0], writes=[yb[c].b0])
                fw.op("act", lambda c=c: nc.scalar.activation(out=yb[c].ap[:, 0:N], in_=yb[c].ap[:, 0:N], func=AF.Identity, bias=cb.ap[:, c:c + 1]),
                      reads=[yb[c].b0, cb.b0], writes=[yb[c].b0])
            fw.op("act", lambda c=c: nc.scalar.activation(out=sq[c].ap[:, 0:N], in_=yb[c].ap[:, 0:N], func=AF.Square), reads=[yb[c].b0], writes=[sq[c].b0])
        for c in range(2):
            fw.op("pe", lambda c=c: nc.tensor.matmul(SQ.ap[:, 0:N], lhsT=ones_f.ap[:, :], rhs=sq[c].ap[:, 0:N], start=(c == 0), stop=(c == 1)),
                  reads=[ones_f.b0, sq[c].b0], writes=[SQ.b0], inc=(c == 1))
        fw.op("dve", lambda: nc.vector.tensor_scalar(out=ms.ap[:, 0:N], in0=SQ.ap[:, 0:N], scalar1=1.0 / 256, scalar2=EPS, op0=ALU.mult, op1=ALU.add),
              reads=[SQ.b0], writes=[ms.b0])
        fw.op("act", lambda: nc.scalar.activation(out=rstd.ap[:, 0:N], in_=ms.ap[:, 0:N], func=AF.Ln), reads=[ms.b0], writes=[rstd.b0])
        fw.op("act", lambda: nc.scalar.activation(out=rstd.ap[:, 0:N], in_=rstd.ap[:, 0:N], func=AF.Exp, scale=-0.5), reads=[rstd.b0], writes=[rstd.b0])
        for c in range(2):
            fw.op("dve", lambda c=c: nc.vector.scalar_tensor_tensor(out=yn.ap[:, 0:N], in0=yb[c].ap[:, 0:N], scalar=cg.ap[:, c:c + 1], in1=rstd.ap[:, 0:N],
                                                                    op0=ALU.mult, op1=ALU.mult), reads=[yb[c].b0, cg.b0, rstd.b0], writes=[yn.b0])
            fw.op("act", lambda: nc.scalar.activation(out=et.ap[:, 0:N], in_=yn.ap[:, 0:N], func=AF.Exp, scale=-1.0), reads=[yn.b0], writes=[et.b0])
            fw.op("dve", lambda: nc.vector.tensor_scalar(out=et.ap[:, 0:N], in0=et.ap[:, 0:N], scalar1=1.0, scalar2=None, op0=ALU.add), reads=[et.b0], writes=[et.b0])
            fw.op("dve", lambda: nc.vector.reciprocal(out=et.ap[:, 0:N], in_=et.ap[:, 0:N]), reads=[et.b0], writes=[et.b0])
            tl = [XT.b[t] for t in range(c0 // 128, c0 // 128 + 4)] if prompt else [XT.b[NT]]
            fw.op("dve", lambda c=c: nc.vector.tensor_tensor(out=XT.ap[:, 3 + c, c0:c0 + N], in0=yn.ap[:, 0:N], in1=et.ap[:, 0:N], op=ALU.mult),
                  reads=[yn.b0, et.b0], writes=tl)


def phase_moba(kb, l, pes, XT, QKT, qkb, Vsb, negT, cst, ident_b, ones_b):
    nc, fw = kb.nc, kb.fw
    NT = kb.NT
    NB = NT // 4
    e48, cmask = cst["e48_b"], cst["cmask_b"]
    S = [kb.ps(pes, "S") for _ in range(3)]
    num = [kb.ps(pes, "num") for _ in range(2)]
    den = [kb.ps(pes, "den") for _ in range(2)]
    P = [kb.sb(pes, "P", [128, 512], BF16) for _ in range(3)]
    rd = [kb.sb(pes, "rd", [128, 512]) for _ in range(2)]
    qz = [[kb.sb(pes, "qz", [128, 512], BF16) for _ in range(2)] for _ in range(2)]
    for par in range(2):
        for k in range(2):
            fw.op("pool", lambda: nc.gpsimd.memset(qz[par][k].ap[:, :], 0.0), writes=[qz[par][k].b0])
    items = []
    i = 0
    it = 0
    for h in range(6):
        for b in range(NB):
            nk = 4 * b + 4
            for kj in range(nk):
                items.append(_moba_tile(kb, h, b, kj, nk, i, it, XT, QKT, qkb, Vsb, negT, e48, cmask, ident_b, ones_b, S, P, num, den, rd, qz))
                i += 1
            it += 1
    emit_pipelined(items, 1)


def _moba_tile(kb, h, b, kj, nk, i, it, XT, QKT, qkb, Vsb, negT, e48, cmask, ident_b, ones_b, S, P, num, den, rd, qz):
    nc, fw = kb.nc, kb.fw
    c, pb = h // 2, 64 * (h % 2)
    nm, dn = num[it % 2], den[it % 2]
    qcols = slice(b * 512, (b + 1) * 512)
    q = qz[h % 2][b % 2]
    Sb, Pb = S[i % 3], P[i % 3]
    diag = kj >= 4 * b

    def st0():
        if kj == 0:
            fw.op("pool", lambda: nc.gpsimd.tensor_copy(out=q.ap[pb:pb + 64, :], in_=QKT.ap[pb:pb + 64, 0 + c, qcols]),
                  reads=[qkb(0 + c, t) for t in range(4 * b, 4 * b + 4)], writes=[q.b0])
        fw.op("pe", lambda: nc.tensor.matmul(Sb.ap[:, :], lhsT=QKT.ap[:, 3 + c, kj * 128:(kj + 1) * 128], rhs=q.ap[:, :],
                                             start=True, stop=False), reads=[qkb(3 + c, kj), q.b0], writes=[Sb.b0], inc=False)
        r = h * 8 + kj // 2
        fw.op("pe", lambda: nc.tensor.matmul(Sb.ap[:, :], lhsT=e48.ap[:, r * 128:(r + 1) * 128], rhs=negT.ap[:, qcols], start=False, stop=not diag),
              reads=[e48.b0] + [negT.b[t] for t in range(4 * b, 4 * b + 4)], writes=[Sb.b0], inc=not diag)
        if diag:
            rr = 4 + kj - 4 * b
            fw.op("pe", lambda: nc.tensor.matmul(Sb.ap[:, :], lhsT=ident_b.ap[:, :], rhs=cmask.ap[:, rr * 512:(rr + 1) * 512], start=False, stop=True),
                  reads=[ident_b.b0, cmask.b0], writes=[Sb.b0])

    def st1():
        fw.op("act", lambda: nc.scalar.activation(out=Pb.ap[:, :], in_=Sb.ap[:, :], func=AF.Exp), reads=[Sb.b0], writes=[Pb.b0])

    def st2():
        fw.op("pe", lambda: nc.tensor.matmul(nm.ap[:, :], lhsT=Vsb.ap[:, kj, c * 128:(c + 1) * 128], rhs=Pb.ap[:, :], start=(kj == 0), stop=(kj == nk - 1)),
              reads=[Vsb.b[kj], Pb.b0], writes=[nm.b0], inc=False)
        fw.op("pe", lambda: nc.tensor.matmul(dn.ap[:, :], lhsT=ones_b.ap[:, :], rhs=Pb.ap[:, :], start=(kj == 0), stop=(kj == nk - 1)),
              reads=[ones_b.b0, Pb.b0], writes=[dn.b0], inc=True)
        if kj == nk - 1:
            rdb = rd[it % 2]
            fw.op("dve", lambda: nc.vector.reciprocal(out=rdb.ap[pb:pb + 64, :], in_=dn.ap[pb:pb + 64, :]), reads=[dn.b0], writes=[rdb.b0])
            fw.op("dve", lambda: nc.vector.tensor_tensor(out=XT.ap[pb:pb + 64, c, qcols], in0=nm.ap[pb:pb + 64, :], in1=rdb.ap[pb:pb + 64, :], op=ALU.mult),
                  reads=[nm.b0, rdb.b0], writes=[XT.b[t] for t in range(4 * b, 4 * b + 4)])
    return [st0, st1, st2]


def phase_sb(kb, l, pes, XT, QKT, qkb, Vsb, cst, ident_b, cvals):
    nc, fw = kb.nc, kb.fw
    NT = kb.NT
    NB = NT // 4
    cmask, trineg, selneg, selb, su16 = cst["cmask_b"], cst["trineg_b"], cst["selneg_b"], cst["selb_b"], cst["su16"]
    Z = [kb.ps(pes, "Z") for _ in range(3)]
    Rb = kb.ps(pes, "Rb"); Cb = kb.ps(pes, "Cb")
    oacc = [kb.ps(pes, "oacc") for _ in range(2)]
    E = [kb.sb(pes, "E", [128, 512]) for _ in range(NT)]
    SPt = [kb.sb(pes, "SPt", [128, 512], BF16) for _ in range(NT)]
    Ab = [kb.sb(pes, "Ab", [128, 512], BF16) for _ in range(3)]
    Rf = kb.sb(pes, "Rf", [16, 512]); chi = kb.sb(pes, "chi", [128, 512], BF16); clo = kb.sb(pes, "clo", [128, 512], BF16)
    fw.op("pool", lambda: nc.gpsimd.memset(chi.ap[:, :], 0.0), writes=[chi.b0])
    fw.op("pool", lambda: nc.gpsimd.memset(clo.ap[:, :], 0.0), writes=[clo.b0])
    qz = [[kb.sb(pes, "qz", [128, 512], BF16) for _ in range(2)] for _ in range(2)]
    for par in range(2):
        for k in range(2):
            fw.op("pool", lambda: nc.gpsimd.memset(qz[par][k].ap[:, :], 0.0), writes=[qz[par][k].b0])
    ctr = {"i": 0}
    hbs = [(h, b) for h in range(6) for b in range(NB)]

    def zmm(h, b, q, Zb, kj, last):
        c = h // 2
        diag = kj >= 4 * b
        fw.op("pe", lambda: nc.tensor.matmul(Zb.ap[:, :], lhsT=QKT.ap[:, 9 + c, kj * 128:(kj + 1) * 128], rhs=q.ap[:, :],
                                             start=True, stop=(last and not diag)), reads=[qkb(9 + c, kj), q.b0], writes=[Zb.b0], inc=(last and not diag))
        if diag:
            rr = kj - 4 * b
            fw.op("pe", lambda: nc.tensor.matmul(Zb.ap[:, :], lhsT=ident_b.ap[:, :], rhs=cmask.ap[:, rr * 512:(rr + 1) * 512], start=False, stop=last),
                  reads=[ident_b.b0, cmask.b0], writes=[Zb.b0], inc=last)

    def P1(n):
        h, b = hbs[n]
        c, pb = h // 2, 64 * (h % 2)
        nk = 4 * b + 4
        qcols = slice(b * 512, (b + 1) * 512)
        q = qz[h % 2][b % 2]
        fw.op("pool", lambda: nc.gpsimd.tensor_copy(out=q.ap[pb:pb + 64, :], in_=QKT.ap[pb:pb + 64, 6 + c, qcols]),
              reads=[qkb(6 + c, t) for t in range(4 * b, 4 * b + 4)], writes=[q.b0])
        for kj in range(nk):
            Zb = Z[ctr["i"] % 3]
            ctr["i"] += 1
            zmm(h, b, q, Zb, kj, True)
            fw.op("act", lambda: nc.scalar.activation(out=E[kj].ap[:, :], in_=Zb.ap[:, :], func=AF.Exp), reads=[Zb.b0], writes=[E[kj].b0])

    def P2(n):
        h, b = hbs[n]
        nk = 4 * b + 4
        for kj in range(nk):
            fw.op("act", lambda kj=kj: nc.scalar.activation(out=SPt[kj].ap[:, :], in_=E[kj].ap[:, :], func=AF.Ln, bias=cvals.ap[:, 0:1]),
                  reads=[E[kj].b0, cvals.b0], writes=[SPt[kj].b0])
        for kj in range(nk):
            fw.op("pe", lambda kj=kj: nc.tensor.matmul(Rb.ap[:, :], lhsT=selneg.ap[:, kj * 128:(kj + 1) * 128], rhs=SPt[kj].ap[:, :], start=(kj == 0), stop=(kj == nk - 1)),
                  reads=[selneg.b0, SPt[kj].b0], writes=[Rb.b0], inc=(kj == nk - 1))
        fw.op("act", lambda: nc.scalar.copy(out=Rf.ap[0:16, :], in_=Rb.ap[0:16, :]), reads=[Rb.b0], writes=[Rf.b0])
        fw.op("pe", lambda: nc.tensor.matmul(Cb.ap[0:16, :], lhsT=su16.ap[0:16, 0:16], rhs=Rf.ap[0:16, :], start=True, stop=True),
              reads=[su16.b0, Rf.b0], writes=[Cb.b0])
        fw.op("dve", lambda: nc.vector.tensor_copy(out=chi.ap[0:16, :], in_=Cb.ap[0:16, :]), reads=[Cb.b0], writes=[chi.b0])
        fw.op("dve", lambda: nc.vector.tensor_tensor(out=clo.ap[0:16, :], in0=Cb.ap[0:16, :], in1=chi.ap[0:16, :], op=ALU.subtract),
              reads=[Cb.b0, chi.b0], writes=[clo.b0])

    def P3(n):
        h, b = hbs[n]
        c, pb = h // 2, 64 * (h % 2)
        nk = 4 * b + 4
        qcols = slice(b * 512, (b + 1) * 512)
        q = qz[h % 2][b % 2]
        oa = oacc[n % 2]
        items = []
        for kj in range(nk):
            i = ctr["i"]
            ctr["i"] += 1
            items.append(_sb_tile(kb, h, b, q, kj, nk, Z[i % 3], Ab[i % 3], oa, zmm, XT, Vsb, SPt, trineg, selb, chi, clo, c, pb, qcols))
        emit_pipelined(items, 1)

    P1(0)
    for n in range(len(hbs)):
        P2(n)
        if n + 1 < len(hbs):
            P1(n + 1)
        P3(n)


def _sb_tile(kb, h, b, q, kj, nk, Zb, A, oa, zmm, XT, Vsb, SPt, trineg, selb, chi, clo, c, pb, qcols):
    nc, fw = kb.nc, kb.fw

    def st0():
        zmm(h, b, q, Zb, kj, False)
        fw.op("pe", lambda: nc.tensor.matmul(Zb.ap[:, :], lhsT=trineg.ap[:, :], rhs=SPt[kj].ap[:, :], start=False, stop=False),
              reads=[trineg.b0, SPt[kj].b0], writes=[Zb.b0], inc=False)
        fw.op("pe", lambda: nc.tensor.matmul(Zb.ap[:, :], lhsT=selb.ap[:, kj * 128:(kj + 1) * 128], rhs=chi.ap[:, :], start=False, stop=False),
              reads=[selb.b0, chi.b0], writes=[Zb.b0], inc=False)
        fw.op("pe", lambda: nc.tensor.matmul(Zb.ap[:, :], lhsT=selb.ap[:, kj * 128:(kj + 1) * 128], rhs=clo.ap[:, :], start=False, stop=True),
              reads=[selb.b0, clo.b0], writes=[Zb.b0], inc=True)

    def st1():
        fw.op("act", lambda: nc.scalar.activation(out=A.ap[:, :], in_=Zb.ap[:, :], func=AF.Exp), reads=[Zb.b0], writes=[A.b0])

    def st2():
        fw.op("pe", lambda: nc.tensor.matmul(oa.ap[:, :], lhsT=Vsb.ap[:, kj, 384 + c * 128:384 + (c + 1) * 128], rhs=A.ap[:, :],
                                             start=(kj == 0), stop=(kj == nk - 1)), reads=[Vsb.b[kj], A.b0], writes=[oa.b0], inc=True)
        if kj == nk - 1:
            fw.op("dve", lambda: nc.vector.tensor_copy(out=XT.ap[pb:pb + 64, 5 + c, qcols], in_=oa.ap[pb:pb + 64, :]), reads=[oa.b0],
                  writes=[XT.b[t] for t in range(4 * b, 4 * b + 4)])
    return [st0, st1, st2]


def phase_mix(kb, l, pes, XT, w_out, gvec, hsrc, hbuf, hb, ident_b, evac, norm_to_T, rstd_from_ssq):
    nc, fw = kb.nc, kb.fw
    NT, NS = kb.NT, kb.NS
    wo = kb.sb(pes, "wo", [128, 8, D], BF16)
    fw.dma("pool", wo.ap[:, :, :], w_out[l, :, :].rearrange("(k p) n -> p k n", p=128), writes=[wo.b0])
    gpost = kb.sb(pes, "gpost", [128, D]); gpre = kb.sb(pes, "gpre", [128, D])
    fw.dma("sp", gpost.ap[:], gvec["g_post_mix"][l, :].partition_broadcast(128), writes=[gpost.b0])
    fw.dma("sp", gpre.ap[:], gvec["g_pre_mlp"][l, :].partition_broadcast(128), writes=[gpre.b0])
    hr = [kb.sb(pes, "hr", [128, D]) for _ in range(2)]
    tmps = [(kb.sb(pes, "junk", [128, D], BF16), kb.sb(pes, "ssq", [128, 1]), kb.sb(pes, "tmp1", [128, 1]),
             kb.sb(pes, "rs", [128, 1]), kb.sb(pes, "abf", [128, D], BF16)) for _ in range(2)]
    s2 = [kb.sb(pes, "s2", [128, 2]) for _ in range(2)]
    stot = [kb.sb(pes, "stot", [128, 1]) for _ in range(2)]
    t1 = [kb.sb(pes, "t1", [128, 1]) for _ in range(2)]
    rs2 = [kb.sb(pes, "rs2", [128, 1]) for _ in range(2)]
    dlt = [kb.sb(pes, "dlt", [128, D]) for _ in range(2)]
    mixb = [[kb.ps(pes, "mix") for _ in range(2)] for _ in range(2)]
    trbs = [kb.ps(pes, "trb", [128, 1024], BF16) for _ in range(2)]
    for t in range(NT + 1):
        R = kb.rows(t)
        k2 = t % 2
        hT = hr[k2]
        fw.dma("sp", hT.ap[0:R, :], hsrc(l, t), reads=[hb[t]], writes=[hT.b0])
        junk = tmps[k2][0]
        for half in range(2):
            bank = mixb[k2][half]
            for c in range(8):
                fw.op("pe", lambda c=c: nc.tensor.matmul(bank.ap[0:R, :], lhsT=XT.ap[:, c, t * 128:t * 128 + R], rhs=wo.ap[:, c, half * 512:(half + 1) * 512],
                                                         start=(c == 0), stop=(c == 7)), reads=[XT.b[t], wo.b0], writes=[bank.b0], inc=(c == 7))
            fw.op("act", lambda: nc.scalar.activation(out=junk.ap[0:R, 0:512], in_=bank.ap[0:R, :], func=AF.Square, accum_out=s2[k2].ap[0:R, half:half + 1]),
                  reads=[bank.b0], writes=[junk.b0, s2[k2].b0])
        fw.op("dve", lambda: nc.vector.tensor_tensor(out=stot[k2].ap[0:R, :], in0=s2[k2].ap[0:R, 0:1], in1=s2[k2].ap[0:R, 1:2], op=ALU.add),
              reads=[s2[k2].b0], writes=[stot[k2].b0])
        rstd_from_ssq(stot[k2].ap[0:R, 0:1], stot[k2].b0, R, D, t1[k2], rs2[k2])
        for half in range(2):
            bank = mixb[k2][half]
            hs = slice(half * 512, (half + 1) * 512)
            fw.op("dve", lambda: nc.vector.scalar_tensor_tensor(out=dlt[k2].ap[0:R, hs], in0=bank.ap[0:R, :], scalar=rs2[k2].ap[0:R, 0:1], in1=gpost.ap[0:R, hs],
                                                                op0=ALU.mult, op1=ALU.mult), reads=[bank.b0, rs2[k2].b0, gpost.b0], writes=[dlt[k2].b0])
        fw.op("pool", lambda: nc.gpsimd.tensor_tensor(out=hT.ap[0:R, :], in0=hT.ap[0:R, :], in1=dlt[k2].ap[0:R, :], op=ALU.add),
              reads=[hT.b0, dlt[k2].b0], writes=[hT.b0])
        fw.dma("sp", hbuf[t * 128:t * 128 + R, :], hT.ap[0:R, :], reads=[hT.b0], writes=[hb[t]])
        norm_to_T(l, t, hT, gpre, XT, trbs[k2], tmps[k2])


def phase_mlp(kb, l, pes, XT, w_up, w_down, gvec, hbuf, hb, hdst, outb, final, evac, rstd_from_ssq):
    nc, fw = kb.nc, kb.fw
    NT, NS = kb.NT, kb.NS
    SP = NT * 128
    NB = NT // 4
    NG, GF = 8, 4
    facc = kb.sb(pes, "facc", [128, NT + 1, D], F32, nb=NT + 1)
    wu = [kb.sb(pes, "wu", [128, 8, 512], BF16) for _ in range(2)]
    wd = [kb.sb(pes, "wd", [128, GF, D], BF16) for _ in range(2)]
    actT = [kb.sb(pes, "actT", [128, GF, 512], BF16) for _ in range(2)]
    rt = [kb.sb(pes, "rt", [128, 512]) for _ in range(2)]
    U = [kb.ps(pes, "U") for _ in range(3)]
    Dn = [kb.ps(pes, "Dn") for _ in range(4)]
    blocks = [(tb * 512, 512, [4 * tb + i for i in range(4)]) for tb in range(NB)] + [(SP, NS, [NT])]
    ctr = {"iu": 0, "idn": 0}

    def make_blk(g, bi, c0, N, tiles):
        wug, wdg = wu[g % 2], wd[g % 2]
        aT = actT[(g * len(blocks) + bi) % 2]

        def st0():
            if bi == 0:
                fw.dma("pool", wug.ap[:, :, :], w_up[l, :, g * 512:(g + 1) * 512].rearrange("(k p) n -> p k n", p=128), writes=[wug.b0])
                fw.dma("pool", wdg.ap[:, :, :], w_down[l, g * 512:(g + 1) * 512, :].rearrange("(f p) n -> p f n", p=128), writes=[wdg.b0])
            for fc in range(GF):
                Ub = U[ctr["iu"] % 3]
                rtb = rt[ctr["iu"] % 2]
                ctr["iu"] += 1
                for k in range(8):
                    fw.op("pe", lambda k=k: nc.tensor.matmul(Ub.ap[:, 0:N], lhsT=wug.ap[:, k, fc * 128:(fc + 1) * 128], rhs=XT.ap[:, k, c0:c0 + N],
                                                             start=(k == 0), stop=(k == 7)), reads=[wug.b0] + [XT.b[t] for t in tiles], writes=[Ub.b0], inc=(k == 7))
                fw.op("act", lambda: nc.scalar.activation(out=rtb.ap[:, 0:N], in_=Ub.ap[:, 0:N], func=AF.Relu), reads=[Ub.b0], writes=[rtb.b0])
                fw.op("pool", lambda: nc.gpsimd.tensor_tensor(out=aT.ap[:, fc, 0:N], in0=rtb.ap[:, 0:N], in1=rtb.ap[:, 0:N], op=ALU.mult),
                      reads=[rtb.b0], writes=[aT.b0])

        def st1():
            for ti, t in enumerate(tiles):
                R = kb.rows(t)
                for half in range(2):
                    Db = Dn[ctr["idn"] % 4]
                    ctr["idn"] += 1
                    hs = slice(half * 512, (half + 1) * 512)
                    for fc in range(GF):
                        fw.op("pe", lambda fc=fc: nc.tensor.matmul(Db.ap[0:R, :], lhsT=aT.ap[:, fc, ti * 128:ti * 128 + R], rhs=wdg.ap[:, fc, hs],
                                                                   start=(fc == 0), stop=(fc == GF - 1)), reads=[aT.b0, wdg.b0], writes=[Db.b0], inc=(fc == GF - 1))
                    if g == 0:
                        fw.op("act", lambda: nc.scalar.copy(out=facc.ap[0:R, t, hs], in_=Db.ap[0:R, :]), reads=[Db.b0], writes=[facc.b[t]])
                    else:
                        fw.op("dve", lambda: nc.vector.tensor_tensor(out=facc.ap[0:R, t, hs], in0=Db.ap[0:R, :], in1=facc.ap[0:R, t, hs], op=ALU.add),
                              reads=[Db.b0, facc.b[t]], writes=[facc.b[t]])
        return [st0, st1]

    items = []
    for g in range(NG):
        for bi, (c0, N, tiles) in enumerate(blocks):
            items.append(make_blk(g, bi, c0, N, tiles))
    emit_pipelined(items, 1)
    gpost = kb.sb(pes, "gpm", [128, D])
    fw.dma("sp", gpost.ap[:], gvec["g_post_mlp"][l, :].partition_broadcast(128), writes=[gpost.b0])
    hr = [kb.sb(pes, "hr", [128, D]) for _ in range(2)]
    junk = [kb.sb(pes, "junk", [128, D], BF16) for _ in range(2)]
    ssq = [kb.sb(pes, "ssq", [128, 1]) for _ in range(2)]
    t1 = [kb.sb(pes, "t1", [128, 1]) for _ in range(2)]
    rs = [kb.sb(pes, "rs", [128, 1]) for _ in range(2)]
    for t in range(NT + 1):
        R = kb.rows(t)
        k2 = t % 2
        hT = hr[k2]
        fw.dma("sp", hT.ap[0:R, :], hbuf[t * 128:t * 128 + R, :], reads=[hb[t]], writes=[hT.b0])
        fw.op("act", lambda: nc.scalar.activation(out=junk[k2].ap[0:R, :], in_=facc.ap[0:R, t, :], func=AF.Square, accum_out=ssq[k2].ap[0:R, 0:1]),
              reads=[facc.b[t]], writes=[junk[k2].b0, ssq[k2].b0])
        rstd_from_ssq(ssq[k2].ap[0:R, 0:1], ssq[k2].b0, R, D, t1[k2], rs[k2])
        fw.op("dve", lambda: nc.vector.scalar_tensor_tensor(out=facc.ap[0:R, t, :], in0=facc.ap[0:R, t, :], scalar=rs[k2].ap[0:R, 0:1], in1=gpost.ap[0:R, :],
                                                            op0=ALU.mult, op1=ALU.mult), reads=[facc.b[t], rs[k2].b0, gpost.b0], writes=[facc.b[t]])
        fw.op("pool", lambda: nc.gpsimd.tensor_tensor(out=hT.ap[0:R, :], in0=hT.ap[0:R, :], in1=facc.ap[0:R, t, :], op=ALU.add),
              reads=[hT.b0, facc.b[t]], writes=[hT.b0])
        fw.dma("sp", hdst(l, t, final), hT.ap[0:R, :], reads=[hT.b0], writes=[outb if final else hb[t]])


def phase_sample(kb, l, pes, XT, QS, VN, ck, cv, idx, cst, ident_b, ones_b, ones_f, cvals, evac):
    nc, fw = kb.nc, kb.fw
    NT, NS, NPG = kb.NT, kb.NS, kb.NPG
    SP = NT * 128
    NBK = NPG // 2
    H6 = 6 * NPG
    trineg, hsel = cst["trineg_b"], cst["hsel"]
    Vs = [kb.sb(pes, "Vs", [128, NPG, 768], BF16, nb=NPG) for _ in range(2)]
    kpg = [kb.sb(pes, "kpg", [128, 768], BF16) for _ in range(4)]
    kT = [kb.sb(pes, "kT", [128, 6, 128], BF16) for _ in range(3)]
    trb = [kb.ps(pes, "trb", [128, 1024], BF16) for _ in range(2)]
    Zs = [kb.ps(pes, "Zs") for _ in range(2)]
    misc = [kb.ps(pes, "misc") for _ in range(2)]
    Oall = kb.ps(pes, "Oall")
    Qblk = kb.sb(pes, "Qblk", [128, NS, 6, 2], BF16)
    fw.op("dve", lambda: nc.vector.memset(Qblk.ap[:], 0.0), writes=[Qblk.b0])
    for e in range(2):
        pb = 64 * e
        for (d0, s0) in ((0, 0), (3, 6)):
            fw.op("dve", lambda: nc.vector.tensor_copy(out=Qblk.ap[pb:pb + 64, :, d0:d0 + 3, e].rearrange("p s c -> p c s"), in_=QS.ap[pb:pb + 64, s0:s0 + 3, :]),
                  reads=[QS.b0], writes=[Qblk.b0])
    vnT = kb.sb(pes, "vnT", [128, 3, NS]); prod = kb.sb(pes, "prod", [128, 3, NS]); pself = kb.sb(pes, "pself", [128, 3, NS])
    for c in range(3):
        fw.op("pe", lambda c=c: nc.tensor.transpose(out=trb[0].ap[:, c * NS:(c + 1) * NS], in_=VN.ap[0:NS, c * 128:(c + 1) * 128], identity=ident_b.ap[0:NS, 0:NS]),
              reads=[VN.b0, ident_b.b0], writes=[trb[0].b0], inc=(c == 2))
    evac(vnT.ap[:, :, :], trb[0].ap[:, 0:3 * NS].rearrange("p (c s) -> p c s", c=3), [trb[0].b0], [vnT.b0])
    fw.op("dve", lambda: nc.vector.tensor_tensor(out=prod.ap[:, :, :], in0=QS.ap[:, 0:3, :], in1=QS.ap[:, 3:6, :], op=ALU.mult), reads=[QS.b0], writes=[prod.b0])
    fw.op("pe", lambda: nc.tensor.matmul(misc[0].ap[:, 0:3 * NS], lhsT=hsel.ap[:, :], rhs=prod.ap[:, :, :].rearrange("p c s -> p (c s)"), start=True, stop=True),
          reads=[hsel.b0, prod.b0], writes=[misc[0].b0])
    fw.op("act", lambda: nc.scalar.activation(out=pself.ap[:, :, :].rearrange("p c s -> p (c s)"), in_=misc[0].ap[:, 0:3 * NS], func=AF.Exp),
          reads=[misc[0].b0], writes=[pself.b0])
    segg = kb.sb(pes, "segg", [128, H6])
    fw.op("dve", lambda: nc.vector.memset(segg.ap[:], 1.0), writes=[segg.b0])
    fw.op("dve", lambda: nc.vector.memset(segg.ap[:, :].rearrange("p (h j) -> p h j", j=NPG)[:, :, 0:1], 0.0), writes=[segg.b0])

    def tmp(name, dt=F32, n=H6):
        return [kb.sb(pes, name, [128, n], dt) for _ in range(2)]
    Ee, SPf, SPb, Csb, Incl, Wa, Wb, Asb = tmp("Ee"), tmp("SPf"), tmp("SPb", BF16), tmp("Csb"), tmp("Incl"), tmp("Wa"), tmp("Wb"), tmp("Asb", BF16)
    Zc, G8, Mx, Ng, Wm, Pm = tmp("Zc"), tmp("G8", F32, 48), tmp("Mx", F32, 48), tmp("Ng", F32, 48), tmp("Wm"), tmp("Pm", BF16)
    gi = 0
    for s in range(NS):
        k2 = s % 2
        V = Vs[k2]
        Zb = Zs[k2]
        for j in range(NPG):
            kp = kpg[gi % 4]
            tb = trb[gi % 2]
            kt = kT[gi % 3]
            gi += 1
            col = s * NPG + j
            fw.gather(kp.ap[:, :], ck[:, :], idx.ap[:, col:col + 1], reads=[idx.b0], writes=[kp.b0], element_offset=l * kb.NPOOL * 128 * 768)
            fw.gather(V.ap[:, j, :], cv[:, :], idx.ap[:, col:col + 1], reads=[idx.b0], writes=[V.b[j]], element_offset=l * kb.NPOOL * 128 * 768)
            for cc in range(6):
                fw.op("pe", lambda cc=cc: nc.tensor.transpose(out=tb.ap[:, cc * 128:(cc + 1) * 128], in_=kp.ap[:, cc * 128:(cc + 1) * 128], identity=ident_b.ap[:, :]),
                      reads=[kp.b0, ident_b.b0], writes=[tb.b0], inc=(cc == 5))
            evac(kt.ap[:, :, :], tb.ap[:, 0:768].rearrange("p (c r) -> p c r", c=6), [tb.b0], [kt.b0])
            for cc in range(6):
                if cc < 3:
                    o = Zb.ap[:, 0:H6].rearrange("p (h j) -> p h j", j=NPG)[:, 2 * cc:2 * cc + 2, j]
                else:
                    o = Zb.ap[:, H6:2 * H6].rearrange("p (h j) -> p h j", j=NPG)[:, 2 * (cc - 3):2 * (cc - 3) + 2, NPG - 1 - j]
                fw.op("pe", lambda cc=cc, o=o: nc.tensor.matmul(o, lhsT=kt.ap[:, cc, :], rhs=Qblk.ap[:, s, cc, :], start=True, stop=True),
                      reads=[kt.b0, Qblk.b0], writes=[Zb.b0], inc=(cc == 5))
        m0, m1 = misc[0], misc[1]
        fw.op("act", lambda: nc.scalar.activation(out=Ee[k2].ap[:, :], in_=Zb.ap[:, H6:2 * H6], func=AF.Exp), reads=[Zb.b0], writes=[Ee[k2].b0])
        fw.op("act", lambda: nc.scalar.activation(out=SPf[k2].ap[:, :], in_=Ee[k2].ap[:, :], func=AF.Ln, bias=cvals.ap[:, 0:1]), reads=[Ee[k2].b0, cvals.b0], writes=[SPf[k2].b0])
        fw.op("dve", lambda: nc.vector.tensor_copy(out=SPb[k2].ap[:, :], in_=SPf[k2].ap[:, :]), reads=[SPf[k2].b0], writes=[SPb[k2].b0])
        fw.op("pe", lambda: nc.tensor.matmul(m0.ap[:, 0:H6], lhsT=trineg.ap[:, :], rhs=SPb[k2].ap[:, :], start=True, stop=True), reads=[trineg.b0, SPb[k2].b0], writes=[m0.b0])
        fw.op("pe", lambda: nc.tensor.matmul(m1.ap[:, 0:H6], lhsT=ones_f.ap[:, :], rhs=SPf[k2].ap[:, :], start=True, stop=True), reads=[ones_f.b0, SPf[k2].b0], writes=[m1.b0])
        fw.op("act", lambda: nc.scalar.copy(out=Csb[k2].ap[:, :], in_=m1.ap[:, 0:H6]), reads=[m1.b0], writes=[Csb[k2].b0])
        fw.op("dve", lambda: nc.vector.tensor_tensor_scan(out=Incl[k2].ap[:, :], data0=segg.ap[:, :], data1=Csb[k2].ap[:, :], initial=0.0, op0=ALU.mult, op1=ALU.add),
              reads=[segg.b0, Csb[k2].b0], writes=[Incl[k2].b0])
        fw.op("dve", lambda: nc.vector.tensor_tensor(out=Wa[k2].ap[:, :], in0=Zb.ap[:, H6:2 * H6], in1=Incl[k2].ap[:, :], op=ALU.subtract), reads=[Zb.b0, Incl[k2].b0], writes=[Wa[k2].b0])
        fw.op("dve", lambda: nc.vector.tensor_tensor(out=Wb[k2].ap[:, :], in0=m0.ap[:, 0:H6], in1=Csb[k2].ap[:, :], op=ALU.add), reads=[m0.b0, Csb[k2].b0], writes=[Wb[k2].b0])
        fw.op("dve", lambda: nc.vector.tensor_tensor(out=Wa[k2].ap[:, :], in0=Wa[k2].ap[:, :], in1=Wb[k2].ap[:, :], op=ALU.add), reads=[Wa[k2].b0, Wb[k2].b0], writes=[Wa[k2].b0])
        fw.op("act", lambda: nc.scalar.activation(out=Asb[k2].ap[:, :], in_=Wa[k2].ap[:, :], func=AF.Exp), reads=[Wa[k2].b0], writes=[Asb[k2].b0])
        Av = Asb[k2].ap[:, :].rearrange("p (h j) -> p h j", j=NPG)
        for c3 in range(3):
            for j in range(NPG):
                fw.op("pe", lambda: nc.tensor.matmul(Oall.ap[:, s * 6 + 2 * c3:s * 6 + 2 * c3 + 2], lhsT=V.ap[:, j, 384 + c3 * 128:384 + (c3 + 1) * 128],
                                                     rhs=Av[:, 2 * c3:2 * c3 + 2, NPG - 1 - j], start=(j == 0), stop=(j == NPG - 1)),
                      reads=[V.b[j], Asb[k2].b0], writes=[Oall.b0], inc=(j == NPG - 1))
        fw.op("act", lambda: nc.scalar.copy(out=Zc[k2].ap[:, :], in_=Zb.ap[:, 0:H6]), reads=[Zb.b0], writes=[Zc[k2].b0])
        fw.op("pe", lambda: nc.tensor.matmul(m1.ap[:, 256:256 + H6], lhsT=ones_f.ap[:, :], rhs=Zc[k2].ap[:, :], start=True, stop=True), reads=[ones_f.b0, Zc[k2].b0], writes=[m1.b0])
        fw.op("dve", lambda: nc.vector.memset(G8[k2].ap[:, :], -1e30), writes=[G8[k2].b0])
        gv = m1.ap[:, 256:256 + H6].rearrange("p (h n two) -> p h n two", h=6, two=2)
        fw.op("dve", lambda: nc.vector.tensor_copy(out=G8[k2].ap[:, :].rearrange("p (h n) -> p h n", h=6)[:, :, 0:NBK].unsqueeze(3), in_=gv[:, :, :, 0:1]), reads=[m1.b0], writes=[G8[k2].b0])
        fw.op("dve", lambda: nc.vector.tensor_tensor(out=G8[k2].ap[:, :].rearrange("p (h n) -> p h n", h=6)[:, :, 0:NBK].unsqueeze(3),
                                                     in0=gv[:, :, :, 1:2], in1=G8[k2].ap[:, :].rearrange("p (h n) -> p h n", h=6)[:, :, 0:NBK].unsqueeze(3), op=ALU.add),
              reads=[m1.b0, G8[k2].b0], writes=[G8[k2].b0])
        for h in range(6):
            fw.op("dve", lambda h=h: nc.vector.max(out=Mx[k2].ap[:, h * 8:(h + 1) * 8], in_=G8[k2].ap[:, h * 8:(h + 1) * 8]), reads=[G8[k2].b0], writes=[Mx[k2].b0])
        for h in range(6):
            fw.op("dve", lambda h=h: nc.vector.tensor_scalar(out=Ng[k2].ap[:, h * 8:(h + 1) * 8], in0=G8[k2].ap[:, h * 8:(h + 1) * 8], scalar1=Mx[k2].ap[:, h * 8 + 2:h * 8 + 3],
                                                             scalar2=NEG, op0=ALU.is_lt, op1=ALU.mult), reads=[G8[k2].b0, Mx[k2].b0], writes=[Ng[k2].b0])
        fw.op("dve", lambda: nc.vector.tensor_tensor(out=Wm[k2].ap[:, :].rearrange("p (h n two) -> p h n two", h=6, two=2),
                                                     in0=Zb.ap[:, 0:H6].rearrange("p (h n two) -> p h n two", h=6, two=2),
                                                     in1=Ng[k2].ap[:, :].rearrange("p (h n) -> p h n", h=6)[:, :, 0:NBK].unsqueeze(3).to_broadcast([128, 6, NBK, 2]), op=ALU.add),
              reads=[Zb.b0, Ng[k2].b0], writes=[Wm[k2].b0])
        fw.op("act", lambda: nc.scalar.activation(out=Pm[k2].ap[:, :], in_=Wm[k2].ap[:, :], func=AF.Exp), reads=[Wm[k2].b0], writes=[Pm[k2].b0])
        Pv = Pm[k2].ap[:, :].rearrange("p (h j) -> p h j", j=NPG)
        for c3 in range(3):
            for j in range(NPG):
                fw.op("pe", lambda: nc.tensor.matmul(Oall.ap[:, 6 * NS + s * 6 + 2 * c3:6 * NS + s * 6 + 2 * c3 + 2], lhsT=V.ap[:, j, c3 * 128:(c3 + 1) * 128],
                                                     rhs=Pv[:, 2 * c3:2 * c3 + 2, j], start=(j == 0), stop=(j == NPG - 1)),
                      reads=[V.b[j], Pm[k2].b0], writes=[Oall.b0], inc=(j == NPG - 1))
        for j in range(NPG):
            fw.op("pe", lambda: nc.tensor.matmul(Oall.ap[:, 12 * NS + s * 6:12 * NS + s * 6 + 6], lhsT=ones_b.ap[:, :], rhs=Pv[:, 0:6, j], start=(j == 0), stop=(j == NPG - 1)),
                  reads=[ones_b.b0, Pm[k2].b0], writes=[Oall.b0], inc=(j == NPG - 1))
    num = kb.sb(pes, "fnum", [128, 3, NS]); den = kb.sb(pes, "fden", [128, 3, NS])
    for c3 in range(3):
        for e in range(2):
            pb = 64 * e
            sbv = Oall.ap[pb:pb + 64, 0:6 * NS].rearrange("p (s x) -> p s x", x=6)[:, :, 2 * c3 + e]
            fw.op("dve", lambda: nc.vector.tensor_copy(out=XT.ap[pb:pb + 64, 5 + c3, SP:SP + NS], in_=sbv), reads=[Oall.b0], writes=[XT.b[NT]])
            mov = Oall.ap[pb:pb + 64, 6 * NS:12 * NS].rearrange("p (s x) -> p s x", x=6)[:, :, 2 * c3 + e]
            dnv = Oall.ap[pb:pb + 64, 12 * NS:18 * NS].rearrange("p (s x) -> p s x", x=6)[:, :, 2 * c3 + e]
            fw.op("dve", lambda: nc.vector.tensor_tensor(out=num.ap[pb:pb + 64, c3, :], in0=pself.ap[pb:pb + 64, c3, :], in1=vnT.ap[pb:pb + 64, c3, :], op=ALU.mult),
                  reads=[pself.b0, vnT.b0], writes=[num.b0])
            fw.op("dve", lambda: nc.vector.tensor_tensor(out=num.ap[pb:pb + 64, c3, :], in0=mov, in1=num.ap[pb:pb + 64, c3, :], op=ALU.add),
                  reads=[Oall.b0, num.b0], writes=[num.b0])
            fw.op("dve", lambda: nc.vector.tensor_tensor(out=den.ap[pb:pb + 64, c3, :], in0=dnv, in1=pself.ap[pb:pb + 64, c3, :], op=ALU.add),
                  reads=[Oall.b0, pself.b0], writes=[den.b0])
    fw.op("dve", lambda: nc.vector.reciprocal(out=den.ap[:, :, :], in_=den.ap[:, :, :]), reads=[den.b0], writes=[den.b0])
    fw.op("dve", lambda: nc.vector.tensor_tensor(out=XT.ap[:, 0:3, SP:SP + NS], in0=num.ap[:, :, :], in1=den.ap[:, :, :], op=ALU.mult), reads=[num.b0, den.b0], writes=[XT.b[NT]])


def core_in_map(inp, c, NT, NS, NPG, NPOOL, consts, depth=2):
    f32 = lambda a: np.ascontiguousarray(np.asarray(a, dtype=np.float32))
    m = {}
    m["x_p"] = f32(inp["x_prompt"][c]).reshape(NT * 128, D)
    m["x_s"] = f32(inp["x_sample"][c * NS:(c + 1) * NS]).reshape(NS, D)
    m["ck"] = inp["_ck"]
    m["cv"] = inp["_cv"]
    m["sconv"] = f32(inp["state_conv"][:, c * NS:(c + 1) * NS]).reshape(depth, NS * 30, 256)
    m["ptab"] = np.ascontiguousarray(np.asarray(inp["page_table"][c * NS:(c + 1) * NS], dtype=np.int32)).reshape(-1)
    for n in ("w_in", "w_out", "w_up", "w_down", "conv_w", "conv_b", "conv_g", "g_pre_mix", "g_post_mix", "g_pre_mlp", "g_post_mlp"):
        m[n] = inp["_" + n]
    for n, v in consts.items():
        m["c_" + n] = v
    return m


_NC_CACHE = {}


def run(inp, NT, NS, NPG, NPOOL, n_cores, do_sample=True, depth=2):
    key = (NT, NS, NPG, NPOOL, do_sample)
    if key not in _NC_CACHE:
        _NC_CACHE[key] = build(NT, NS, NPG, NPOOL, depth, do_sample)
    nc = _NC_CACHE[key]
    consts = make_consts(NT, NPG * 128)
    inp = dict(inp)
    f32 = lambda a: np.ascontiguousarray(np.asarray(a, dtype=np.float32))
    inp["_ck"] = f32(inp["cache_k"]).reshape(depth * NPOOL * 128, 768)
    inp["_cv"] = f32(inp["cache_v"]).reshape(depth * NPOOL * 128, 768)
    for n in ("w_in", "w_out", "w_up", "w_down", "conv_w", "conv_b", "conv_g", "g_pre_mix", "g_post_mix", "g_pre_mlp", "g_post_mlp"):
        inp["_" + n] = f32(inp[n])
    in_maps = [core_in_map(inp, c, NT, NS, NPG, NPOOL, consts, depth) for c in range(n_cores)]
    res = run_bass_kernel_spmd(nc, in_maps, core_ids=list(range(n_cores)))
    rs = res.results
    SP = NT * 128
    y_p = np.stack([r["y_p"] for r in rs]).reshape(n_cores, SP, D)
    y_s = np.concatenate([r["y_s"] for r in rs]).reshape(n_cores * NS, 1, D)
    kr_p = np.stack([r["kr_p"] for r in rs], axis=1).reshape(depth, n_cores, SP, 12, 64)
    vr_p = np.stack([r["vr_p"] for r in rs], axis=1).reshape(depth, n_cores, SP, 12, 64)
    cv_p = np.stack([r["cv_p"] for r in rs], axis=1).reshape(depth, n_cores, 30, 256)
    kr_s = np.concatenate([r["kr_s"] for r in rs], axis=1).reshape(depth, n_cores * NS, 1, 12, 64)
    vr_s = np.concatenate([r["vr_s"] for r in rs], axis=1).reshape(depth, n_cores * NS, 1, 12, 64)
    cv_s = np.concatenate([r["cv_s"].reshape(depth, NS, 30, 256) for r in rs], axis=1)
    return (y_p, y_s, kr_p, vr_p, cv_p, kr_s, vr_s, cv_s)


def kernel(x_prompt, x_sample, cache_k, cache_v, state_conv, page_table, w_in, w_out, conv_w, conv_b, conv_g,
           w_up, w_down, g_pre_mix, g_post_mix, g_pre_mlp, g_post_mlp):
    inp = dict(x_prompt=x_prompt, x_sample=x_sample, cache_k=cache_k, cache_v=cache_v, state_conv=state_conv,
               page_table=page_table, w_in=w_in, w_out=w_out, conv_w=conv_w, conv_b=conv_b, conv_g=conv_g,
               w_up=w_up, w_down=w_down, g_pre_mix=g_pre_mix, g_post_mix=g_post_mix, g_pre_mlp=g_pre_mlp, g_post_mlp=g_post_mlp)
    n_cores = 8
    NT = x_prompt.shape[1] // 128
    NS = x_sample.shape[0] // n_cores
    NPG = page_table.shape[1]
    NPOOL = cache_k.shape[1]
    return run(inp, NT, NS, NPG, NPOOL, n_cores)
```

```python
import contextlib
import os
import numpy as np
import concourse.bass as bass
import concourse.mybir as mybir
from concourse.bass_utils import run_bass_kernel_spmd

F32 = mybir.dt.float32
BF16 = mybir.dt.bfloat16
I32 = mybir.dt.int32
AF = mybir.ActivationFunctionType
ALU = mybir.AluOpType
AX = mybir.AxisListType

D = 1024
DIN = 2816
DFF = 4096
HD = 64
NEG = -30000.0
EPS = 1e-6
QA, KA, VA, UV, UG, QC, KC, VC = 0, 384, 768, 1152, 1408, 1664, 2048, 2432


class Lane:
    __slots__ = ("sem", "val", "inc")

    def __init__(self, sem, inc):
        self.sem, self.val, self.inc = sem, 0, inc


class Buf:
    __slots__ = ("w", "r", "excl")

    def __init__(self, excl=False):
        self.w = None
        self.r = {}
        self.excl = excl


class FW:
    NDMA = 8

    def __init__(self, nc, es):
        self.nc = nc
        self.engs = {"pe": nc.tensor, "act": nc.scalar, "dve": nc.vector, "pool": nc.gpsimd, "sp": nc.sync}
        self.lanes = {k: Lane(es.enter_context(nc.semaphore("s_" + k)), 1) for k in ("pe", "act", "dve", "pool")}
        self.dlanes = {q: [Lane(es.enter_context(nc.semaphore(f"d_{q}{i}")), 16) for i in range(self.NDMA)]
                       for q in ("sp", "pool", "act")}
        self.rr = {"sp": 0, "pool": 0, "act": 0}
        self.known = {k: {} for k in self.engs}
        self.n_inst = 0

    def _wait(self, issuer, lane, val):
        k = self.known[issuer]
        if k.get(lane, 0) < val:
            self.engs[issuer].wait_ge(lane.sem, val)
            k[lane] = val

    def _deps(self, issuer, reads, writes, own=None):
        for b in reads:
            if b.w is not None and b.w[0] is not own:
                self._wait(issuer, b.w[0], b.w[1])
        for b in writes:
            if b.w is not None and b.w[0] is not own:
                self._wait(issuer, b.w[0], b.w[1])
            for lane, val in b.r.items():
                if lane is not own:
                    self._wait(issuer, lane, val)

    @staticmethod
    def _commit(lane, val, reads, writes):
        for b in writes:
            b.w = (lane, val)
            b.r = {}
        for b in reads:
            if b.r.get(lane, 0) < val:
                b.r[lane] = val

    def op(self, eng, fn, reads=(), writes=(), inc=True):
        lane = self.lanes[eng]
        if any(b.excl for b in reads):
            writes = list(writes) + [b for b in reads if b.excl]
            reads = [b for b in reads if not b.excl]
        self._deps(eng, reads, writes, own=lane if eng == "pe" else None)
        inst = fn()
        self.n_inst += 1
        if inc:
            lane.val += 1
            inst.then_inc(lane.sem, 1)
            self._commit(lane, lane.val, reads, writes)
        else:
            self._commit(lane, lane.val + 1, reads, writes)
        return inst

    def dma(self, q, out, in_, reads=(), writes=(), **kw):
        lanes = self.dlanes[q]
        lane = lanes[self.rr[q] % self.NDMA]
        self.rr[q] += 1
        self._deps(q, reads, writes)
        self._wait(q, lane, lane.val)
        lane.val += 16
        self.engs[q].dma_start(out=out, in_=in_, **kw).then_inc(lane.sem, 16)
        self.n_inst += 1
        self._commit(lane, lane.val, reads, writes)

    def gather(self, out, in_, idx_ap, reads=(), writes=(), element_offset=0):
        q = "pool"
        lanes = self.dlanes[q]
        lane = lanes[self.rr[q] % self.NDMA]
        self.rr[q] += 1
        self._deps(q, reads, writes)
        self._wait(q, lane, lane.val)
        lane.val += 16
        self.nc.gpsimd.indirect_dma_start(out=out, out_offset=None, in_=in_,
                                          in_offset=bass.IndirectOffsetOnAxis(ap=idx_ap, axis=0),
                                          element_offset=element_offset).then_inc(lane.sem, 16)
        self.n_inst += 1
        self._commit(lane, lane.val, reads, writes)

    def barrier(self):
        all_l = list(self.lanes.values()) + [l for ls in self.dlanes.values() for l in ls]
        for issuer in ("pe", "act", "dve", "pool", "sp"):
            for l in all_l:
                if l.val > 0 and not (issuer in self.lanes and self.lanes[issuer] is l):
                    self._wait(issuer, l, l.val)

    def finish(self):
        for ls in self.dlanes.values():
            for l in ls:
                if l.val > 0:
                    self._wait("sp", l, l.val)
        for l in self.lanes.values():
            if l.val > 0:
                self._wait("sp", l, l.val)


def emit_pipelined(items, skew=1):
    n = len(items)
    K = max(len(it) for it in items) if items else 0
    for slot in range(n + (K - 1) * skew):
        for k in range(K):
            i = slot - k * skew
            if 0 <= i < n and k < len(items[i]):
                items[i][k]()


class T:
    def __init__(self, ap, nb=1, excl=False):
        self.ap = ap
        self.b = [Buf(excl) for _ in range(nb)]
        self.b0 = self.b[0]

    def __getitem__(self, k):
        return self.ap[k]


def make_consts(NT, past_len):
    c = {}
    c["ident"] = np.eye(128, dtype=np.float32)
    j = np.arange(128)
    c["trineg"] = -(j[:, None] >= j[None, :]).astype(np.float32)
    seln = np.zeros((128, 16, 128), np.float32)
    for kj in range(16):
        seln[:, kj, kj] = -1.0
    c["selneg"] = seln.reshape(128, 2048)
    c["su16"] = (np.arange(16)[:, None] > np.arange(16)[None, :]).astype(np.float32)
    selb = np.zeros((128, 16, 128), np.float32)
    for kj in range(16):
        selb[kj, kj, :] = 1.0
    c["selb"] = selb.reshape(128, 2048)
    e48 = np.zeros((128, 48, 128), np.float32)
    for r in range(48):
        e48[r, r, :] = 1.0
    c["e48"] = e48.reshape(128, 48 * 128)
    p = np.arange(128)[:, None]
    f = np.arange(512)[None, :]
    cm = np.zeros((128, 8, 512), np.float32)
    for r in range(4):
        cm[:, r, :] = np.where((128 * r + p) < f, 0.0, NEG)
        cm[:, 4 + r, :] = np.where((128 * r + p) <= f, 0.0, NEG)
    c["cmask"] = cm.reshape(128, 8 * 512)
    half = 8
    inv_freq = (np.float32(500000.0) ** (-np.arange(half, dtype=np.float32) / np.float32(half))).astype(np.float32)
    pos = np.zeros((128, NT + 1), np.float32)
    for t in range(NT):
        pos[:, t] = t * 128 + np.arange(128)
    pos[:, NT] = past_len
    ang = pos[:, :, None].astype(np.float32) * inv_freq[None, None, :]
    cs, sn = np.cos(ang).astype(np.float32), np.sin(ang).astype(np.float32)
    c["ropec"] = np.concatenate([cs, cs], axis=2).reshape(128, (NT + 1) * 16)
    c["ropes"] = np.concatenate([sn, sn], axis=2).reshape(128, (NT + 1) * 16)
    gb = np.zeros((128, 8, 8), np.float32)
    for qb in range(8):
        for n in range(8):
            gb[:, qb, n] = 0.0 if n < qb else (1e30 if n == qb else -1e30)
    c["gbias"] = gb.reshape(128, 64)
    c["piota"] = np.arange(128, dtype=np.float32).reshape(128, 1)
    hs = np.zeros((128, 128), np.float32)
    hs[0:64, 0:64] = 1.0
    hs[64:128, 64:128] = 1.0
    c["hsel"] = hs
    c["pcol"] = np.stack([(np.arange(128) >= 64).astype(np.float32), (np.arange(128) % 64).astype(np.float32)], axis=1)
    c["tristr"] = -(j[:, None] > j[None, :]).astype(np.float32)
    return c


CONST_SHAPES = lambda NT: {"ident": [128, 128], "trineg": [128, 128], "selneg": [128, 2048], "su16": [16, 16],
                           "selb": [128, 2048], "e48": [128, 6144], "cmask": [128, 4096],
                           "ropec": [128, (NT + 1) * 16], "ropes": [128, (NT + 1) * 16], "gbias": [128, 64],
                           "piota": [128, 1], "hsel": [128, 128], "pcol": [128, 2], "tristr": [128, 128]}


class KB:
    def __init__(self, NT, NS, NPG, NPOOL, depth=2):
        self.NT, self.NS, self.NPG, self.NPOOL, self.depth = NT, NS, NPG, NPOOL, depth
        self.NTOK = NT * 128 + NS
        self.nc = bass.Bass("TRN2", target_bir_lowering=False)
        self.es = contextlib.ExitStack()
        self.fw = FW(self.nc, self.es)
        self.cnt = 0

    def dram(self, name, shape, dt=F32, kind="ExternalInput"):
        return self.nc.dram_tensor(name, list(shape), dt, kind=kind).ap()

    def sb(self, es, name, shape, dt=F32, nb=1):
        self.cnt += 1
        return T(es.enter_context(self.nc.sbuf_tensor(f"{name}_{self.cnt}", list(shape), dt)), nb)

    def ps(self, es, name, shape=(128, 512), dt=F32):
        self.cnt += 1
        return T(es.enter_context(self.nc.psum_tensor(f"{name}_{self.cnt}", list(shape), dt)), 1, excl=True)

    def rows(self, t):
        return 128 if t < self.NT else self.NS


def build(NT=16, NS=16, NPG=16, NPOOL=2560, depth=2, do_sample=True):
    import os
    UPTO = int(os.environ.get('UPTO', '99'))
    kb = KB(NT, NS, NPG, NPOOL, depth)
    nc, fw, es = kb.nc, kb.fw, kb.es
    NTOK = kb.NTOK
    NB = NT // 4
    SP = NT * 128
    x_p = kb.dram("x_p", [SP, D]); x_s = kb.dram("x_s", [NS, D])
    ck = kb.dram("ck", [depth * NPOOL * 128, 768]); cv = kb.dram("cv", [depth * NPOOL * 128, 768])
    sconv = kb.dram("sconv", [depth, NS * 30, 256]); ptab = kb.dram("ptab", [NS * NPG], I32)
    w_in = kb.dram("w_in", [depth, D, DIN]); w_out = kb.dram("w_out", [depth, D, D])
    w_up = kb.dram("w_up", [depth, D, DFF]); w_down = kb.dram("w_down", [depth, DFF, D])
    conv_w = kb.dram("conv_w", [depth, 31, 256]); conv_b = kb.dram("conv_b", [depth, 256]); conv_g = kb.dram("conv_g", [depth, 256])
    gvec = {n: kb.dram(n, [depth, D]) for n in ("g_pre_mix", "g_post_mix", "g_pre_mlp", "g_post_mlp")}
    cdram = {n: kb.dram("c_" + n, shp) for n, shp in CONST_SHAPES(NT).items()}
    y_p = kb.dram("y_p", [SP, D], kind="ExternalOutput"); y_s = kb.dram("y_s", [NS, D], kind="ExternalOutput")
    kr_p = kb.dram("kr_p", [depth, SP, 768], kind="ExternalOutput"); vr_p = kb.dram("vr_p", [depth, SP, 768], kind="ExternalOutput")
    cv_p = kb.dram("cv_p", [depth, 30, 256], kind="ExternalOutput")
    kr_s = kb.dram("kr_s", [depth, NS, 768], kind="ExternalOutput"); vr_s = kb.dram("vr_s", [depth, NS, 768], kind="ExternalOutput")
    cv_s = kb.dram("cv_s", [depth, NS * 30, 256], kind="ExternalOutput")
    hbuf = kb.dram("hbuf", [SP + 128, D], kind="Internal")
    hb = [Buf() for _ in range(NT + 1)]
    outb = Buf()

    def hsrc(l, t):
        R = kb.rows(t)
        if l == 0:
            return x_p[t * 128:(t + 1) * 128, :] if t < NT else x_s[0:NS, :]
        return hbuf[t * 128:t * 128 + R, :]

    def hdst(l, t, final):
        R = kb.rows(t)
        if final:
            return y_p[t * 128:(t + 1) * 128, :] if t < NT else y_s[0:NS, :]
        return hbuf[t * 128:t * 128 + R, :]

    cst = {}
    for n, dt in (("ident", F32), ("su16", F32), ("ropec", F32), ("ropes", F32), ("gbias", F32), ("hsel", F32)):
        cst[n] = kb.sb(es, "c_" + n, CONST_SHAPES(NT)[n], dt)
        fw.dma("sp", cst[n].ap[:], cdram[n][:, :], writes=[cst[n].b0])
    for n in ("ident", "trineg", "tristr", "selneg", "selb", "e48", "cmask"):
        cst[n + "_b"] = kb.sb(es, "cb_" + n, CONST_SHAPES(NT)[n], BF16)
        fw.dma("pool", cst[n + "_b"].ap[:], cdram[n][:, :], writes=[cst[n + "_b"].b0])
    ident_f, ident_b = cst["ident"], cst["ident_b"]
    ones_b = kb.sb(es, "ones_b", [128, 128], BF16)
    fw.op("dve", lambda: nc.vector.memset(ones_b.ap[:], 1.0), writes=[ones_b.b0])
    ones_f = kb.sb(es, "ones_f", [128, 128], F32)
    fw.op("dve", lambda: nc.vector.memset(ones_f.ap[:], 1.0), writes=[ones_f.b0])
    cvals = kb.sb(es, "cvals", [128, 4], F32)
    fw.op("dve", lambda: nc.vector.memset(cvals.ap[:, 0:1], 1.0), writes=[cvals.b0])
    fw.op("dve", lambda: nc.vector.memset(cvals.ap[:, 1:2], -0.5), writes=[cvals.b0])
    fw.op("dve", lambda: nc.vector.memset(cvals.ap[:, 2:3], EPS), writes=[cvals.b0])
    mhalf = kb.sb(es, "mhalf", [128, 512], F32)
    fw.op("pool", lambda: nc.gpsimd.memset(mhalf.ap[:], -0.5), writes=[mhalf.b0])
    NBK = NPG // 2
    idx_i = kb.sb(es, "idx_i", [128, NS * NPG], I32)
    idx_f = kb.sb(es, "idx_f", [128, NS * NPG], F32)
    idx_t = kb.sb(es, "idx_t", [128, NS * NBK], F32)
    idx = kb.sb(es, "idx2", [128, NS * NBK], I32)
    pcol = kb.sb(es, "pcol", [128, 2], F32)
    fw.dma("sp", pcol.ap[:], cdram["pcol"][:, :], writes=[pcol.b0])
    fw.dma("sp", idx_i.ap[:], ptab.partition_broadcast(128), writes=[idx_i.b0])
    fw.op("dve", lambda: nc.vector.tensor_copy(out=idx_f.ap[:], in_=idx_i.ap[:]), reads=[idx_i.b0], writes=[idx_f.b0])
    pv_ = idx_f.ap[:, :].rearrange("p (q two) -> p q two", two=2)
    fw.op("dve", lambda: nc.vector.tensor_tensor(out=idx_t.ap[:, :].unsqueeze(2), in0=pv_[:, :, 1:2], in1=pv_[:, :, 0:1], op=ALU.subtract), reads=[idx_f.b0], writes=[idx_t.b0])
    fw.op("dve", lambda: nc.vector.scalar_tensor_tensor(out=idx_t.ap[:, :].unsqueeze(2), in0=idx_t.ap[:, :].unsqueeze(2), scalar=pcol.ap[:, 0:1], in1=pv_[:, :, 0:1],
                                                        op0=ALU.mult, op1=ALU.add), reads=[idx_t.b0, pcol.b0, idx_f.b0], writes=[idx_t.b0])
    fw.op("dve", lambda: nc.vector.tensor_scalar(out=idx_t.ap[:], in0=idx_t.ap[:], scalar1=64.0, scalar2=pcol.ap[:, 1:2], op0=ALU.mult, op1=ALU.add),
          reads=[idx_t.b0, pcol.b0], writes=[idx_t.b0])
    fw.op("dve", lambda: nc.vector.tensor_copy(out=idx.ap[:], in_=idx_t.ap[:]), reads=[idx_t.b0], writes=[idx.b0])
    ev = [0]

    def evac(out, in_, reads, writes, scale=None):
        ev[0] += 1
        if ev[0] % 2 == 0:
            if scale is None:
                fw.op("act", lambda: nc.scalar.copy(out=out, in_=in_), reads=reads, writes=writes)
            else:
                fw.op("act", lambda: nc.scalar.activation(out=out, in_=in_, func=AF.Identity, scale=scale), reads=reads, writes=writes)
        else:
            if scale is None:
                fw.op("dve", lambda: nc.vector.tensor_copy(out=out, in_=in_), reads=reads, writes=writes)
            else:
                fw.op("dve", lambda: nc.vector.tensor_scalar(out=out, in0=in_, scalar1=scale, scalar2=None, op0=ALU.mult), reads=reads, writes=writes)

    def rstd_from_ssq(ssq_ap, ssq_b, R, n, tmp, rs):
        fw.op("dve", lambda: nc.vector.tensor_scalar(out=tmp.ap[0:R, 0:1], in0=ssq_ap, scalar1=1.0 / n, scalar2=EPS, op0=ALU.mult, op1=ALU.add),
              reads=[ssq_b], writes=[tmp.b0])
        fw.op("pool", lambda: nc.gpsimd.tensor_tensor(out=rs.ap[0:R, 0:1], in0=tmp.ap[0:R, 0:1], in1=cvals.ap[0:R, 1:2], op=ALU.pow),
              reads=[tmp.b0, cvals.b0], writes=[rs.b0])

    def norm_to_T(l, t, hT, gbc, XT, trb, pes_tmps):
        R = kb.rows(t)
        junk, ssq, tmp1, rs, abf = pes_tmps
        fw.op("act", lambda: nc.scalar.activation(out=junk.ap[0:R, :], in_=hT.ap[0:R, :], func=AF.Square, accum_out=ssq.ap[0:R, 0:1]),
              reads=[hT.b0], writes=[junk.b0, ssq.b0])
        rstd_from_ssq(ssq.ap[0:R, 0:1], ssq.b0, R, D, tmp1, rs)
        fw.op("dve", lambda: nc.vector.scalar_tensor_tensor(out=abf.ap[0:R, :], in0=hT.ap[0:R, :], scalar=rs.ap[0:R, 0:1], in1=gbc.ap[0:R, :],
                                                            op0=ALU.mult, op1=ALU.mult), reads=[hT.b0, rs.b0, gbc.b0], writes=[abf.b0])
        for c in range(8):
            fw.op("pe", lambda c=c: nc.tensor.transpose(out=trb.ap[:, c * 128:c * 128 + R], in_=abf.ap[0:R, c * 128:(c + 1) * 128],
                                                        identity=ident_b.ap[0:R, 0:R]), reads=[abf.b0, ident_b.b0], writes=[trb.b0], inc=(c == 7))
        evac(XT.ap[:, :, t * 128:t * 128 + R], trb.ap[:, :].rearrange("p (c r) -> p c r", c=8)[:, :, 0:R], [trb.b0], [XT.b[t]])

    for l in range(depth):
        final = (l == depth - 1)
        with contextlib.ExitStack() as les:
            XT = kb.sb(les, "XT", [128, 8, NTOK], BF16, nb=NT + 1)
            QS = kb.sb(les, "QS", [128, 12, NS], BF16)
            VN = kb.sb(les, "VN", [NS, 768], BF16)
            with contextlib.ExitStack() as aes:
                QKT = kb.sb(aes, "QKT", [128, 12, SP], BF16, nb=12 * (NT + 1))
                Vsb = kb.sb(aes, "Vsb", [128, NT, 768], BF16, nb=NT + 1)

                def qkb(ch, t):
                    return QKT.b[ch * (NT + 1) + t]
                with contextlib.ExitStack() as nes:
                    negT = kb.sb(nes, "negT", [128, SP], BF16, nb=NT)
                    fw.op("pool", lambda: nc.gpsimd.memset(negT.ap[:, :], 0.0), writes=list(negT.b))
                    with contextlib.ExitStack() as ues:
                        uT = kb.sb(ues, "uT", [128, 2, 30 + SP], BF16, nb=NT + 1)
                        usT = kb.sb(ues, "usT", [128, 2, NS, 31], F32)
                        fw.op("pool", lambda: nc.gpsimd.memset(uT.ap[:, :, 0:30], 0.0), writes=[uT.b[NT]])
                        with contextlib.ExitStack() as pes:
                            gbc = kb.sb(pes, "gbc", [128, D])
                            fw.dma("sp", gbc.ap[:], gvec["g_pre_mix"][l, :].partition_broadcast(128), writes=[gbc.b0])
                            hr = [kb.sb(pes, "hr", [128, D]) for _ in range(2)]
                            tmps = [(kb.sb(pes, "junk", [128, D], BF16), kb.sb(pes, "ssq", [128, 1]), kb.sb(pes, "tmp1", [128, 1]),
                                     kb.sb(pes, "rs", [128, 1]), kb.sb(pes, "abf", [128, D], BF16)) for _ in range(2)]
                            trbs = [kb.ps(pes, "trb", [128, 1024], BF16) for _ in range(2)]
                            for t in range(NT + 1):
                                R = kb.rows(t)
                                hT = hr[t % 2]
                                fw.dma("sp", hT.ap[0:R, :], hsrc(l, t), reads=[hb[t]], writes=[hT.b0])
                                norm_to_T(l, t, hT, gbc, XT, trbs[t % 2], tmps[t % 2])
                            fw.barrier()
                        with contextlib.ExitStack() as pes:
                          if UPTO >= 1:
                            phase_a1(kb, l, pes, XT, QKT, qkb, Vsb, negT, uT, usT, w_in, cst, ident_f, ident_b, evac,
                                     kr_p, vr_p, kr_s, vr_s, cv_p, cv_s, outb, QS, VN)
                            fw.barrier()
                        with contextlib.ExitStack() as pes:
                          if UPTO >= 2:
                            phase_conv(kb, l, pes, XT, uT, usT, sconv, conv_w, conv_b, conv_g, cv_s, outb, cst, ident_f, ident_b,
                                       ones_f, mhalf, cvals, evac)
                            fw.barrier()
                    with contextlib.ExitStack() as pes:
                      if UPTO >= 3:
                        phase_moba(kb, l, pes, XT, QKT, qkb, Vsb, negT, cst, ident_b, ones_b)
                        fw.barrier()
                with contextlib.ExitStack() as pes:
                  if UPTO >= 4:
                    phase_sb(kb, l, pes, XT, QKT, qkb, Vsb, cst, ident_b, cvals)
                    fw.barrier()
            with contextlib.ExitStack() as pes:
                if do_sample and UPTO >= 5:
                    phase_sample(kb, l, pes, XT, QS, VN, ck, cv, idx, cst, ident_b, ones_b, ones_f, cvals, evac)
                else:
                    for ch in (0, 1, 2, 5, 6, 7):
                        fw.op("pool", lambda ch=ch: nc.gpsimd.memset(XT.ap[:, ch, SP:SP + NS], 0.0), writes=[XT.b[NT]])
                fw.barrier()
            with contextlib.ExitStack() as pes:
              if UPTO >= 5:
                phase_mix(kb, l, pes, XT, w_out, gvec, hsrc, hbuf, hb, ident_b, evac, norm_to_T, rstd_from_ssq)
                fw.barrier()
            with contextlib.ExitStack() as pes:
              if UPTO >= 6:
                phase_mlp(kb, l, pes, XT, w_up, w_down, gvec, hbuf, hb, hdst, outb, final, evac, rstd_from_ssq)
                fw.barrier()
    fw.finish()
    return nc


def phase_a1(kb, l, pes, XT, QKT, qkb, Vsb, negT, uT, usT, w_in, cst, ident_f, ident_b, evac,
             kr_p, vr_p, kr_s, vr_s, cv_p, cv_s, outb, QS, VN):
    nc, fw = kb.nc, kb.fw
    NT, NS = kb.NT, kb.NS
    SP = NT * 128
    wr = [kb.sb(pes, "wg", [128, 8, 512], BF16) for _ in range(2)]
    stgs = [kb.sb(pes, "stg", [128, 512]) for _ in range(3)]
    cb16 = [kb.sb(pes, "cb16", [128, 384], BF16) for _ in range(3)]
    rtmp = kb.sb(pes, "rtmp", [128, 2, 96])
    qaT = kb.sb(pes, "qaT", [128, 3, 128])
    ksum = kb.sb(pes, "ksum", [128, 3, max(NT, 2)])
    kmT = kb.sb(pes, "kmT", [128, 3, 8])
    g1 = kb.sb(pes, "g1", [128, 48]); mx = kb.sb(pes, "mx", [128, 48]); thr = kb.sb(pes, "thr", [128, 6])
    negm = kb.sb(pes, "negm", [128, 48], BF16)
    gtmp = kb.sb(pes, "gtmp", [128, 256])
    mm = [kb.ps(pes, "mm") for _ in range(2)]
    fbs = [kb.ps(pes, "fb") for _ in range(2)]
    trb = [kb.ps(pes, "trb", [128, 1024], BF16) for _ in range(2)]
    gbank = kb.ps(pes, "gbank"); gb2 = kb.ps(pes, "gb2")
    ropec, ropes, gbias = cst["ropec"], cst["ropes"], cst["gbias"]
    cnt = {"mm": 0, "stg": 0, "cb": 0, "tr": 0, "fb": 0}

    def nxt(k, lst):
        cnt[k] += 1
        return lst[cnt[k] % len(lst)]

    fw.op("dve", lambda: nc.vector.memset(kmT.ap[:], 0.0), writes=[kmT.b0])

    def rope(stg, t, R):
        X = stg.ap[0:R, 0:384].rearrange("p (h d) -> p h d", h=6)
        A = rtmp.ap[0:R, 0, :].rearrange("p (h d) -> p h d", h=6)
        B = rtmp.ap[0:R, 1, :].rearrange("p (h d) -> p h d", h=6)
        cc = ropec.ap[0:R, t * 16:(t + 1) * 16].unsqueeze(1).to_broadcast([R, 6, 16])
        ss = ropes.ap[0:R, t * 16:(t + 1) * 16].unsqueeze(1).to_broadcast([R, 6, 16])
        fw.op("dve", lambda: nc.vector.tensor_tensor(out=A, in0=X[:, :, 0:16], in1=cc, op=ALU.mult), reads=[stg.b0, ropec.b0], writes=[rtmp.b0])
        fw.op("dve", lambda: nc.vector.tensor_tensor(out=B, in0=X[:, :, 0:16], in1=ss, op=ALU.mult), reads=[stg.b0, ropes.b0], writes=[rtmp.b0])
        fw.op("dve", lambda: nc.vector.tensor_tensor(out=X[:, :, 0:8], in0=A[:, :, 0:8], in1=B[:, :, 8:16], op=ALU.subtract), reads=[rtmp.b0], writes=[stg.b0])
        fw.op("dve", lambda: nc.vector.tensor_tensor(out=X[:, :, 8:16], in0=A[:, :, 8:16], in1=B[:, :, 0:8], op=ALU.add), reads=[rtmp.b0], writes=[stg.b0])

    def to_T_b(cb, tb, ncol, R, dst, dch0, dcol0, dbufs):
        nch = ncol // 128
        for c in range(nch):
            fw.op("pe", lambda c=c: nc.tensor.transpose(out=tb.ap[:, c * 128:c * 128 + R], in_=cb.ap[0:R, c * 128:(c + 1) * 128],
                                                        identity=ident_b.ap[0:R, 0:R]), reads=[cb.b0, ident_b.b0], writes=[tb.b0], inc=(c == nch - 1))
        evac(dst.ap[:, dch0:dch0 + nch, dcol0:dcol0 + R], tb.ap[:, 0:nch * 128].rearrange("p (c r) -> p c r", c=nch)[:, :, 0:R], [tb.b0], dbufs)

    def fp32_T(stg, fb, n):
        for c in range(n):
            fw.op("pe", lambda c=c: nc.tensor.transpose(out=fb.ap[:, c * 128:(c + 1) * 128], in_=stg.ap[0:128, c * 128:(c + 1) * 128],
                                                        identity=ident_f.ap[:, :]), reads=[stg.b0, ident_f.b0], writes=[fb.b0], inc=(c == n - 1))

    def make_unit(gi, kind, col0, ncol, t, last_of_group):
        R = kb.rows(t)
        prompt = t < NT
        wg = wr[gi % 2]
        bank = nxt("mm", mm)
        stg = nxt("stg", stgs)
        rowsl = slice(t * 128, (t + 1) * 128)
        needs_T = kind in ("ka", "kc", "qa", "qc") or (kind == "u" and prompt)
        cb = nxt("cb", cb16) if needs_T else None
        tb = nxt("tr", trb) if needs_T else None
        fb = nxt("fb", fbs) if ((kind in ("ka", "qa") and prompt) or (kind == "u" and not prompt)) else None
        tb2 = nxt("tr", trb) if (kind == "qa" and prompt) else None

        def st0():
            if t == 0:
                fw.dma("pool", wg.ap[:, :, 0:ncol], w_in[l, :, col0:col0 + ncol].rearrange("(k p) n -> p k n", p=128), writes=[wg.b0])
            for k in range(8):
                fw.op("pe", lambda k=k: nc.tensor.matmul(bank.ap[0:R, 0:ncol], lhsT=XT.ap[:, k, t * 128:t * 128 + R], rhs=wg.ap[:, k, 0:ncol],
                                                         start=(k == 0), stop=(k == 7)), reads=[XT.b[t], wg.b0], writes=[bank.b0], inc=(k == 7))

        def st1():
            evac(stg.ap[0:R, 0:ncol], bank.ap[0:R, 0:ncol], [bank.b0], [stg.b0])
            if kind in ("ka", "kc"):
                off = 0 if kind == "ka" else 384
                if kind == "ka":
                    rope(stg, t, R)
                dst = kr_p[l, rowsl, off:off + 384] if prompt else kr_s[l, 0:NS, off:off + 384]
                fw.dma("sp", dst, stg.ap[0:R, 0:384], reads=[stg.b0], writes=[outb])
                evac(cb.ap[0:R, 0:384], stg.ap[0:R, 0:384], [stg.b0], [cb.b0])
            elif kind in ("qa", "qc"):
                if kind == "qa":
                    rope(stg, t, R)
                evac(cb.ap[0:R, 0:384], stg.ap[0:R, 0:384], [stg.b0], [cb.b0], scale=0.125)
            elif kind in ("va", "vc"):
                off = 0 if kind == "va" else 384
                dst = vr_p[l, rowsl, off:off + 384] if prompt else vr_s[l, 0:NS, off:off + 384]
                fw.dma("sp", dst, stg.ap[0:R, 0:384], reads=[stg.b0], writes=[outb])
                if prompt:
                    evac(Vsb.ap[0:R, t, off:off + 384], stg.ap[0:R, 0:384], [stg.b0], [Vsb.b[t]])
                else:
                    evac(VN.ap[0:R, off:off + 384], stg.ap[0:R, 0:384], [stg.b0], [VN.b0])
            elif kind == "u":
                fw.op("act", lambda: nc.scalar.activation(out=gtmp.ap[0:R, :], in_=stg.ap[0:R, 256:512], func=AF.Exp, scale=-1.0),
                      reads=[stg.b0], writes=[gtmp.b0])
                fw.op("dve", lambda: nc.vector.tensor_scalar(out=gtmp.ap[0:R, :], in0=gtmp.ap[0:R, :], scalar1=1.0, scalar2=None, op0=ALU.add),
                      reads=[gtmp.b0], writes=[gtmp.b0])
                fw.op("dve", lambda: nc.vector.reciprocal(out=gtmp.ap[0:R, :], in_=gtmp.ap[0:R, :]), reads=[gtmp.b0], writes=[gtmp.b0])
                fw.op("dve", lambda: nc.vector.tensor_tensor(out=stg.ap[0:R, 0:256], in0=stg.ap[0:R, 0:256], in1=gtmp.ap[0:R, :], op=ALU.mult),
                      reads=[stg.b0, gtmp.b0], writes=[stg.b0])
                if prompt:
                    if t == NT - 1:
                        fw.dma("sp", cv_p[l, 0:30, :], stg.ap[98:128, 0:256], reads=[stg.b0], writes=[outb])
                    evac(cb.ap[0:R, 0:256], stg.ap[0:R, 0:256], [stg.b0], [cb.b0])
                else:
                    fw.dma("sp", cv_s[l, :, :].rearrange("(s j) c -> s j c", j=30)[:, 29, :], stg.ap[0:NS, 0:256], reads=[stg.b0], writes=[outb])

        def st2():
            if kind in ("ka", "kc"):
                ch0 = 3 if kind == "ka" else 9
                if prompt:
                    to_T_b(cb, tb, 384, R, QKT, ch0, t * 128, [qkb(ch0 + c, t) for c in range(3)])
                else:
                    to_T_b(cb, tb, 384, R, QS, ch0, 0, [QS.b0])
                if kind == "ka" and prompt:
                    fp32_T(stg, fb, 3)
                    fw.op("dve", lambda: nc.vector.tensor_reduce(out=ksum.ap[:, :, t], in_=fb.ap[:, 0:384].rearrange("p (c r) -> p c r", c=3),
                                                                 axis=AX.X, op=ALU.add), reads=[fb.b0], writes=[ksum.b0])
                if kind == "ka" and last_of_group:
                    nb = NT // 2
                    kv = ksum.ap[:, :, 0:2 * nb].rearrange("p c (n two) -> p c n two", two=2)
                    fw.op("dve", lambda: nc.vector.tensor_tensor(out=kmT.ap[:, :, 0:nb].unsqueeze(3), in0=kv[:, :, :, 0:1], in1=kv[:, :, :, 1:2], op=ALU.add),
                          reads=[ksum.b0], writes=[kmT.b0])
                    fw.op("dve", lambda: nc.vector.tensor_scalar(out=kmT.ap[:, :, 0:nb], in0=kmT.ap[:, :, 0:nb], scalar1=1.0 / 256, scalar2=None, op0=ALU.mult),
                          reads=[kmT.b0], writes=[kmT.b0])
            elif kind in ("qa", "qc"):
                ch0 = 0 if kind == "qa" else 6
                if prompt:
                    to_T_b(cb, tb, 384, R, QKT, ch0, t * 128, [qkb(ch0 + c, t) for c in range(3)])
                else:
                    to_T_b(cb, tb, 384, R, QS, ch0, 0, [QS.b0])
                if kind == "qa" and prompt:
                    fp32_T(stg, fb, 3)
                    evac(qaT.ap[:, :, :], fb.ap[:, 0:384].rearrange("p (c r) -> p c r", c=3), [fb.b0], [qaT.b0])
                    for par, gbk in ((0, gbank), (1, gb2)):
                        for h in range(par, 6, 2):
                            c, pb = h // 2, 64 * par
                            fw.op("pe", lambda h=h, c=c, pb=pb, gbk=gbk: nc.tensor.matmul(gbk.ap[0:128, h * 8:(h + 1) * 8], lhsT=qaT.ap[pb:pb + 64, c, 0:128],
                                                                                          rhs=kmT.ap[pb:pb + 64, c, 0:8], start=True, stop=True),
                                  reads=[qaT.b0, kmT.b0], writes=[gbk.b0], inc=(h >= 4))
                    qbk = t // 2
                    for par, gbk in ((0, gbank), (1, gb2)):
                        fw.op("dve", lambda par=par, gbk=gbk: nc.vector.tensor_tensor(out=g1.ap[:, :].rearrange("p (c two e) -> p c two e", c=3, two=2)[:, :, par, :],
                                                                                      in0=gbk.ap[:, 0:48].rearrange("p (c two e) -> p c two e", c=3, two=2)[:, :, par, :],
                                                                                      in1=gbias.ap[:, qbk * 8:(qbk + 1) * 8].unsqueeze(1).to_broadcast([128, 3, 8]), op=ALU.add),
                              reads=[gbk.b0, gbias.b0], writes=[g1.b0])
                    for h in range(6):
                        fw.op("dve", lambda h=h: nc.vector.max(out=mx.ap[:, h * 8:(h + 1) * 8], in_=g1.ap[:, h * 8:(h + 1) * 8]), reads=[g1.b0], writes=[mx.b0])
                    fw.op("dve", lambda: nc.vector.tensor_scalar(out=thr.ap[:, 0:6].unsqueeze(2), in0=mx.ap[:, :].rearrange("p (h e) -> p h e", h=6)[:, :, 3:4],
                                                                 scalar1=-1e29, scalar2=None, op0=ALU.max), reads=[mx.b0], writes=[thr.b0])
                    for h in range(6):
                        fw.op("dve", lambda h=h: nc.vector.tensor_scalar(out=negm.ap[:, h * 8:(h + 1) * 8], in0=g1.ap[:, h * 8:(h + 1) * 8],
                                                                         scalar1=thr.ap[:, h:h + 1], scalar2=NEG, op0=ALU.is_lt, op1=ALU.mult),
                              reads=[g1.b0, thr.b0], writes=[negm.b0])
                    fw.op("pe", lambda: nc.tensor.transpose(out=tb2.ap[0:48, 0:128], in_=negm.ap[0:128, 0:48], identity=ident_b.ap[:, :]),
                          reads=[negm.b0, ident_b.b0], writes=[tb2.b0])
                    evac(negT.ap[0:48, rowsl], tb2.ap[0:48, 0:128], [tb2.b0], [negT.b[t]])
            elif kind == "u":
                if prompt:
                    to_T_b(cb, tb, 256, R, uT, 0, 30 + t * 128, [uT.b[t]])
                else:
                    for c in range(2):
                        fw.op("pe", lambda c=c: nc.tensor.transpose(out=fb.ap[:, c * NS:(c + 1) * NS], in_=stg.ap[0:NS, c * 128:(c + 1) * 128],
                                                                    identity=ident_f.ap[0:NS, 0:NS]), reads=[stg.b0, ident_f.b0], writes=[fb.b0], inc=(c == 1))
                    evac(usT.ap[:, :, :, 30], fb.ap[:, 0:2 * NS].rearrange("p (c s) -> p c s", c=2), [fb.b0], [usT.b0])
        return [st0, st1, st2]

    groups = [("ka", KA, 384), ("qa", QA, 384), ("va", VA, 384), ("u", UV, 512), ("kc", KC, 384), ("qc", QC, 384), ("vc", VC, 384)]
    items = []
    for gi, (kind, col0, ncol) in enumerate(groups):
        for t in range(NT + 1):
            items.append(make_unit(gi, kind, col0, ncol, t, t == NT))
    emit_pipelined(items, 1)


def phase_conv(kb, l, pes, XT, uT, usT, sconv, conv_w, conv_b, conv_g, cv_s, outb, cst, ident_f, ident_b, ones_f, mhalf, cvals, evac):
    nc, fw = kb.nc, kb.fw
    NT, NS = kb.NT, kb.NS
    SP = NT * 128
    NB = NT // 4
    cwn = kb.sb(pes, "cwn", [33, 256])
    cwT = kb.sb(pes, "cwT", [128, 2, 33])
    fw.dma("sp", cwn.ap[0:31, :], conv_w[l, :, :], writes=[cwn.b0])
    fw.dma("sp", cwn.ap[31:32, :], conv_b[l:l + 1, :], writes=[cwn.b0])
    fw.dma("sp", cwn.ap[32:33, :], conv_g[l:l + 1, :], writes=[cwn.b0])
    FB0 = kb.ps(pes, "FB0")
    for c in range(2):
        fw.op("pe", lambda c=c: nc.tensor.transpose(out=FB0.ap[:, c * 33:(c + 1) * 33], in_=cwn.ap[0:33, c * 128:(c + 1) * 128], identity=ident_f.ap[0:33, 0:33]),
              reads=[cwn.b0, ident_f.b0], writes=[FB0.b0], inc=(c == 1))
    evac(cwT.ap[:, :, :], FB0.ap[:, 0:66].rearrange("p (c j) -> p c j", c=2), [FB0.b0], [cwT.b0])

    class _V:
        def __init__(self, ap, b0):
            self.ap, self.b0 = ap, b0
    cw = _V(cwT.ap[:, :, 0:31], cwT.b0)
    cb = _V(cwT.ap[:, :, 31], cwT.b0)
    cg = _V(cwT.ap[:, :, 32], cwT.b0)
    diag = [kb.sb(pes, "diag", [128, 31, 128], BF16) for _ in range(2)]
    for c in range(2):
        for j in range(31):
            fw.op("dve", lambda c=c, j=j: nc.vector.tensor_scalar(out=diag[c].ap[:, j, :], in0=ident_b.ap[:, :], scalar1=cw.ap[:, c, j:j + 1], scalar2=None,
                                                                  op0=ALU.mult), reads=[ident_b.b0, cw.b0], writes=[diag[c].b0])
    yb = [kb.sb(pes, "yb", [128, 512]) for _ in range(2)]
    sq = [kb.sb(pes, "sq", [128, 512]) for _ in range(2)]
    ms = kb.sb(pes, "ms", [128, 512]); rstd = kb.sb(pes, "rstd", [128, 512])
    yn = kb.sb(pes, "yn", [128, 512]); et = kb.sb(pes, "et", [128, 512])
    Y = [kb.ps(pes, "Y") for _ in range(2)]
    SQ = kb.ps(pes, "SQ"); FB = kb.ps(pes, "FB")
    st = [kb.sb(pes, "st", [120, 256]) for _ in range(2)]
    for i in range(NS // 4):
        s_ = st[i % 2]
        fw.dma("sp", s_.ap[:, :], sconv[l, i * 120:(i + 1) * 120, :], writes=[s_.b0])
        for s in range(4):
            fw.dma("sp", cv_s[l, (4 * i + s) * 30:(4 * i + s) * 30 + 29, :], s_.ap[s * 30 + 1:s * 30 + 30, :], reads=[s_.b0], writes=[outb])
        for c in range(2):
            fw.op("pe", lambda c=c: nc.tensor.transpose(out=FB.ap[:, c * 120:(c + 1) * 120], in_=s_.ap[0:120, c * 128:(c + 1) * 128], identity=ident_f.ap[0:120, 0:120]),
                  reads=[s_.b0, ident_f.b0], writes=[FB.b0], inc=(c == 1))
        evac(usT.ap[:, :, 4 * i:4 * i + 4, 0:30], FB.ap[:, 0:240].rearrange("p (c s j) -> p c s j", c=2, s=4), [FB.b0], [usT.b0])
    prod = kb.sb(pes, "prod", [128, NS, 31])
    blocks = [(tb * 512, 512, True) for tb in range(NB)] + [(SP, NS, False)]
    for (c0, N, prompt) in blocks:
        for c in range(2):
            if prompt:
                for j in range(31):
                    fw.op("pe", lambda c=c, j=j: nc.tensor.matmul(Y[c].ap[:, 0:N], lhsT=diag[c].ap[:, j, :], rhs=uT.ap[:, c, c0 + j:c0 + j + N],
                                                                  start=(j == 0), stop=(j == 30)),
                          reads=[diag[c].b0] + [uT.b[t] for t in range(max(0, c0 // 128 - 1), c0 // 128 + 4)] + [uT.b[NT]], writes=[Y[c].b0], inc=(j == 30))
                fw.op("act", lambda c=c: nc.scalar.activation(out=yb[c].ap[:, 0:N], in_=Y[c].ap[:, 0:N], func=AF.Identity, bias=cb.ap[:, c:c + 1]),
                      reads=[Y[c].b0, cb.b0], writes=[yb[c].b0])
            else:
                fw.op("dve", lambda c=c: nc.vector.tensor_tensor(out=prod.ap[:, :, :], in0=usT.ap[:, c, :, :],
                                                                 in1=cw.ap[:, c, :].unsqueeze(1).to_broadcast([128, NS, 31]), op=ALU.mult),
                      reads=[usT.b0, cw.b0], writes=[prod.b0])
                fw.op("dve", lambda c=c: nc.vector.tensor_reduce(out=yb[c].ap[:, 0:N], in_=prod.ap[:, :, :], axis=AX.X, op=ALU.add), reads=[prod.b0], writes=[yb[c].b0])
                fw.op("act", lambda c=c: nc.scalar.activation(out=yb[c].ap[:, 0:N], in_=yb[c].ap[:, 0:N], func=AF.Identity, bias=cb.ap[:, c:c + 1]),
                      reads=[yb[c].b0, cb.b0], writes=[yb[c].b0])
            fw.op("act", lambda c=c: nc.scalar.activation(out=sq[c].ap[:, 0:N], in_=yb[c].ap[:, 0:N], func=AF.Square), reads=[yb[c].b0], writes=[sq[c].b0])
        for c in range(2):
            fw.op("pe", lambda c=c: nc.tensor.matmul(SQ.ap[:, 0:N], lhsT=ones_f.ap[:, :], rhs=sq[c].ap[:, 0:N], start=(c == 0), stop=(c == 1)),
                  reads=[ones_f.b0, sq[c].b0], writes=[SQ.b0], inc=(c == 1))
        fw.op("dve", lambda: nc.vector.tensor_scalar(out=ms.ap[:, 0:N], in0=SQ.ap[:, 0:N], scalar1=1.0 / 256, scalar2=EPS, op0=ALU.mult, op1=ALU.add),
              reads=[SQ.b0], writes=[ms.b0])
        fw.op("act", lambda: nc.scalar.activation(out=rstd.ap[:, 0:N], in_=ms.ap[:, 0:N], func=AF.Ln), reads=[ms.b0], writes=[rstd.b0])
        fw.op("act", lambda: nc.scalar.activation(out=rstd.ap[:, 0:N], in_=rstd.ap[:, 0:N], func=AF.Exp, scale=-0.5), reads=[rstd.b0], writes=[rstd.b0])
        for c in range(2):
            fw.op("dve", lambda c=c: nc.vector.scalar_tensor_tensor(out=yn.ap[:, 0:N], in0=yb[c].ap[:, 0:N], scalar=cg.ap[:, c:c + 1], in1=rstd.ap[:, 0:N],
                                                                    op0=ALU.mult, op1=ALU.mult), reads=[yb[c].b0, cg.b0, rstd.b0], writes=[yn.b0])
            fw.op("act", lambda: nc.scalar.activation(out=et.ap[:, 0:N], in_=yn.ap[:, 0:N], func=AF.Exp, scale=-1.0), reads=[yn.b0], writes=[et.b0])
            fw.op("dve", lambda: nc.vector.tensor_scalar(out=et.ap[:, 0:N], in0=et.ap[:, 0:N], scalar1=1.0, scalar2=None, op0=ALU.add), reads=[et.b0], writes=[et.b0])
            fw.op("dve", lambda: nc.vector.reciprocal(out=et.ap[:, 0:N], in_=et.ap[:, 0:N]), reads=[et.b0], writes=[et.b0])
            tl = [XT.b[t] for t in range(c0 // 128, c0 // 128 + 4)] if prompt else [XT.b[NT]]
            fw.op("dve", lambda c=c: nc.vector.tensor_tensor(out=XT.ap[:, 3 + c, c0:c0 + N], in0=yn.ap[:, 0:N], in1=et.ap[:, 0:N], op=ALU.mult),
                  reads=[yn.b0, et.b0], writes=tl)


def phase_moba(kb, l, pes, XT, QKT, qkb, Vsb, negT, cst, ident_b, ones_b):
    nc, fw = kb.nc, kb.fw
    NT = kb.NT
    NB = NT // 4
    e48, cmask = cst["e48_b"], cst["cmask_b"]
    S = [kb.ps(pes, "S") for _ in range(3)]
    num = [kb.ps(pes, "num") for _ in range(2)]
    den = [kb.ps(pes, "den") for _ in range(2)]
    P = [kb.sb(pes, "P", [128, 512], BF16) for _ in range(3)]
    rd = [kb.sb(pes, "rd", [128, 512]) for _ in range(2)]
    qz = [[kb.sb(pes, "qz", [128, 512], BF16) for _ in range(2)] for _ in range(2)]
    for par in range(2):
        for k in range(2):
            fw.op("pool", lambda: nc.gpsimd.memset(qz[par][k].ap[:, :], 0.0), writes=[qz[par][k].b0])
    items = []
    i = 0
    it = 0
    for h in range(6):
        for b in range(NB):
            nk = 4 * b + 4
            for kj in range(nk):
                items.append(_moba_tile(kb, h, b, kj, nk, i, it, XT, QKT, qkb, Vsb, negT, e48, cmask, ident_b, ones_b, S, P, num, den, rd, qz))
                i += 1
            it += 1
    emit_pipelined(items, 1)


def _moba_tile(kb, h, b, kj, nk, i, it, XT, QKT, qkb, Vsb, negT, e48, cmask, ident_b, ones_b, S, P, num, den, rd, qz):
    nc, fw = kb.nc, kb.fw
    c, pb = h // 2, 64 * (h % 2)
    nm, dn = num[it % 2], den[it % 2]
    qcols = slice(b * 512, (b + 1) * 512)
    q = qz[h % 2][b % 2]
    Sb, Pb = S[i % 3], P[i % 3]
    diag = kj >= 4 * b

    def st0():
        if kj == 0:
            fw.op("pool", lambda: nc.gpsimd.tensor_copy(out=q.ap[pb:pb + 64, :], in_=QKT.ap[pb:pb + 64, 0 + c, qcols]),
                  reads=[qkb(0 + c, t) for t in range(4 * b, 4 * b + 4)], writes=[q.b0])
        fw.op("pe", lambda: nc.tensor.matmul(Sb.ap[:, :], lhsT=QKT.ap[:, 3 + c, kj * 128:(kj + 1) * 128], rhs=q.ap[:, :],
                                             start=True, stop=False), reads=[qkb(3 + c, kj), q.b0], writes=[Sb.b0], inc=False)
        r = h * 8 + kj // 2
        fw.op("pe", lambda: nc.tensor.matmul(Sb.ap[:, :], lhsT=e48.ap[:, r * 128:(r + 1) * 128], rhs=negT.ap[:, qcols], start=False, stop=not diag),
              reads=[e48.b0] + [negT.b[t] for t in range(4 * b, 4 * b + 4)], writes=[Sb.b0], inc=not diag)
        if diag:
            rr = 4 + kj - 4 * b
            fw.op("pe", lambda: nc.tensor.matmul(Sb.ap[:, :], lhsT=ident_b.ap[:, :], rhs=cmask.ap[:, rr * 512:(rr + 1) * 512], start=False, stop=True),
                  reads=[ident_b.b0, cmask.b0], writes=[Sb.b0])

    def st1():
        fw.op("act", lambda: nc.scalar.activation(out=Pb.ap[:, :], in_=Sb.ap[:, :], func=AF.Exp), reads=[Sb.b0], writes=[Pb.b0])

    def st2():
        fw.op("pe", lambda: nc.tensor.matmul(nm.ap[:, :], lhsT=Vsb.ap[:, kj, c * 128:(c + 1) * 128], rhs=Pb.ap[:, :], start=(kj == 0), stop=(kj == nk - 1)),
              reads=[Vsb.b[kj], Pb.b0], writes=[nm.b0], inc=False)
        fw.op("pe", lambda: nc.tensor.matmul(dn.ap[:, :], lhsT=ones_b.ap[:, :], rhs=Pb.ap[:, :], start=(kj == 0), stop=(kj == nk - 1)),
              reads=[ones_b.b0, Pb.b0], writes=[dn.b0], inc=True)
        if kj == nk - 1:
            rdb = rd[it % 2]
            fw.op("dve", lambda: nc.vector.reciprocal(out=rdb.ap[pb:pb + 64, :], in_=dn.ap[pb:pb + 64, :]), reads=[dn.b0], writes=[rdb.b0])
            fw.op("dve", lambda: nc.vector.tensor_tensor(out=XT.ap[pb:pb + 64, c, qcols], in0=nm.ap[pb:pb + 64, :], in1=rdb.ap[pb:pb + 64, :], op=ALU.mult),
                  reads=[nm.b0, rdb.b0], writes=[XT.b[t] for t in range(4 * b, 4 * b + 4)])
    return [st0, st1, st2]


def phase_sb(kb, l, pes, XT, QKT, qkb, Vsb, cst, ident_b, cvals):
    nc, fw = kb.nc, kb.fw
    NT = kb.NT
    NB = NT // 4
    cmask, trineg, selneg, selb, su16 = cst["cmask_b"], cst["trineg_b"], cst["selneg_b"], cst["selb_b"], cst["su16"]
    Z = [kb.ps(pes, "Z") for _ in range(3)]
    Rb = kb.ps(pes, "Rb"); Cb = kb.ps(pes, "Cb")
    oacc = [kb.ps(pes, "oacc") for _ in range(2)]
    E = [kb.sb(pes, "E", [128, 512]) for _ in range(NT)]
    SPt = [kb.sb(pes, "SPt", [128, 512], BF16) for _ in range(NT)]
    Ab = [kb.sb(pes, "Ab", [128, 512], BF16) for _ in range(3)]
    Rf = kb.sb(pes, "Rf", [16, 512]); chi = kb.sb(pes, "chi", [128, 512], BF16); clo = kb.sb(pes, "clo", [128, 512], BF16)
    fw.op("pool", lambda: nc.gpsimd.memset(chi.ap[:, :], 0.0), writes=[chi.b0])
    fw.op("pool", lambda: nc.gpsimd.memset(clo.ap[:, :], 0.0), writes=[clo.b0])
    qz = [[kb.sb(pes, "qz", [128, 512], BF16) for _ in range(2)] for _ in range(2)]
    for par in range(2):
        for k in range(2):
            fw.op("pool", lambda: nc.gpsimd.memset(qz[par][k].ap[:, :], 0.0), writes=[qz[par][k].b0])
    ctr = {"i": 0}
    hbs = [(h, b) for h in range(6) for b in range(NB)]

    def zmm(h, b, q, Zb, kj, last):
        c = h // 2
        diag = kj >= 4 * b
        fw.op("pe", lambda: nc.tensor.matmul(Zb.ap[:, :], lhsT=QKT.ap[:, 9 + c, kj * 128:(kj + 1) * 128], rhs=q.ap[:, :],
                                             start=True, stop=(last and not diag)), reads=[qkb(9 + c, kj), q.b0], writes=[Zb.b0], inc=(last and not diag))
        if diag:
            rr = kj - 4 * b
            fw.op("pe", lambda: nc.tensor.matmul(Zb.ap[:, :], lhsT=ident_b.ap[:, :], rhs=cmask.ap[:, rr * 512:(rr + 1) * 512], start=False, stop=last),
                  reads=[ident_b.b0, cmask.b0], writes=[Zb.b0], inc=last)

    def P1(n):
        h, b = hbs[n]
        c, pb = h // 2, 64 * (h % 2)
        nk = 4 * b + 4
        qcols = slice(b * 512, (b + 1) * 512)
        q = qz[h % 2][b % 2]
        fw.op("pool", lambda: nc.gpsimd.tensor_copy(out=q.ap[pb:pb + 64, :], in_=QKT.ap[pb:pb + 64, 6 + c, qcols]),
              reads=[qkb(6 + c, t) for t in range(4 * b, 4 * b + 4)], writes=[q.b0])
        for kj in range(nk):
            Zb = Z[ctr["i"] % 3]
            ctr["i"] += 1
            zmm(h, b, q, Zb, kj, True)
            fw.op("act", lambda: nc.scalar.activation(out=E[kj].ap[:, :], in_=Zb.ap[:, :], func=AF.Exp), reads=[Zb.b0], writes=[E[kj].b0])

    def P2(n):
        h, b = hbs[n]
        nk = 4 * b + 4
        for kj in range(nk):
            fw.op("act", lambda kj=kj: nc.scalar.activation(out=SPt[kj].ap[:, :], in_=E[kj].ap[:, :], func=AF.Ln, bias=cvals.ap[:, 0:1]),
                  reads=[E[kj].b0, cvals.b0], writes=[SPt[kj].b0])
        for kj in range(nk):
            fw.op("pe", lambda kj=kj: nc.tensor.matmul(Rb.ap[:, :], lhsT=selneg.ap[:, kj * 128:(kj + 1) * 128], rhs=SPt[kj].ap[:, :], start=(kj == 0), stop=(kj == nk - 1)),
                  reads=[selneg.b0, SPt[kj].b0], writes=[Rb.b0], inc=(kj == nk - 1))
        fw.op("act", lambda: nc.scalar.copy(out=Rf.ap[0:16, :], in_=Rb.ap[0:16, :]), reads=[Rb.b0], writes=[Rf.b0])
        fw.op("pe", lambda: nc.tensor.matmul(Cb.ap[0:16, :], lhsT=su16.ap[0:16, 0:16], rhs=Rf.ap[0:16, :], start=True, stop=True),
              reads=[su16.b0, Rf.b0], writes=[Cb.b0])
        fw.op("dve", lambda: nc.vector.tensor_copy(out=chi.ap[0:16, :], in_=Cb.ap[0:16, :]), reads=[Cb.b0], writes=[chi.b0])
        fw.op("dve", lambda: nc.vector.tensor_tensor(out=clo.ap[0:16, :], in0=Cb.ap[0:16, :], in1=chi.ap[0:16, :], op=ALU.subtract),
              reads=[Cb.b0, chi.b0], writes=[clo.b0])

    def P3(n):
        h, b = hbs[n]
        c, pb = h // 2, 64 * (h % 2)
        nk = 4 * b + 4
        qcols = slice(b * 512, (b + 1) * 512)
        q = qz[h % 2][b % 2]
        oa = oacc[n % 2]
        items = []
        for kj in range(nk):
            i = ctr["i"]
            ctr["i"] += 1
            items.append(_sb_tile(kb, h, b, q, kj, nk, Z[i % 3], Ab[i % 3], oa, zmm, XT, Vsb, SPt, trineg, selb, chi, clo, c, pb, qcols))
        emit_pipelined(items, 1)

    P1(0)
    for n in range(len(hbs)):
        P2(n)
        if n + 1 < len(hbs):
            P1(n + 1)
        P3(n)


def _sb_tile(kb, h, b, q, kj, nk, Zb, A, oa, zmm, XT, Vsb, SPt, trineg, selb, chi, clo, c, pb, qcols):
    nc, fw = kb.nc, kb.fw

    def st0():
        zmm(h, b, q, Zb, kj, False)
        fw.op("pe", lambda: nc.tensor.matmul(Zb.ap[:, :], lhsT=trineg.ap[:, :], rhs=SPt[kj].ap[:, :], start=False, stop=False),
              reads=[trineg.b0, SPt[kj].b0], writes=[Zb.b0], inc=False)
        fw.op("pe", lambda: nc.tensor.matmul(Zb.ap[:, :], lhsT=selb.ap[:, kj * 128:(kj + 1) * 128], rhs=chi.ap[:, :], start=False, stop=False),
              reads=[selb.b0, chi.b0], writes=[Zb.b0], inc=False)
        fw.op("pe", lambda: nc.tensor.matmul(Zb.ap[:, :], lhsT=selb.ap[:, kj * 128:(kj + 1) * 128], rhs=clo.ap[:, :], start=False, stop=True),
              reads=[selb.b0, clo.b0], writes=[Zb.b0], inc=True)

    def st1():
        fw.op("act", lambda: nc.scalar.activation(out=A.ap[:, :], in_=Zb.ap[:, :], func=AF.Exp), reads=[Zb.b0], writes=[A.b0])

    def st2():
        fw.op("pe", lambda: nc.tensor.matmul(oa.ap[:, :], lhsT=Vsb.ap[:, kj, 384 + c * 128:384 + (c + 1) * 128], rhs=A.ap[:, :],
                                             start=(kj == 0), stop=(kj == nk - 1)), reads=[Vsb.b[kj], A.b0], writes=[oa.b0], inc=True)
        if kj == nk - 1:
            fw.op("dve", lambda: nc.vector.tensor_copy(out=XT.ap[pb:pb + 64, 5 + c, qcols], in_=oa.ap[pb:pb + 64, :]), reads=[oa.b0],
                  writes=[XT.b[t] for t in range(4 * b, 4 * b + 4)])
    return [st0, st1, st2]


def phase_mix(kb, l, pes, XT, w_out, gvec, hsrc, hbuf, hb, ident_b, evac, norm_to_T, rstd_from_ssq):
    nc, fw = kb.nc, kb.fw
    NT, NS = kb.NT, kb.NS
    wo = kb.sb(pes, "wo", [128, 8, D], BF16)
    fw.dma("pool", wo.ap[:, :, :], w_out[l, :, :].rearrange("(k p) n -> p k n", p=128), writes=[wo.b0])
    gpost = kb.sb(pes, "gpost", [128, D]); gpre = kb.sb(pes, "gpre", [128, D])
    fw.dma("sp", gpost.ap[:], gvec["g_post_mix"][l, :].partition_broadcast(128), writes=[gpost.b0])
    fw.dma("sp", gpre.ap[:], gvec["g_pre_mlp"][l, :].partition_broadcast(128), writes=[gpre.b0])
    hr = [kb.sb(pes, "hr", [128, D]) for _ in range(2)]
    tmps = [(kb.sb(pes, "junk", [128, D], BF16), kb.sb(pes, "ssq", [128, 1]), kb.sb(pes, "tmp1", [128, 1]),
             kb.sb(pes, "rs", [128, 1]), kb.sb(pes, "abf", [128, D], BF16)) for _ in range(2)]
    s2 = [kb.sb(pes, "s2", [128, 2]) for _ in range(2)]
    stot = [kb.sb(pes, "stot", [128, 1]) for _ in range(2)]
    t1 = [kb.sb(pes, "t1", [128, 1]) for _ in range(2)]
    rs2 = [kb.sb(pes, "rs2", [128, 1]) for _ in range(2)]
    dlt = [kb.sb(pes, "dlt", [128, D]) for _ in range(2)]
    mixb = [[kb.ps(pes, "mix") for _ in range(2)] for _ in range(2)]
    trbs = [kb.ps(pes, "trb", [128, 1024], BF16) for _ in range(2)]
    for t in range(NT + 1):
        R = kb.rows(t)
        k2 = t % 2
        hT = hr[k2]
        fw.dma("sp", hT.ap[0:R, :], hsrc(l, t), reads=[hb[t]], writes=[hT.b0])
        junk = tmps[k2][0]
        for half in range(2):
            bank = mixb[k2][half]
            for c in range(8):
                fw.op("pe", lambda c=c: nc.tensor.matmul(bank.ap[0:R, :], lhsT=XT.ap[:, c, t * 128:t * 128 + R], rhs=wo.ap[:, c, half * 512:(half + 1) * 512],
                                                         start=(c == 0), stop=(c == 7)), reads=[XT.b[t], wo.b0], writes=[bank.b0], inc=(c == 7))
            fw.op("act", lambda: nc.scalar.activation(out=junk.ap[0:R, 0:512], in_=bank.ap[0:R, :], func=AF.Square, accum_out=s2[k2].ap[0:R, half:half + 1]),
                  reads=[bank.b0], writes=[junk.b0, s2[k2].b0])
        fw.op("dve", lambda: nc.vector.tensor_tensor(out=stot[k2].ap[0:R, :], in0=s2[k2].ap[0:R, 0:1], in1=s2[k2].ap[0:R, 1:2], op=ALU.add),
              reads=[s2[k2].b0], writes=[stot[k2].b0])
        rstd_from_ssq(stot[k2].ap[0:R, 0:1], stot[k2].b0, R, D, t1[k2], rs2[k2])
        for half in range(2):
            bank = mixb[k2][half]
            hs = slice(half * 512, (half + 1) * 512)
            fw.op("dve", lambda: nc.vector.scalar_tensor_tensor(out=dlt[k2].ap[0:R, hs], in0=bank.ap[0:R, :], scalar=rs2[k2].ap[0:R, 0:1], in1=gpost.ap[0:R, hs],
                                                                op0=ALU.mult, op1=ALU.mult), reads=[bank.b0, rs2[k2].b0, gpost.b0], writes=[dlt[k2].b0])
        fw.op("pool", lambda: nc.gpsimd.tensor_tensor(out=hT.ap[0:R, :], in0=hT.ap[0:R, :], in1=dlt[k2].ap[0:R, :], op=ALU.add),
              reads=[hT.b0, dlt[k2].b0], writes=[hT.b0])
        fw.dma("sp", hbuf[t * 128:t * 128 + R, :], hT.ap[0:R, :], reads=[hT.b0], writes=[hb[t]])
        norm_to_T(l, t, hT, gpre, XT, trbs[k2], tmps[k2])


def phase_mlp(kb, l, pes, XT, w_up, w_down, gvec, hbuf, hb, hdst, outb, final, evac, rstd_from_ssq):
    nc, fw = kb.nc, kb.fw
    NT, NS = kb.NT, kb.NS
    SP = NT * 128
    NB = NT // 4
    NG, GF = 8, 4
    facc = kb.sb(pes, "facc", [128, NT + 1, D], F32, nb=NT + 1)
    wu = [kb.sb(pes, "wu", [128, 8, 512], BF16) for _ in range(2)]
    wd = [kb.sb(pes, "wd", [128, GF, D], BF16) for _ in range(2)]
    actT = [kb.sb(pes, "actT", [128, GF, 512], BF16) for _ in range(2)]
    rt = [kb.sb(pes, "rt", [128, 512]) for _ in range(2)]
    U = [kb.ps(pes, "U") for _ in range(3)]
    Dn = [kb.ps(pes, "Dn") for _ in range(4)]
    blocks = [(tb * 512, 512, [4 * tb + i for i in range(4)]) for tb in range(NB)] + [(SP, NS, [NT])]
    ctr = {"iu": 0, "idn": 0}

    def make_blk(g, bi, c0, N, tiles):
        wug, wdg = wu[g % 2], wd[g % 2]
        aT = actT[(g * len(blocks) + bi) % 2]

        def st0():
            if bi == 0:
                fw.dma("pool", wug.ap[:, :, :], w_up[l, :, g * 512:(g + 1) * 512].rearrange("(k p) n -> p k n", p=128), writes=[wug.b0])
                fw.dma("pool", wdg.ap[:, :, :], w_down[l, g * 512:(g + 1) * 512, :].rearrange("(f p) n -> p f n", p=128), writes=[wdg.b0])
            for fc in range(GF):
                Ub = U[ctr["iu"] % 3]
                rtb = rt[ctr["iu"] % 2]
                ctr["iu"] += 1
                for k in range(8):
                    fw.op("pe", lambda k=k: nc.tensor.matmul(Ub.ap[:, 0:N], lhsT=wug.ap[:, k, fc * 128:(fc + 1) * 128], rhs=XT.ap[:, k, c0:c0 + N],
                                                             start=(k == 0), stop=(k == 7)), reads=[wug.b0] + [XT.b[t] for t in tiles], writes=[Ub.b0], inc=(k == 7))
                fw.op("act", lambda: nc.scalar.activation(out=rtb.ap[:, 0:N], in_=Ub.ap[:, 0:N], func=AF.Relu), reads=[Ub.b0], writes=[rtb.b0])
                fw.op("pool", lambda: nc.gpsimd.tensor_tensor(out=aT.ap[:, fc, 0:N], in0=rtb.ap[:, 0:N], in1=rtb.ap[:, 0:N], op=ALU.mult),
                      reads=[rtb.b0], writes=[aT.b0])

        def st1():
            for ti, t in enumerate(tiles):
                R = kb.rows(t)
                for half in range(2):
                    Db = Dn[ctr["idn"] % 4]
                    ctr["idn"] += 1
                    hs = slice(half * 512, (half + 1) * 512)
                    for fc in range(GF):
                        fw.op("pe", lambda fc=fc: nc.tensor.matmul(Db.ap[0:R, :], lhsT=aT.ap[:, fc, ti * 128:ti * 128 + R], rhs=wdg.ap[:, fc, hs],
                                                                   start=(fc == 0), stop=(fc == GF - 1)), reads=[aT.b0, wdg.b0], writes=[Db.b0], inc=(fc == GF - 1))
                    if g == 0:
                        fw.op("act", lambda: nc.scalar.copy(out=facc.ap[0:R, t, hs], in_=Db.ap[0:R, :]), reads=[Db.b0], writes=[facc.b[t]])
                    else:
                        fw.op("dve", lambda: nc.vector.tensor_tensor(out=facc.ap[0:R, t, hs], in0=Db.ap[0:R, :], in1=facc.ap[0:R, t, hs], op=ALU.add),
                              reads=[Db.b0, facc.b[t]], writes=[facc.b[t]])
        return [st0, st1]

    items = []
    for g in range(NG):
        for bi, (c0, N, tiles) in enumerate(blocks):
            items.append(make_blk(g, bi, c0, N, tiles))
    emit_pipelined(items, 1)
    gpost = kb.sb(pes, "gpm", [128, D])
    fw.dma("sp", gpost.ap[:], gvec["g_post_mlp"][l, :].partition_broadcast(128), writes=[gpost.b0])
    hr = [kb.sb(pes, "hr", [128, D]) for _ in range(2)]
    junk = [kb.sb(pes, "junk", [128, D], BF16) for _ in range(2)]
    ssq = [kb.sb(pes, "ssq", [128, 1]) for _ in range(2)]
    t1 = [kb.sb(pes, "t1", [128, 1]) for _ in range(2)]
    rs = [kb.sb(pes, "rs", [128, 1]) for _ in range(2)]
    for t in range(NT + 1):
        R = kb.rows(t)
        k2 = t % 2
        hT = hr[k2]
        fw.dma("sp", hT.ap[0:R, :], hbuf[t * 128:t * 128 + R, :], reads=[hb[t]], writes=[hT.b0])
        fw.op("act", lambda: nc.scalar.activation(out=junk[k2].ap[0:R, :], in_=facc.ap[0:R, t, :], func=AF.Square, accum_out=ssq[k2].ap[0:R, 0:1]),
              reads=[facc.b[t]], writes=[junk[k2].b0, ssq[k2].b0])
        rstd_from_ssq(ssq[k2].ap[0:R, 0:1], ssq[k2].b0, R, D, t1[k2], rs[k2])
        fw.op("dve", lambda: nc.vector.scalar_tensor_tensor(out=facc.ap[0:R, t, :], in0=facc.ap[0:R, t, :], scalar=rs[k2].ap[0:R, 0:1], in1=gpost.ap[0:R, :],
                                                            op0=ALU.mult, op1=ALU.mult), reads=[facc.b[t], rs[k2].b0, gpost.b0], writes=[facc.b[t]])
        fw.op("pool", lambda: nc.gpsimd.tensor_tensor(out=hT.ap[0:R, :], in0=hT.ap[0:R, :], in1=facc.ap[0:R, t, :], op=ALU.add),
              reads=[hT.b0, facc.b[t]], writes=[hT.b0])
        fw.dma("sp", hdst(l, t, final), hT.ap[0:R, :], reads=[hT.b0], writes=[outb if final else hb[t]])


def phase_sample(kb, l, pes, XT, QS, VN, ck, cv, idx, cst, ident_b, ones_b, ones_f, cvals, evac):
    nc, fw = kb.nc, kb.fw
    NT, NS, NPG = kb.NT, kb.NS, kb.NPG
    SP = NT * 128
    NBK = NPG // 2
    H6 = 6 * NPG
    trineg, hsel = cst["trineg_b"], cst["hsel"]
    Vs = [kb.sb(pes, "Vs", [128, NPG, 768], BF16, nb=NPG) for _ in range(3)]
    kT = [kb.sb(pes, "kT", [128, 6, 128], BF16) for _ in range(3)]
    trb = [kb.ps(pes, "trb", [128, 1024], BF16) for _ in range(2)]
    Zs = [kb.ps(pes, "Zs") for _ in range(2)]
    misc = [kb.ps(pes, "misc") for _ in range(2)]
    Oall = kb.ps(pes, "Oall")
    Qblk = kb.sb(pes, "Qblk", [128, NS, 6, 2], BF16)
    fw.op("dve", lambda: nc.vector.memset(Qblk.ap[:], 0.0), writes=[Qblk.b0])
    for e in range(2):
        pb = 64 * e
        for (d0, s0) in ((0, 0), (3, 6)):
            fw.op("dve", lambda: nc.vector.tensor_copy(out=Qblk.ap[pb:pb + 64, :, d0:d0 + 3, e].rearrange("p s c -> p c s"), in_=QS.ap[pb:pb + 64, s0:s0 + 3, :]),
                  reads=[QS.b0], writes=[Qblk.b0])
    vnT = kb.sb(pes, "vnT", [128, 3, NS]); prod = kb.sb(pes, "prod", [128, 3, NS]); pself = kb.sb(pes, "pself", [128, 3, NS])
    for c in range(3):
        fw.op("pe", lambda c=c: nc.tensor.transpose(out=trb[0].ap[:, c * NS:(c + 1) * NS], in_=VN.ap[0:NS, c * 128:(c + 1) * 128], identity=ident_b.ap[0:NS, 0:NS]),
              reads=[VN.b0, ident_b.b0], writes=[trb[0].b0], inc=(c == 2))
    evac(vnT.ap[:, :, :], trb[0].ap[:, 0:3 * NS].rearrange("p (c s) -> p c s", c=3), [trb[0].b0], [vnT.b0])
    fw.op("dve", lambda: nc.vector.tensor_tensor(out=prod.ap[:, :, :], in0=QS.ap[:, 0:3, :], in1=QS.ap[:, 3:6, :], op=ALU.mult), reads=[QS.b0], writes=[prod.b0])
    fw.op("pe", lambda: nc.tensor.matmul(misc[0].ap[:, 0:3 * NS], lhsT=hsel.ap[:, :], rhs=prod.ap[:, :, :].rearrange("p c s -> p (c s)"), start=True, stop=True),
          reads=[hsel.b0, prod.b0], writes=[misc[0].b0])
    fw.op("act", lambda: nc.scalar.activation(out=pself.ap[:, :, :].rearrange("p c s -> p (c s)"), in_=misc[0].ap[:, 0:3 * NS], func=AF.Exp),
          reads=[misc[0].b0], writes=[pself.b0])
    segg = kb.sb(pes, "segg", [128, H6])
    fw.op("dve", lambda: nc.vector.memset(segg.ap[:], 1.0), writes=[segg.b0])
    fw.op("dve", lambda: nc.vector.memset(segg.ap[:, :].rearrange("p (h j) -> p h j", j=NPG)[:, :, 0:1], 0.0), writes=[segg.b0])

    def tmp(name, dt=F32, n=H6):
        return [kb.sb(pes, name, [128, n], dt) for _ in range(2)]
    Ee, SPf, SPb, Csb, Incl, Wa, Wb, Asb = tmp("Ee"), tmp("SPf"), tmp("SPb", BF16), tmp("Csb"), tmp("Incl"), tmp("Wa"), tmp("Wb"), tmp("Asb", BF16)
    Zc, G8, Mx, Ng, Wm, Pm = tmp("Zc"), tmp("G8", F32, 48), tmp("Mx", F32, 48), tmp("Ng", F32, 48), tmp("Wm"), tmp("Pm", BF16)
    tristr = cst["tristr_b"]
    kpg = [kb.sb(pes, "kpg2", [128, 2, 768], BF16) for _ in range(4)]
    Cs2, Inc2, Car = tmp("Cs2", F32, 6 * NBK), tmp("Inc2", F32, 6 * NBK), tmp("Car", F32, 6 * NBK)
    segg8 = kb.sb(pes, "segg8", [128, 6 * NBK])
    fw.op("dve", lambda: nc.vector.memset(segg8.ap[:], 1.0), writes=[segg8.b0])
    fw.op("dve", lambda: nc.vector.memset(segg8.ap[:, :].rearrange("p (h n) -> p h n", n=NBK)[:, :, 0:1], 0.0), writes=[segg8.b0])
    ckv = ck.rearrange("(r two) c -> r (two c)", two=2)
    cvv = cv.rearrange("(r two) c -> r (two c)", two=2)
    eo = l * kb.NPOOL * 128 * 768
    ctr = {"gi": 0, "ti": 0}
    items = []
    for s_ in range(NS):
        items.append(_sample_item(locals(), s_))
    emit_pipelined(items, 1)
    num = kb.sb(pes, "fnum", [128, 3, NS]); den = kb.sb(pes, "fden", [128, 3, NS])
    for c3 in range(3):
        for e in range(2):
            pb = 64 * e
            sbv = Oall.ap[pb:pb + 64, 0:6 * NS].rearrange("p (s x) -> p s x", x=6)[:, :, 2 * c3 + e]
            fw.op("dve", lambda: nc.vector.tensor_copy(out=XT.ap[pb:pb + 64, 5 + c3, SP:SP + NS], in_=sbv), reads=[Oall.b0], writes=[XT.b[NT]])
            mov = Oall.ap[pb:pb + 64, 6 * NS:12 * NS].rearrange("p (s x) -> p s x", x=6)[:, :, 2 * c3 + e]
            dnv = Oall.ap[pb:pb + 64, 12 * NS:18 * NS].rearrange("p (s x) -> p s x", x=6)[:, :, 2 * c3 + e]
            fw.op("dve", lambda: nc.vector.tensor_tensor(out=num.ap[pb:pb + 64, c3, :], in0=pself.ap[pb:pb + 64, c3, :], in1=vnT.ap[pb:pb + 64, c3, :], op=ALU.mult),
                  reads=[pself.b0, vnT.b0], writes=[num.b0])
            fw.op("dve", lambda: nc.vector.tensor_tensor(out=num.ap[pb:pb + 64, c3, :], in0=mov, in1=num.ap[pb:pb + 64, c3, :], op=ALU.add),
                  reads=[Oall.b0, num.b0], writes=[num.b0])
            fw.op("dve", lambda: nc.vector.tensor_tensor(out=den.ap[pb:pb + 64, c3, :], in0=dnv, in1=pself.ap[pb:pb + 64, c3, :], op=ALU.add),
                  reads=[Oall.b0, pself.b0], writes=[den.b0])
    fw.op("dve", lambda: nc.vector.reciprocal(out=den.ap[:, :, :], in_=den.ap[:, :, :]), reads=[den.b0], writes=[den.b0])
    fw.op("dve", lambda: nc.vector.tensor_tensor(out=XT.ap[:, 0:3, SP:SP + NS], in0=num.ap[:, :, :], in1=den.ap[:, :, :], op=ALU.mult), reads=[num.b0, den.b0], writes=[XT.b[NT]])


def _sample_item(env, s):
    g = env
    kb, fw, nc = g["kb"], g["fw"], g["nc"]
    names = ["NS", "NPG", "NBK", "H6", "Vs", "Zs", "misc", "Oall", "Qblk", "kpg", "kT", "trb", "idx", "ckv", "cvv", "eo", "evac", "ident_b", "ones_b", "ones_f", "cvals",
             "trineg", "tristr", "segg8", "Ee", "SPf", "SPb", "Csb", "Cs2", "Inc2", "Car", "Wa", "Asb", "Zc", "G8", "Mx", "Ng", "Wm", "Pm", "ctr"]
    NS, NPG, NBK, H6, Vs, Zs, misc, Oall, Qblk, kpg, kT, trb, idx, ckv, cvv, eo, evac, ident_b, ones_b, ones_f, cvals, \
        trineg, tristr, segg8, Ee, SPf, SPb, Csb, Cs2, Inc2, Car, Wa, Asb, Zc, G8, Mx, Ng, Wm, Pm, ctr = [g[n] for n in names]
    k2 = s % 2
    V = Vs[s % 3]
    Zb = Zs[k2]

    def st0():
        Zb = Zs[k2]
        for n in range(NBK):
            kp = kpg[ctr["gi"] % 4]
            ctr["gi"] += 1
            col = s * NBK + n
            fw.gather(kp.ap[:, :, :].rearrange("p a c -> p (a c)"), ckv, idx.ap[:, col:col + 1], reads=[idx.b0], writes=[kp.b0], element_offset=eo)
            fw.gather(V.ap[:, 2 * n:2 * n + 2, :].rearrange("p a c -> p (a c)"), cvv, idx.ap[:, col:col + 1], reads=[idx.b0], writes=[V.b[2 * n], V.b[2 * n + 1]], element_offset=eo)
            for e in range(2):
                j = 2 * n + e
                tb = trb[ctr["ti"] % 2]
                kt = kT[ctr["ti"] % 3]
                ctr["ti"] += 1
                for cc in range(6):
                    fw.op("pe", lambda cc=cc: nc.tensor.transpose(out=tb.ap[:, cc * 128:(cc + 1) * 128], in_=kp.ap[:, e, cc * 128:(cc + 1) * 128], identity=ident_b.ap[:, :]),
                          reads=[kp.b0, ident_b.b0], writes=[tb.b0], inc=(cc == 5))
                evac(kt.ap[:, :, :], tb.ap[:, 0:768].rearrange("p (c r) -> p c r", c=6), [tb.b0], [kt.b0])
                for cc in range(6):
                    if cc < 3:
                        o = Zb.ap[:, 0:H6].rearrange("p (h j) -> p h j", j=NPG)[:, 2 * cc:2 * cc + 2, j]
                    else:
                        o = Zb.ap[:, H6:2 * H6].rearrange("p (h j) -> p h j", j=NPG)[:, 2 * (cc - 3):2 * (cc - 3) + 2, 2 * (NBK - 1 - n) + e]
                    fw.op("pe", lambda cc=cc, o=o: nc.tensor.matmul(o, lhsT=kt.ap[:, cc, :], rhs=Qblk.ap[:, s, cc, :], start=True, stop=True),
                          reads=[kt.b0, Qblk.b0], writes=[Zb.b0], inc=(cc == 5))

    def st1():
        m0, m1 = misc[0], misc[1]
        fw.op("act", lambda: nc.scalar.activation(out=Ee[k2].ap[:, :], in_=Zb.ap[:, H6:2 * H6], func=AF.Exp), reads=[Zb.b0], writes=[Ee[k2].b0])
        fw.op("act", lambda: nc.scalar.activation(out=SPf[k2].ap[:, :], in_=Ee[k2].ap[:, :], func=AF.Ln, bias=cvals.ap[:, 0:1]), reads=[Ee[k2].b0, cvals.b0], writes=[SPf[k2].b0])
        fw.op("dve", lambda: nc.vector.tensor_copy(out=SPb[k2].ap[:, :], in_=SPf[k2].ap[:, :]), reads=[SPf[k2].b0], writes=[SPb[k2].b0])
        spv = SPb[k2].ap[:, :].rearrange("p (q two) -> p q two", two=2)
        m0v = m0.ap[:, 0:H6].rearrange("p (q two) -> p q two", two=2)
        fw.op("pe", lambda: nc.tensor.matmul(m0v[:, :, 0], lhsT=trineg.ap[:, :], rhs=spv[:, :, 0], start=True, stop=False), reads=[trineg.b0, SPb[k2].b0], writes=[m0.b0], inc=False)
        fw.op("pe", lambda: nc.tensor.matmul(m0v[:, :, 0], lhsT=trineg.ap[:, :], rhs=spv[:, :, 1], start=False, stop=True), reads=[trineg.b0, SPb[k2].b0], writes=[m0.b0], inc=False)
        fw.op("pe", lambda: nc.tensor.matmul(m0v[:, :, 1], lhsT=tristr.ap[:, :], rhs=spv[:, :, 0], start=True, stop=False), reads=[tristr.b0, SPb[k2].b0], writes=[m0.b0], inc=False)
        fw.op("pe", lambda: nc.tensor.matmul(m0v[:, :, 1], lhsT=trineg.ap[:, :], rhs=spv[:, :, 1], start=False, stop=True), reads=[trineg.b0, SPb[k2].b0], writes=[m0.b0], inc=True)
        fw.op("pe", lambda: nc.tensor.matmul(m1.ap[:, 0:H6], lhsT=ones_f.ap[:, :], rhs=SPf[k2].ap[:, :], start=True, stop=True), reads=[ones_f.b0, SPf[k2].b0], writes=[m1.b0])
        fw.op("act", lambda: nc.scalar.copy(out=Csb[k2].ap[:, :], in_=m1.ap[:, 0:H6]), reads=[m1.b0], writes=[Csb[k2].b0])
        csv = Csb[k2].ap[:, :].rearrange("p (q two) -> p q two", two=2)
        fw.op("dve", lambda: nc.vector.tensor_tensor(out=Cs2[k2].ap[:, :].unsqueeze(2), in0=csv[:, :, 0:1], in1=csv[:, :, 1:2], op=ALU.add), reads=[Csb[k2].b0], writes=[Cs2[k2].b0])
        fw.op("dve", lambda: nc.vector.tensor_tensor_scan(out=Inc2[k2].ap[:, :], data0=segg8.ap[:, :], data1=Cs2[k2].ap[:, :], initial=0.0, op0=ALU.mult, op1=ALU.add),
              reads=[segg8.b0, Cs2[k2].b0], writes=[Inc2[k2].b0])
        fw.op("dve", lambda: nc.vector.tensor_tensor(out=Car[k2].ap[:, :], in0=Inc2[k2].ap[:, :], in1=Cs2[k2].ap[:, :], op=ALU.subtract), reads=[Inc2[k2].b0, Cs2[k2].b0], writes=[Car[k2].b0])
        fw.op("dve", lambda: nc.vector.tensor_tensor(out=Wa[k2].ap[:, :].rearrange("p (q two) -> p q two", two=2), in0=Zb.ap[:, H6:2 * H6].rearrange("p (q two) -> p q two", two=2),
                                                     in1=Car[k2].ap[:, :].unsqueeze(2).to_broadcast([128, 6 * NBK, 2]), op=ALU.subtract), reads=[Zb.b0, Car[k2].b0], writes=[Wa[k2].b0])
        fw.op("dve", lambda: nc.vector.tensor_tensor(out=Wa[k2].ap[:, :], in0=m0.ap[:, 0:H6], in1=Wa[k2].ap[:, :], op=ALU.add), reads=[m0.b0, Wa[k2].b0], writes=[Wa[k2].b0])
        fw.op("act", lambda: nc.scalar.activation(out=Asb[k2].ap[:, :], in_=Wa[k2].ap[:, :], func=AF.Exp), reads=[Wa[k2].b0], writes=[Asb[k2].b0])
        fw.op("act", lambda: nc.scalar.copy(out=Zc[k2].ap[:, :], in_=Zb.ap[:, 0:H6]), reads=[Zb.b0], writes=[Zc[k2].b0])
        fw.op("pe", lambda: nc.tensor.matmul(m1.ap[:, 256:256 + H6], lhsT=ones_f.ap[:, :], rhs=Zc[k2].ap[:, :], start=True, stop=True), reads=[ones_f.b0, Zc[k2].b0], writes=[m1.b0])
        fw.op("dve", lambda: nc.vector.memset(G8[k2].ap[:, :], -1e30), writes=[G8[k2].b0])
        gv = m1.ap[:, 256:256 + H6].rearrange("p (h n two) -> p h n two", h=6, two=2)
        fw.op("dve", lambda: nc.vector.tensor_copy(out=G8[k2].ap[:, :].rearrange("p (h n) -> p h n", h=6)[:, :, 0:NBK].unsqueeze(3), in_=gv[:, :, :, 0:1]), reads=[m1.b0], writes=[G8[k2].b0])
        fw.op("dve", lambda: nc.vector.tensor_tensor(out=G8[k2].ap[:, :].rearrange("p (h n) -> p h n", h=6)[:, :, 0:NBK].unsqueeze(3),
                                                     in0=gv[:, :, :, 1:2], in1=G8[k2].ap[:, :].rearrange("p (h n) -> p h n", h=6)[:, :, 0:NBK].unsqueeze(3), op=ALU.add),
              reads=[m1.b0, G8[k2].b0], writes=[G8[k2].b0])
        for h in range(6):
            fw.op("dve", lambda h=h: nc.vector.max(out=Mx[k2].ap[:, h * 8:(h + 1) * 8], in_=G8[k2].ap[:, h * 8:(h + 1) * 8]), reads=[G8[k2].b0], writes=[Mx[k2].b0])
        for h in range(6):
            fw.op("dve", lambda h=h: nc.vector.tensor_scalar(out=Ng[k2].ap[:, h * 8:(h + 1) * 8], in0=G8[k2].ap[:, h * 8:(h + 1) * 8], scalar1=Mx[k2].ap[:, h * 8 + 2:h * 8 + 3],
                                                             scalar2=NEG, op0=ALU.is_lt, op1=ALU.mult), reads=[G8[k2].b0, Mx[k2].b0], writes=[Ng[k2].b0])
        fw.op("dve", lambda: nc.vector.tensor_tensor(out=Wm[k2].ap[:, :].rearrange("p (h n two) -> p h n two", h=6, two=2),
                                                     in0=Zb.ap[:, 0:H6].rearrange("p (h n two) -> p h n two", h=6, two=2),
                                                     in1=Ng[k2].ap[:, :].rearrange("p (h n) -> p h n", h=6)[:, :, 0:NBK].unsqueeze(3).to_broadcast([128, 6, NBK, 2]), op=ALU.add),
              reads=[Zb.b0, Ng[k2].b0], writes=[Wm[k2].b0])
        fw.op("act", lambda: nc.scalar.activation(out=Pm[k2].ap[:, :], in_=Wm[k2].ap[:, :], func=AF.Exp), reads=[Wm[k2].b0], writes=[Pm[k2].b0])

    def st2():
        Av = Asb[k2].ap[:, :].rearrange("p (h j) -> p h j", j=NPG)
        for c3 in range(3):
            for j in range(NPG):
                n_, e_ = j // 2, j % 2
                fw.op("pe", lambda: nc.tensor.matmul(Oall.ap[:, s * 6 + 2 * c3:s * 6 + 2 * c3 + 2], lhsT=V.ap[:, j, 384 + c3 * 128:384 + (c3 + 1) * 128],
                                                     rhs=Av[:, 2 * c3:2 * c3 + 2, 2 * (NBK - 1 - n_) + e_], start=(j == 0), stop=(j == NPG - 1)),
                      reads=[V.b[j], Asb[k2].b0], writes=[Oall.b0], inc=(j == NPG - 1))
        Pv = Pm[k2].ap[:, :].rearrange("p (h j) -> p h j", j=NPG)
        for c3 in range(3):
            for j in range(NPG):
                fw.op("pe", lambda: nc.tensor.matmul(Oall.ap[:, 6 * NS + s * 6 + 2 * c3:6 * NS + s * 6 + 2 * c3 + 2], lhsT=V.ap[:, j, c3 * 128:(c3 + 1) * 128],
                                                     rhs=Pv[:, 2 * c3:2 * c3 + 2, j], start=(j == 0), stop=(j == NPG - 1)),
                      reads=[V.b[j], Pm[k2].b0], writes=[Oall.b0], inc=(j == NPG - 1))
        for j in range(NPG):
            fw.op("pe", lambda: nc.tensor.matmul(Oall.ap[:, 12 * NS + s * 6:12 * NS + s * 6 + 6], lhsT=ones_b.ap[:, :], rhs=Pv[:, 0:6, j], start=(j == 0), stop=(j == NPG - 1)),
                  reads=[ones_b.b0, Pm[k2].b0], writes=[Oall.b0], inc=(j == NPG - 1))

    return [st0, st1, st2]


def core_in_map(inp, c, NT, NS, NPG, NPOOL, consts, depth=2):
    f32 = lambda a: np.ascontiguousarray(np.asarray(a, dtype=np.float32))
    m = {}
    m["x_p"] = f32(inp["x_prompt"][c]).reshape(NT * 128, D)
    m["x_s"] = f32(inp["x_sample"][c * NS:(c + 1) * NS]).reshape(NS, D)
    m["ck"] = inp["_ck"]
    m["cv"] = inp["_cv"]
    m["sconv"] = f32(inp["state_conv"][:, c * NS:(c + 1) * NS]).reshape(depth, NS * 30, 256)
    m["ptab"] = np.ascontiguousarray(np.asarray(inp["page_table"][c * NS:(c + 1) * NS], dtype=np.int32)).reshape(-1)
    for n in ("w_in", "w_out", "w_up", "w_down", "conv_w", "conv_b", "conv_g", "g_pre_mix", "g_post_mix", "g_pre_mlp", "g_post_mlp"):
        m[n] = inp["_" + n]
    for n, v in consts.items():
        m["c_" + n] = v
    return m


_NC_CACHE = {}


def run(inp, NT, NS, NPG, NPOOL, n_cores, do_sample=True, depth=2):
    key = (NT, NS, NPG, NPOOL, do_sample)
    if key not in _NC_CACHE:
        _NC_CACHE[key] = build(NT, NS, NPG, NPOOL, depth, do_sample)
    nc = _NC_CACHE[key]
    consts = make_consts(NT, NPG * 128)
    inp = dict(inp)
    f32 = lambda a: np.ascontiguousarray(np.asarray(a, dtype=np.float32))
    inp["_ck"] = f32(inp["cache_k"]).reshape(depth * NPOOL * 128, 768)
    inp["_cv"] = f32(inp["cache_v"]).reshape(depth * NPOOL * 128, 768)
    for n in ("w_in", "w_out", "w_up", "w_down", "conv_w", "conv_b", "conv_g", "g_pre_mix", "g_post_mix", "g_pre_mlp", "g_post_mlp"):
        inp["_" + n] = f32(inp[n])
    in_maps = [core_in_map(inp, c, NT, NS, NPG, NPOOL, consts, depth) for c in range(n_cores)]
    res = run_bass_kernel_spmd(nc, in_maps, core_ids=list(range(n_cores)))
    rs = res.results
    SP = NT * 128
    y_p = np.stack([r["y_p"] for r in rs]).reshape(n_cores, SP, D)
    y_s = np.concatenate([r["y_s"] for r in rs]).reshape(n_cores * NS, 1, D)
    kr_p = np.stack([r["kr_p"] for r in rs], axis=1).reshape(depth, n_cores, SP, 12, 64)
    vr_p = np.stack([r["vr_p"] for r in rs], axis=1).reshape(depth, n_cores, SP, 12, 64)
    cv_p = np.stack([r["cv_p"] for r in rs], axis=1).reshape(depth, n_cores, 30, 256)
    kr_s = np.concatenate([r["kr_s"] for r in rs], axis=1).reshape(depth, n_cores * NS, 1, 12, 64)
    vr_s = np.concatenate([r["vr_s"] for r in rs], axis=1).reshape(depth, n_cores * NS, 1, 12, 64)
    cv_s = np.concatenate([r["cv_s"].reshape(depth, NS, 30, 256) for r in rs], axis=1)
    return (y_p, y_s, kr_p, vr_p, cv_p, kr_s, vr_s, cv_s)


def kernel(x_prompt, x_sample, cache_k, cache_v, state_conv, page_table, w_in, w_out, conv_w, conv_b, conv_g,
           w_up, w_down, g_pre_mix, g_post_mix, g_pre_mlp, g_post_mlp):
    inp = dict(x_prompt=x_prompt, x_sample=x_sample, cache_k=cache_k, cache_v=cache_v, state_conv=state_conv,
               page_table=page_table, w_in=w_in, w_out=w_out, conv_w=conv_w, conv_b=conv_b, conv_g=conv_g,
               w_up=w_up, w_down=w_down, g_pre_mix=g_pre_mix, g_post_mix=g_post_mix, g_pre_mlp=g_pre_mlp, g_post_mlp=g_post_mlp)
    n_cores = 8
    NT = x_prompt.shape[1] // 128
    NS = x_sample.shape[0] // n_cores
    NPG = page_table.shape[1]
    NPOOL = cache_k.shape[1]
    return run(inp, NT, NS, NPG, NPOOL, n_cores)
```

```python
import contextlib
import os
import numpy as np
import concourse.bass as bass
import concourse.mybir as mybir
from concourse.bass_utils import run_bass_kernel_spmd

F32 = mybir.dt.float32
BF16 = mybir.dt.bfloat16
I32 = mybir.dt.int32
AF = mybir.ActivationFunctionType
ALU = mybir.AluOpType
AX = mybir.AxisListType

D = 1024
DIN = 2816
DFF = 4096
HD = 64
NEG = -30000.0
EPS = 1e-6
QA, KA, VA, UV, UG, QC, KC, VC = 0, 384, 768, 1152, 1408, 1664, 2048, 2432


class Lane:
    __slots__ = ("sem", "val", "inc")

    def __init__(self, sem, inc):
        self.sem, self.val, self.inc = sem, 0, inc


class Buf:
    __slots__ = ("w", "r", "excl")

    def __init__(self, excl=False):
        self.w = None
        self.r = {}
        self.excl = excl


class FW:
    NDMA = 8

    def __init__(self, nc, es):
        self.nc = nc
        self.engs = {"pe": nc.tensor, "act": nc.scalar, "dve": nc.vector, "pool": nc.gpsimd, "sp": nc.sync}
        self.lanes = {k: Lane(es.enter_context(nc.semaphore("s_" + k)), 1) for k in ("pe", "act", "dve", "pool")}
        self.dlanes = {q: [Lane(es.enter_context(nc.semaphore(f"d_{q}{i}")), 16) for i in range(self.NDMA)]
                       for q in ("sp", "pool", "act")}
        self.rr = {"sp": 0, "pool": 0, "act": 0}
        self.known = {k: {} for k in self.engs}
        self.n_inst = 0

    def _wait(self, issuer, lane, val):
        k = self.known[issuer]
        if k.get(lane, 0) < val:
            self.engs[issuer].wait_ge(lane.sem, val)
            k[lane] = val

    def _deps(self, issuer, reads, writes, own=None):
        for b in reads:
            if b.w is not None and b.w[0] is not own:
                self._wait(issuer, b.w[0], b.w[1])
        for b in writes:
            if b.w is not None and b.w[0] is not own:
                self._wait(issuer, b.w[0], b.w[1])
            for lane, val in b.r.items():
                if lane is not own:
                    self._wait(issuer, lane, val)

    @staticmethod
    def _commit(lane, val, reads, writes):
        for b in writes:
            b.w = (lane, val)
            b.r = {}
        for b in reads:
            if b.r.get(lane, 0) < val:
                b.r[lane] = val

    def op(self, eng, fn, reads=(), writes=(), inc=True):
        lane = self.lanes[eng]
        if any(b.excl for b in reads):
            writes = list(writes) + [b for b in reads if b.excl]
            reads = [b for b in reads if not b.excl]
        self._deps(eng, reads, writes, own=lane if eng == "pe" else None)
        inst = fn()
        self.n_inst += 1
        if inc:
            lane.val += 1
            inst.then_inc(lane.sem, 1)
            self._commit(lane, lane.val, reads, writes)
        else:
            self._commit(lane, lane.val + 1, reads, writes)
        return inst

    def dma(self, q, out, in_, reads=(), writes=(), **kw):
        lanes = self.dlanes[q]
        lane = lanes[self.rr[q] % self.NDMA]
        self.rr[q] += 1
        self._deps(q, reads, writes)
        self._wait(q, lane, lane.val)
        lane.val += 16
        self.engs[q].dma_start(out=out, in_=in_, **kw).then_inc(lane.sem, 16)
        self.n_inst += 1
        self._commit(lane, lane.val, reads, writes)

    def gather(self, out, in_, idx_ap, reads=(), writes=(), element_offset=0):
        q = "pool"
        lanes = self.dlanes[q]
        lane = lanes[self.rr[q] % self.NDMA]
        self.rr[q] += 1
        self._deps(q, reads, writes)
        self._wait(q, lane, lane.val)
        lane.val += 16
        self.nc.gpsimd.indirect_dma_start(out=out, out_offset=None, in_=in_,
                                          in_offset=bass.IndirectOffsetOnAxis(ap=idx_ap, axis=0),
                                          element_offset=element_offset).then_inc(lane.sem, 16)
        self.n_inst += 1
        self._commit(lane, lane.val, reads, writes)

    def barrier(self):
        all_l = list(self.lanes.values()) + [l for ls in self.dlanes.values() for l in ls]
        for issuer in ("pe", "act", "dve", "pool", "sp"):
            for l in all_l:
                if l.val > 0 and not (issuer in self.lanes and self.lanes[issuer] is l):
                    self._wait(issuer, l, l.val)

    def finish(self):
        for ls in self.dlanes.values():
            for l in ls:
                if l.val > 0:
                    self._wait("sp", l, l.val)
        for l in self.lanes.values():
            if l.val > 0:
                self._wait("sp", l, l.val)


def emit_pipelined(items, skew=1):
    n = len(items)
    K = max(len(it) for it in items) if items else 0
    for slot in range(n + (K - 1) * skew):
        for k in range(K):
            i = slot - k * skew
            if 0 <= i < n and k < len(items[i]):
                items[i][k]()


class T:
    def __init__(self, ap, nb=1, excl=False):
        self.ap = ap
        self.b = [Buf(excl) for _ in range(nb)]
        self.b0 = self.b[0]

    def __getitem__(self, k):
        return self.ap[k]


def make_consts(NT, past_len):
    c = {}
    c["ident"] = np.eye(128, dtype=np.float32)
    j = np.arange(128)
    c["trineg"] = -(j[:, None] >= j[None, :]).astype(np.float32)
    seln = np.zeros((128, 16, 128), np.float32)
    for kj in range(16):
        seln[:, kj, kj] = -1.0
    c["selneg"] = seln.reshape(128, 2048)
    c["su16"] = (np.arange(16)[:, None] > np.arange(16)[None, :]).astype(np.float32)
    selb = np.zeros((128, 16, 128), np.float32)
    for kj in range(16):
        selb[kj, kj, :] = 1.0
    c["selb"] = selb.reshape(128, 2048)
    e48 = np.zeros((128, 48, 128), np.float32)
    for r in range(48):
        e48[r, r, :] = 1.0
    c["e48"] = e48.reshape(128, 48 * 128)
    p = np.arange(128)[:, None]
    f = np.arange(512)[None, :]
    cm = np.zeros((128, 8, 512), np.float32)
    for r in range(4):
        cm[:, r, :] = np.where((128 * r + p) < f, 0.0, NEG)
        cm[:, 4 + r, :] = np.where((128 * r + p) <= f, 0.0, NEG)
    c["cmask"] = cm.reshape(128, 8 * 512)
    half = 8
    inv_freq = (np.float32(500000.0) ** (-np.arange(half, dtype=np.float32) / np.float32(half))).astype(np.float32)
    pos = np.zeros((128, NT + 1), np.float32)
    for t in range(NT):
        pos[:, t] = t * 128 + np.arange(128)
    pos[:, NT] = past_len
    ang = pos[:, :, None].astype(np.float32) * inv_freq[None, None, :]
    cs, sn = np.cos(ang).astype(np.float32), np.sin(ang).astype(np.float32)
    c["ropec"] = np.concatenate([cs, cs], axis=2).reshape(128, (NT + 1) * 16)
    c["ropes"] = np.concatenate([sn, sn], axis=2).reshape(128, (NT + 1) * 16)
    gb = np.zeros((128, 8, 8), np.float32)
    for qb in range(8):
        for n in range(8):
            gb[:, qb, n] = 0.0 if n < qb else (1e30 if n == qb else -1e30)
    c["gbias"] = gb.reshape(128, 64)
    c["piota"] = np.arange(128, dtype=np.float32).reshape(128, 1)
    hs = np.zeros((128, 128), np.float32)
    hs[0:64, 0:64] = 1.0
    hs[64:128, 64:128] = 1.0
    c["hsel"] = hs
    c["pcol"] = np.stack([(np.arange(128) >= 64).astype(np.float32), (np.arange(128) % 64).astype(np.float32)], axis=1)
    c["tristr"] = -(j[:, None] > j[None, :]).astype(np.float32)
    return c


CONST_SHAPES = lambda NT: {"ident": [128, 128], "trineg": [128, 128], "selneg": [128, 2048], "su16": [16, 16],
                           "selb": [128, 2048], "e48": [128, 6144], "cmask": [128, 4096],
                           "ropec": [128, (NT + 1) * 16], "ropes": [128, (NT + 1) * 16], "gbias": [128, 64],
                           "piota": [128, 1], "hsel": [128, 128], "pcol": [128, 2], "tristr": [128, 128]}


class KB:
    def __init__(self, NT, NS, NPG, NPOOL, depth=2):
        self.NT, self.NS, self.NPG, self.NPOOL, self.depth = NT, NS, NPG, NPOOL, depth
        self.NTOK = NT * 128 + NS
        self.nc = bass.Bass("TRN2", target_bir_lowering=False)
        self.es = contextlib.ExitStack()
        self.fw = FW(self.nc, self.es)
        self.cnt = 0

    def dram(self, name, shape, dt=F32, kind="ExternalInput"):
        return self.nc.dram_tensor(name, list(shape), dt, kind=kind).ap()

    def sb(self, es, name, shape, dt=F32, nb=1):
        self.cnt += 1
        return T(es.enter_context(self.nc.sbuf_tensor(f"{name}_{self.cnt}", list(shape), dt)), nb)

    def ps(self, es, name, shape=(128, 512), dt=F32):
        self.cnt += 1
        return T(es.enter_context(self.nc.psum_tensor(f"{name}_{self.cnt}", list(shape), dt)), 1, excl=True)

    def rows(self, t):
        return 128 if t < self.NT else self.NS


def build(NT=16, NS=16, NPG=16, NPOOL=2560, depth=2, do_sample=True):
    import os
    UPTO = int(os.environ.get('UPTO', '99'))
    kb = KB(NT, NS, NPG, NPOOL, depth)
    nc, fw, es = kb.nc, kb.fw, kb.es
    NTOK = kb.NTOK
    NB = NT // 4
    SP = NT * 128
    x_p = kb.dram("x_p", [SP, D]); x_s = kb.dram("x_s", [NS, D])
    ck = kb.dram("ck", [depth * NPOOL * 128, 768]); cv = kb.dram("cv", [depth * NPOOL * 128, 768])
    sconv = kb.dram("sconv", [depth, NS * 30, 256]); ptab = kb.dram("ptab", [NS * NPG], I32)
    w_in = kb.dram("w_in", [depth, D, DIN]); w_out = kb.dram("w_out", [depth, D, D])
    w_up = kb.dram("w_up", [depth, D, DFF]); w_down = kb.dram("w_down", [depth, DFF, D])
    conv_w = kb.dram("conv_w", [depth, 31, 256]); conv_b = kb.dram("conv_b", [depth, 256]); conv_g = kb.dram("conv_g", [depth, 256])
    gvec = {n: kb.dram(n, [depth, D]) for n in ("g_pre_mix", "g_post_mix", "g_pre_mlp", "g_post_mlp")}
    cdram = {n: kb.dram("c_" + n, shp) for n, shp in CONST_SHAPES(NT).items()}
    y_p = kb.dram("y_p", [SP, D], kind="ExternalOutput"); y_s = kb.dram("y_s", [NS, D], kind="ExternalOutput")
    kr_p = kb.dram("kr_p", [depth, SP, 768], kind="ExternalOutput"); vr_p = kb.dram("vr_p", [depth, SP, 768], kind="ExternalOutput")
    cv_p = kb.dram("cv_p", [depth, 30, 256], kind="ExternalOutput")
    kr_s = kb.dram("kr_s", [depth, NS, 768], kind="ExternalOutput"); vr_s = kb.dram("vr_s", [depth, NS, 768], kind="ExternalOutput")
    cv_s = kb.dram("cv_s", [depth, NS * 30, 256], kind="ExternalOutput")
    hbuf = kb.dram("hbuf", [SP + 128, D], kind="Internal")
    hb = [Buf() for _ in range(NT + 1)]
    outb = Buf()

    def hsrc(l, t):
        R = kb.rows(t)
        if l == 0:
            return x_p[t * 128:(t + 1) * 128, :] if t < NT else x_s[0:NS, :]
        return hbuf[t * 128:t * 128 + R, :]

    def hdst(l, t, final):
        R = kb.rows(t)
        if final:
            return y_p[t * 128:(t + 1) * 128, :] if t < NT else y_s[0:NS, :]
        return hbuf[t * 128:t * 128 + R, :]

    cst = {}
    for n, dt in (("ident", F32), ("su16", F32), ("ropec", F32), ("ropes", F32), ("gbias", F32), ("hsel", F32)):
        cst[n] = kb.sb(es, "c_" + n, CONST_SHAPES(NT)[n], dt)
        fw.dma("sp", cst[n].ap[:], cdram[n][:, :], writes=[cst[n].b0])
    for n in ("ident", "trineg", "tristr", "selneg", "selb", "e48", "cmask"):
        cst[n + "_b"] = kb.sb(es, "cb_" + n, CONST_SHAPES(NT)[n], BF16)
        fw.dma("pool", cst[n + "_b"].ap[:], cdram[n][:, :], writes=[cst[n + "_b"].b0])
    ident_f, ident_b = cst["ident"], cst["ident_b"]
    ones_b = kb.sb(es, "ones_b", [128, 128], BF16)
    fw.op("dve", lambda: nc.vector.memset(ones_b.ap[:], 1.0), writes=[ones_b.b0])
    ones_f = kb.sb(es, "ones_f", [128, 128], F32)
    fw.op("dve", lambda: nc.vector.memset(ones_f.ap[:], 1.0), writes=[ones_f.b0])
    cvals = kb.sb(es, "cvals", [128, 4], F32)
    fw.op("dve", lambda: nc.vector.memset(cvals.ap[:, 0:1], 1.0), writes=[cvals.b0])
    fw.op("dve", lambda: nc.vector.memset(cvals.ap[:, 1:2], -0.5), writes=[cvals.b0])
    fw.op("dve", lambda: nc.vector.memset(cvals.ap[:, 2:3], EPS), writes=[cvals.b0])
    mhalf = kb.sb(es, "mhalf", [128, 512], F32)
    fw.op("pool", lambda: nc.gpsimd.memset(mhalf.ap[:], -0.5), writes=[mhalf.b0])
    NBK = NPG // 2
    idx_i = kb.sb(es, "idx_i", [128, NS * NPG], I32)
    idx_f = kb.sb(es, "idx_f", [128, NS * NPG], F32)
    idx_t = kb.sb(es, "idx_t", [128, NS * NBK], F32)
    idx = kb.sb(es, "idx2", [128, NS * NBK], I32)
    pcol = kb.sb(es, "pcol", [128, 2], F32)
    fw.dma("sp", pcol.ap[:], cdram["pcol"][:, :], writes=[pcol.b0])
    fw.dma("sp", idx_i.ap[:], ptab.partition_broadcast(128), writes=[idx_i.b0])
    fw.op("dve", lambda: nc.vector.tensor_copy(out=idx_f.ap[:], in_=idx_i.ap[:]), reads=[idx_i.b0], writes=[idx_f.b0])
    pv_ = idx_f.ap[:, :].rearrange("p (q two) -> p q two", two=2)
    fw.op("dve", lambda: nc.vector.tensor_tensor(out=idx_t.ap[:, :].unsqueeze(2), in0=pv_[:, :, 1:2], in1=pv_[:, :, 0:1], op=ALU.subtract), reads=[idx_f.b0], writes=[idx_t.b0])
    fw.op("dve", lambda: nc.vector.scalar_tensor_tensor(out=idx_t.ap[:, :].unsqueeze(2), in0=idx_t.ap[:, :].unsqueeze(2), scalar=pcol.ap[:, 0:1], in1=pv_[:, :, 0:1],
                                                        op0=ALU.mult, op1=ALU.add), reads=[idx_t.b0, pcol.b0, idx_f.b0], writes=[idx_t.b0])
    fw.op("dve", lambda: nc.vector.tensor_scalar(out=idx_t.ap[:], in0=idx_t.ap[:], scalar1=64.0, scalar2=pcol.ap[:, 1:2], op0=ALU.mult, op1=ALU.add),
          reads=[idx_t.b0, pcol.b0], writes=[idx_t.b0])
    fw.op("dve", lambda: nc.vector.tensor_copy(out=idx.ap[:], in_=idx_t.ap[:]), reads=[idx_t.b0], writes=[idx.b0])
    ev = [0]

    def evac(out, in_, reads, writes, scale=None):
        ev[0] += 1
        if ev[0] % 2 == 0:
            if scale is None:
                fw.op("act", lambda: nc.scalar.copy(out=out, in_=in_), reads=reads, writes=writes)
            else:
                fw.op("act", lambda: nc.scalar.activation(out=out, in_=in_, func=AF.Identity, scale=scale), reads=reads, writes=writes)
        else:
            if scale is None:
                fw.op("dve", lambda: nc.vector.tensor_copy(out=out, in_=in_), reads=reads, writes=writes)
            else:
                fw.op("dve", lambda: nc.vector.tensor_scalar(out=out, in0=in_, scalar1=scale, scalar2=None, op0=ALU.mult), reads=reads, writes=writes)

    def rstd_from_ssq(ssq_ap, ssq_b, R, n, tmp, rs):
        fw.op("dve", lambda: nc.vector.tensor_scalar(out=tmp.ap[0:R, 0:1], in0=ssq_ap, scalar1=1.0 / n, scalar2=EPS, op0=ALU.mult, op1=ALU.add),
              reads=[ssq_b], writes=[tmp.b0])
        fw.op("pool", lambda: nc.gpsimd.tensor_tensor(out=rs.ap[0:R, 0:1], in0=tmp.ap[0:R, 0:1], in1=cvals.ap[0:R, 1:2], op=ALU.pow),
              reads=[tmp.b0, cvals.b0], writes=[rs.b0])

    def norm_to_T(l, t, hT, gbc, XT, trb, pes_tmps):
        R = kb.rows(t)
        junk, ssq, tmp1, rs, abf = pes_tmps
        fw.op("act", lambda: nc.scalar.activation(out=junk.ap[0:R, :], in_=hT.ap[0:R, :], func=AF.Square, accum_out=ssq.ap[0:R, 0:1]),
              reads=[hT.b0], writes=[junk.b0, ssq.b0])
        rstd_from_ssq(ssq.ap[0:R, 0:1], ssq.b0, R, D, tmp1, rs)
        fw.op("dve", lambda: nc.vector.scalar_tensor_tensor(out=abf.ap[0:R, :], in0=hT.ap[0:R, :], scalar=rs.ap[0:R, 0:1], in1=gbc.ap[0:R, :],
                                                            op0=ALU.mult, op1=ALU.mult), reads=[hT.b0, rs.b0, gbc.b0], writes=[abf.b0])
        for c in range(8):
            fw.op("pe", lambda c=c: nc.tensor.transpose(out=trb.ap[:, c * 128:c * 128 + R], in_=abf.ap[0:R, c * 128:(c + 1) * 128],
                                                        identity=ident_b.ap[0:R, 0:R]), reads=[abf.b0, ident_b.b0], writes=[trb.b0], inc=(c == 7))
        evac(XT.ap[:, :, t * 128:t * 128 + R], trb.ap[:, :].rearrange("p (c r) -> p c r", c=8)[:, :, 0:R], [trb.b0], [XT.b[t]])

    for l in range(depth):
        final = (l == depth - 1)
        with contextlib.ExitStack() as les:
            XT = kb.sb(les, "XT", [128, 8, NTOK], BF16, nb=NT + 1)
            QS = kb.sb(les, "QS", [128, 12, NS], BF16)
            VN = kb.sb(les, "VN", [NS, 768], BF16)
            with contextlib.ExitStack() as aes:
                QKT = kb.sb(aes, "QKT", [128, 12, SP], BF16, nb=12 * (NT + 1))
                Vsb = kb.sb(aes, "Vsb", [128, NT, 768], BF16, nb=NT + 1)

                def qkb(ch, t):
                    return QKT.b[ch * (NT + 1) + t]
                with contextlib.ExitStack() as nes:
                    negT = kb.sb(nes, "negT", [128, SP], BF16, nb=NT)
                    fw.op("pool", lambda: nc.gpsimd.memset(negT.ap[:, :], 0.0), writes=list(negT.b))
                    with contextlib.ExitStack() as ues:
                        uT = kb.sb(ues, "uT", [128, 2, 30 + SP], BF16, nb=NT + 1)
                        usT = kb.sb(ues, "usT", [128, 2, NS, 31], F32)
                        fw.op("pool", lambda: nc.gpsimd.memset(uT.ap[:, :, 0:30], 0.0), writes=[uT.b[NT]])
                        with contextlib.ExitStack() as pes:
                            gbc = kb.sb(pes, "gbc", [128, D])
                            fw.dma("sp", gbc.ap[:], gvec["g_pre_mix"][l, :].partition_broadcast(128), writes=[gbc.b0])
                            hr = [kb.sb(pes, "hr", [128, D]) for _ in range(2)]
                            tmps = [(kb.sb(pes, "junk", [128, D], BF16), kb.sb(pes, "ssq", [128, 1]), kb.sb(pes, "tmp1", [128, 1]),
                                     kb.sb(pes, "rs", [128, 1]), kb.sb(pes, "abf", [128, D], BF16)) for _ in range(2)]
                            trbs = [kb.ps(pes, "trb", [128, 1024], BF16) for _ in range(2)]
                            for t in range(NT + 1):
                                R = kb.rows(t)
                                hT = hr[t % 2]
                                fw.dma("sp", hT.ap[0:R, :], hsrc(l, t), reads=[hb[t]], writes=[hT.b0])
                                norm_to_T(l, t, hT, gbc, XT, trbs[t % 2], tmps[t % 2])
                            fw.barrier()
                        with contextlib.ExitStack() as pes:
                          if UPTO >= 1:
                            phase_a1(kb, l, pes, XT, QKT, qkb, Vsb, negT, uT, usT, w_in, cst, ident_f, ident_b, evac,
                                     kr_p, vr_p, kr_s, vr_s, cv_p, cv_s, outb, QS, VN)
                            fw.barrier()
                        with contextlib.ExitStack() as pes:
                          if UPTO >= 2:
                            phase_conv(kb, l, pes, XT, uT, usT, sconv, conv_w, conv_b, conv_g, cv_s, outb, cst, ident_f, ident_b,
                                       ones_f, mhalf, cvals, evac)
                            fw.barrier()
                    with contextlib.ExitStack() as pes:
                      if UPTO >= 3:
                        phase_moba(kb, l, pes, XT, QKT, qkb, Vsb, negT, cst, ident_b, ones_b)
                        fw.barrier()
                with contextlib.ExitStack() as pes:
                  if UPTO >= 4:
                    phase_sb(kb, l, pes, XT, QKT, qkb, Vsb, cst, ident_b, cvals)
                    fw.barrier()
            with contextlib.ExitStack() as pes:
                if do_sample and UPTO >= 5:
                    phase_sample(kb, l, pes, XT, QS, VN, ck, cv, idx, cst, ident_b, ones_b, ones_f, cvals, evac)
                else:
                    for ch in (0, 1, 2, 5, 6, 7):
                        fw.op("pool", lambda ch=ch: nc.gpsimd.memset(XT.ap[:, ch, SP:SP + NS], 0.0), writes=[XT.b[NT]])
                fw.barrier()
            with contextlib.ExitStack() as pes:
              if UPTO >= 5:
                phase_mix(kb, l, pes, XT, w_out, gvec, hsrc, hbuf, hb, ident_b, evac, norm_to_T, rstd_from_ssq)
                fw.barrier()
            with contextlib.ExitStack() as pes:
              if UPTO >= 6:
                phase_mlp(kb, l, pes, XT, w_up, w_down, gvec, hbuf, hb, hdst, outb, final, evac, rstd_from_ssq)
                fw.barrier()
    fw.finish()
    return nc


def phase_a1(kb, l, pes, XT, QKT, qkb, Vsb, negT, uT, usT, w_in, cst, ident_f, ident_b, evac,
             kr_p, vr_p, kr_s, vr_s, cv_p, cv_s, outb, QS, VN):
    nc, fw = kb.nc, kb.fw
    NT, NS = kb.NT, kb.NS
    SP = NT * 128
    wr = [kb.sb(pes, "wg", [128, 8, 512], BF16) for _ in range(2)]
    stgs = [kb.sb(pes, "stg", [128, 512]) for _ in range(3)]
    cb16 = [kb.sb(pes, "cb16", [128, 384], BF16) for _ in range(3)]
    rtmp = kb.sb(pes, "rtmp", [128, 2, 96])
    qaT = kb.sb(pes, "qaT", [128, 3, 128])
    ksum = kb.sb(pes, "ksum", [128, 3, max(NT, 2)])
    kmT = kb.sb(pes, "kmT", [128, 3, 8])
    g1 = kb.sb(pes, "g1", [128, 48]); mx = kb.sb(pes, "mx", [128, 48]); thr = kb.sb(pes, "thr", [128, 6])
    negm = kb.sb(pes, "negm", [128, 48], BF16)
    gtmp = kb.sb(pes, "gtmp", [128, 256])
    mm = [kb.ps(pes, "mm") for _ in range(2)]
    fbs = [kb.ps(pes, "fb") for _ in range(2)]
    trb = [kb.ps(pes, "trb", [128, 1024], BF16) for _ in range(2)]
    gbank = kb.ps(pes, "gbank"); gb2 = kb.ps(pes, "gb2")
    ropec, ropes, gbias = cst["ropec"], cst["ropes"], cst["gbias"]
    cnt = {"mm": 0, "stg": 0, "cb": 0, "tr": 0, "fb": 0}

    def nxt(k, lst):
        cnt[k] += 1
        return lst[cnt[k] % len(lst)]

    fw.op("dve", lambda: nc.vector.memset(kmT.ap[:], 0.0), writes=[kmT.b0])

    def rope(stg, t, R):
        X = stg.ap[0:R, 0:384].rearrange("p (h d) -> p h d", h=6)
        A = rtmp.ap[0:R, 0, :].rearrange("p (h d) -> p h d", h=6)
        B = rtmp.ap[0:R, 1, :].rearrange("p (h d) -> p h d", h=6)
        cc = ropec.ap[0:R, t * 16:(t + 1) * 16].unsqueeze(1).to_broadcast([R, 6, 16])
        ss = ropes.ap[0:R, t * 16:(t + 1) * 16].unsqueeze(1).to_broadcast([R, 6, 16])
        fw.op("dve", lambda: nc.vector.tensor_tensor(out=A, in0=X[:, :, 0:16], in1=cc, op=ALU.mult), reads=[stg.b0, ropec.b0], writes=[rtmp.b0])
        fw.op("dve", lambda: nc.vector.tensor_tensor(out=B, in0=X[:, :, 0:16], in1=ss, op=ALU.mult), reads=[stg.b0, ropes.b0], writes=[rtmp.b0])
        fw.op("dve", lambda: nc.vector.tensor_tensor(out=X[:, :, 0:8], in0=A[:, :, 0:8], in1=B[:, :, 8:16], op=ALU.subtract), reads=[rtmp.b0], writes=[stg.b0])
        fw.op("dve", lambda: nc.vector.tensor_tensor(out=X[:, :, 8:16], in0=A[:, :, 8:16], in1=B[:, :, 0:8], op=ALU.add), reads=[rtmp.b0], writes=[stg.b0])

    def to_T_b(cb, tb, ncol, R, dst, dch0, dcol0, dbufs):
        nch = ncol // 128
        for c in range(nch):
            fw.op("pe", lambda c=c: nc.tensor.transpose(out=tb.ap[:, c * 128:c * 128 + R], in_=cb.ap[0:R, c * 128:(c + 1) * 128],
                                                        identity=ident_b.ap[0:R, 0:R]), reads=[cb.b0, ident_b.b0], writes=[tb.b0], inc=(c == nch - 1))
        evac(dst.ap[:, dch0:dch0 + nch, dcol0:dcol0 + R], tb.ap[:, 0:nch * 128].rearrange("p (c r) -> p c r", c=nch)[:, :, 0:R], [tb.b0], dbufs)

    def fp32_T(stg, fb, n):
        for c in range(n):
            fw.op("pe", lambda c=c: nc.tensor.transpose(out=fb.ap[:, c * 128:(c + 1) * 128], in_=stg.ap[0:128, c * 128:(c + 1) * 128],
                                                        identity=ident_f.ap[:, :]), reads=[stg.b0, ident_f.b0], writes=[fb.b0], inc=(c == n - 1))

    def make_unit(gi, kind, col0, ncol, t, last_of_group):
        R = kb.rows(t)
        prompt = t < NT
        wg = wr[gi % 2]
        bank = nxt("mm", mm)
        stg = nxt("stg", stgs)
        rowsl = slice(t * 128, (t + 1) * 128)
        needs_T = kind in ("ka", "kc", "qa", "qc") or (kind == "u" and prompt)
        cb = nxt("cb", cb16) if needs_T else None
        tb = nxt("tr", trb) if needs_T else None
        fb = nxt("fb", fbs) if ((kind in ("ka", "qa") and prompt) or (kind == "u" and not prompt)) else None
        tb2 = nxt("tr", trb) if (kind == "qa" and prompt) else None

        def st0():
            if t == 0:
                for g2 in ([0, 1] if gi == 0 else [gi + 1]):
                    if g2 < len(groups):
                        _, c2, n2 = groups[g2]
                        w2 = wr[g2 % 2]
                        fw.dma("pool", w2.ap[:, :, 0:n2], w_in[l, :, c2:c2 + n2].rearrange("(k p) n -> p k n", p=128), writes=[w2.b0])
            for k in range(8):
                fw.op("pe", lambda k=k: nc.tensor.matmul(bank.ap[0:R, 0:ncol], lhsT=XT.ap[:, k, t * 128:t * 128 + R], rhs=wg.ap[:, k, 0:ncol],
                                                         start=(k == 0), stop=(k == 7)), reads=[XT.b[t], wg.b0], writes=[bank.b0], inc=(k == 7))

        def st1():
            evac(stg.ap[0:R, 0:ncol], bank.ap[0:R, 0:ncol], [bank.b0], [stg.b0])
            if kind in ("ka", "kc"):
                off = 0 if kind == "ka" else 384
                if kind == "ka":
                    rope(stg, t, R)
                dst = kr_p[l, rowsl, off:off + 384] if prompt else kr_s[l, 0:NS, off:off + 384]
                fw.dma("sp", dst, stg.ap[0:R, 0:384], reads=[stg.b0], writes=[outb])
                evac(cb.ap[0:R, 0:384], stg.ap[0:R, 0:384], [stg.b0], [cb.b0])
            elif kind in ("qa", "qc"):
                if kind == "qa":
                    rope(stg, t, R)
                evac(cb.ap[0:R, 0:384], stg.ap[0:R, 0:384], [stg.b0], [cb.b0], scale=0.125)
            elif kind in ("va", "vc"):
                off = 0 if kind == "va" else 384
                dst = vr_p[l, rowsl, off:off + 384] if prompt else vr_s[l, 0:NS, off:off + 384]
                fw.dma("sp", dst, stg.ap[0:R, 0:384], reads=[stg.b0], writes=[outb])
                if prompt:
                    evac(Vsb.ap[0:R, t, off:off + 384], stg.ap[0:R, 0:384], [stg.b0], [Vsb.b[t]])
                else:
                    evac(VN.ap[0:R, off:off + 384], stg.ap[0:R, 0:384], [stg.b0], [VN.b0])
            elif kind == "u":
                fw.op("act", lambda: nc.scalar.activation(out=gtmp.ap[0:R, :], in_=stg.ap[0:R, 256:512], func=AF.Exp, scale=-1.0),
                      reads=[stg.b0], writes=[gtmp.b0])
                fw.op("dve", lambda: nc.vector.tensor_scalar(out=gtmp.ap[0:R, :], in0=gtmp.ap[0:R, :], scalar1=1.0, scalar2=None, op0=ALU.add),
                      reads=[gtmp.b0], writes=[gtmp.b0])
                fw.op("dve", lambda: nc.vector.reciprocal(out=gtmp.ap[0:R, :], in_=gtmp.ap[0:R, :]), reads=[gtmp.b0], writes=[gtmp.b0])
                fw.op("dve", lambda: nc.vector.tensor_tensor(out=stg.ap[0:R, 0:256], in0=stg.ap[0:R, 0:256], in1=gtmp.ap[0:R, :], op=ALU.mult),
                      reads=[stg.b0, gtmp.b0], writes=[stg.b0])
                if prompt:
                    if t == NT - 1:
                        fw.dma("sp", cv_p[l, 0:30, :], stg.ap[98:128, 0:256], reads=[stg.b0], writes=[outb])
                    evac(cb.ap[0:R, 0:256], stg.ap[0:R, 0:256], [stg.b0], [cb.b0])
                else:
                    fw.dma("sp", cv_s[l, :, :].rearrange("(s j) c -> s j c", j=30)[:, 29, :], stg.ap[0:NS, 0:256], reads=[stg.b0], writes=[outb])

        def st2():
            if kind in ("ka", "kc"):
                ch0 = 3 if kind == "ka" else 9
                if prompt:
                    to_T_b(cb, tb, 384, R, QKT, ch0, t * 128, [qkb(ch0 + c, t) for c in range(3)])
                else:
                    to_T_b(cb, tb, 384, R, QS, ch0, 0, [QS.b0])
                if kind == "ka" and prompt:
                    fp32_T(stg, fb, 3)
                    fw.op("dve", lambda: nc.vector.tensor_reduce(out=ksum.ap[:, :, t], in_=fb.ap[:, 0:384].rearrange("p (c r) -> p c r", c=3),
                                                                 axis=AX.X, op=ALU.add), reads=[fb.b0], writes=[ksum.b0])
                if kind == "ka" and last_of_group:
                    nb = NT // 2
                    kv = ksum.ap[:, :, 0:2 * nb].rearrange("p c (n two) -> p c n two", two=2)
                    fw.op("dve", lambda: nc.vector.tensor_tensor(out=kmT.ap[:, :, 0:nb].unsqueeze(3), in0=kv[:, :, :, 0:1], in1=kv[:, :, :, 1:2], op=ALU.add),
                          reads=[ksum.b0], writes=[kmT.b0])
                    fw.op("dve", lambda: nc.vector.tensor_scalar(out=kmT.ap[:, :, 0:nb], in0=kmT.ap[:, :, 0:nb], scalar1=1.0 / 256, scalar2=None, op0=ALU.mult),
                          reads=[kmT.b0], writes=[kmT.b0])
            elif kind in ("qa", "qc"):
                ch0 = 0 if kind == "qa" else 6
                if prompt:
                    to_T_b(cb, tb, 384, R, QKT, ch0, t * 128, [qkb(ch0 + c, t) for c in range(3)])
                else:
                    to_T_b(cb, tb, 384, R, QS, ch0, 0, [QS.b0])
                if kind == "qa" and prompt:
                    fp32_T(stg, fb, 3)
                    evac(qaT.ap[:, :, :], fb.ap[:, 0:384].rearrange("p (c r) -> p c r", c=3), [fb.b0], [qaT.b0])
                    for par, gbk in ((0, gbank), (1, gb2)):
                        for h in range(par, 6, 2):
                            c, pb = h // 2, 64 * par
                            fw.op("pe", lambda h=h, c=c, pb=pb, gbk=gbk: nc.tensor.matmul(gbk.ap[0:128, h * 8:(h + 1) * 8], lhsT=qaT.ap[pb:pb + 64, c, 0:128],
                                                                                          rhs=kmT.ap[pb:pb + 64, c, 0:8], start=True, stop=True),
                                  reads=[qaT.b0, kmT.b0], writes=[gbk.b0], inc=(h >= 4))
                    qbk = t // 2
                    for par, gbk in ((0, gbank), (1, gb2)):
                        fw.op("dve", lambda par=par, gbk=gbk: nc.vector.tensor_tensor(out=g1.ap[:, :].rearrange("p (c two e) -> p c two e", c=3, two=2)[:, :, par, :],
                                                                                      in0=gbk.ap[:, 0:48].rearrange("p (c two e) -> p c two e", c=3, two=2)[:, :, par, :],
                                                                                      in1=gbias.ap[:, qbk * 8:(qbk + 1) * 8].unsqueeze(1).to_broadcast([128, 3, 8]), op=ALU.add),
                              reads=[gbk.b0, gbias.b0], writes=[g1.b0])
                    for h in range(6):
                        fw.op("dve", lambda h=h: nc.vector.max(out=mx.ap[:, h * 8:(h + 1) * 8], in_=g1.ap[:, h * 8:(h + 1) * 8]), reads=[g1.b0], writes=[mx.b0])
                    fw.op("dve", lambda: nc.vector.tensor_scalar(out=thr.ap[:, 0:6].unsqueeze(2), in0=mx.ap[:, :].rearrange("p (h e) -> p h e", h=6)[:, :, 3:4],
                                                                 scalar1=-1e29, scalar2=None, op0=ALU.max), reads=[mx.b0], writes=[thr.b0])
                    for h in range(6):
                        fw.op("dve", lambda h=h: nc.vector.tensor_scalar(out=negm.ap[:, h * 8:(h + 1) * 8], in0=g1.ap[:, h * 8:(h + 1) * 8],
                                                                         scalar1=thr.ap[:, h:h + 1], scalar2=NEG, op0=ALU.is_lt, op1=ALU.mult),
                              reads=[g1.b0, thr.b0], writes=[negm.b0])
                    fw.op("pe", lambda: nc.tensor.transpose(out=tb2.ap[0:48, 0:128], in_=negm.ap[0:128, 0:48], identity=ident_b.ap[:, :]),
                          reads=[negm.b0, ident_b.b0], writes=[tb2.b0])
                    evac(negT.ap[0:48, rowsl], tb2.ap[0:48, 0:128], [tb2.b0], [negT.b[t]])
            elif kind == "u":
                if prompt:
                    to_T_b(cb, tb, 256, R, uT, 0, 30 + t * 128, [uT.b[t]])
                else:
                    for c in range(2):
                        fw.op("pe", lambda c=c: nc.tensor.transpose(out=fb.ap[:, c * NS:(c + 1) * NS], in_=stg.ap[0:NS, c * 128:(c + 1) * 128],
                                                                    identity=ident_f.ap[0:NS, 0:NS]), reads=[stg.b0, ident_f.b0], writes=[fb.b0], inc=(c == 1))
                    evac(usT.ap[:, :, :, 30], fb.ap[:, 0:2 * NS].rearrange("p (c s) -> p c s", c=2), [fb.b0], [usT.b0])
        return [st0, st1, st2]

    groups = [("ka", KA, 384), ("qa", QA, 384), ("va", VA, 384), ("u", UV, 512), ("kc", KC, 384), ("qc", QC, 384), ("vc", VC, 384)]
    items = []
    for gi, (kind, col0, ncol) in enumerate(groups):
        for t in range(NT + 1):
            items.append(make_unit(gi, kind, col0, ncol, t, t == NT))
    emit_pipelined(items, 1)


def phase_conv(kb, l, pes, XT, uT, usT, sconv, conv_w, conv_b, conv_g, cv_s, outb, cst, ident_f, ident_b, ones_f, mhalf, cvals, evac):
    nc, fw = kb.nc, kb.fw
    NT, NS = kb.NT, kb.NS
    SP = NT * 128
    NB = NT // 4
    cwn = kb.sb(pes, "cwn", [33, 256])
    cwT = kb.sb(pes, "cwT", [128, 2, 33])
    fw.dma("sp", cwn.ap[0:31, :], conv_w[l, :, :], writes=[cwn.b0])
    fw.dma("sp", cwn.ap[31:32, :], conv_b[l:l + 1, :], writes=[cwn.b0])
    fw.dma("sp", cwn.ap[32:33, :], conv_g[l:l + 1, :], writes=[cwn.b0])
    FB0 = kb.ps(pes, "FB0")
    for c in range(2):
        fw.op("pe", lambda c=c: nc.tensor.transpose(out=FB0.ap[:, c * 33:(c + 1) * 33], in_=cwn.ap[0:33, c * 128:(c + 1) * 128], identity=ident_f.ap[0:33, 0:33]),
              reads=[cwn.b0, ident_f.b0], writes=[FB0.b0], inc=(c == 1))
    evac(cwT.ap[:, :, :], FB0.ap[:, 0:66].rearrange("p (c j) -> p c j", c=2), [FB0.b0], [cwT.b0])

    class _V:
        def __init__(self, ap, b0):
            self.ap, self.b0 = ap, b0
    cw = _V(cwT.ap[:, :, 0:31], cwT.b0)
    cb = _V(cwT.ap[:, :, 31], cwT.b0)
    cg = _V(cwT.ap[:, :, 32], cwT.b0)
    diag = [kb.sb(pes, "diag", [128, 31, 128], BF16) for _ in range(2)]
    for c in range(2):
        for j in range(31):
            fw.op("dve", lambda c=c, j=j: nc.vector.tensor_scalar(out=diag[c].ap[:, j, :], in0=ident_b.ap[:, :], scalar1=cw.ap[:, c, j:j + 1], scalar2=None,
                                                                  op0=ALU.mult), reads=[ident_b.b0, cw.b0], writes=[diag[c].b0])
    yb = [kb.sb(pes, "yb", [128, 512]) for _ in range(2)]
    sq = [kb.sb(pes, "sq", [128, 512]) for _ in range(2)]
    ms = kb.sb(pes, "ms", [128, 512]); rstd = kb.sb(pes, "rstd", [128, 512])
    yn = kb.sb(pes, "yn", [128, 512]); et = kb.sb(pes, "et", [128, 512])
    Y = [kb.ps(pes, "Y") for _ in range(2)]
    SQ = kb.ps(pes, "SQ"); FB = kb.ps(pes, "FB")
    st = [kb.sb(pes, "st", [120, 256]) for _ in range(2)]
    for i in range(NS // 4):
        s_ = st[i % 2]
        fw.dma("sp", s_.ap[:, :], sconv[l, i * 120:(i + 1) * 120, :], writes=[s_.b0])
        for s in range(4):
            fw.dma("sp", cv_s[l, (4 * i + s) * 30:(4 * i + s) * 30 + 29, :], s_.ap[s * 30 + 1:s * 30 + 30, :], reads=[s_.b0], writes=[outb])
        for c in range(2):
            fw.op("pe", lambda c=c: nc.tensor.transpose(out=FB.ap[:, c * 120:(c + 1) * 120], in_=s_.ap[0:120, c * 128:(c + 1) * 128], identity=ident_f.ap[0:120, 0:120]),
                  reads=[s_.b0, ident_f.b0], writes=[FB.b0], inc=(c == 1))
        evac(usT.ap[:, :, 4 * i:4 * i + 4, 0:30], FB.ap[:, 0:240].rearrange("p (c s j) -> p c s j", c=2, s=4), [FB.b0], [usT.b0])
    prod = kb.sb(pes, "prod", [128, NS, 31])
    blocks = [(tb * 512, 512, True) for tb in range(NB)] + [(SP, NS, False)]
    for (c0, N, prompt) in blocks:
        for c in range(2):
            if prompt:
                for j in range(31):
                    fw.op("pe", lambda c=c, j=j: nc.tensor.matmul(Y[c].ap[:, 0:N], lhsT=diag[c].ap[:, j, :], rhs=uT.ap[:, c, c0 + j:c0 + j + N],
                                                                  start=(j == 0), stop=(j == 30)),
                          reads=[diag[c].b0] + [uT.b[t] for t in range(max(0, c0 // 128 - 1), c0 // 128 + 4)] + [uT.b[NT]], writes=[Y[c].b0], inc=(j == 30))
                fw.op("act", lambda c=c: nc.scalar.activation(out=yb[c].ap[:, 0:N], in_=Y[c].ap[:, 0:N], func=AF.Identity, bias=cb.ap[:, c:c + 1]),
                      reads=[Y[c].b0, cb.b0], writes=[yb[c].b0])
            else:
                fw.op("dve", lambda c=c: nc.vector.tensor_tensor(out=prod.ap[:, :, :], in0=usT.ap[:, c, :, :],
                                                                 in1=cw.ap[:, c, :].unsqueeze(1).to_broadcast([128, NS, 31]), op=ALU.mult),
                      reads=[usT.b0, cw.b0], writes=[prod.b0])
                fw.op("dve", lambda c=c: nc.vector.tensor_reduce(out=yb[c].ap[:, 0:N], in_=prod.ap[:, :, :], axis=AX.X, op=ALU.add), reads=[prod.b0], writes=[yb[c].b0])
                fw.op("act", lambda c=c: nc.scalar.activation(out=yb[c].ap[:, 0:N], in_=yb[c].ap[:, 0:N], func=AF.Identity, bias=cb.ap[:, c:c + 1]),
                      reads=[yb[c].b0, cb.b0], writes=[yb[c].b0])
            fw.op("act", lambda c=c: nc.scalar.activation(out=sq[c].ap[:, 0:N], in_=yb[c].ap[:, 0:N], func=AF.Square), reads=[yb[c].b0], writes=[sq[c].b0])
        for c in range(2):
            fw.op("pe", lambda c=c: nc.tensor.matmul(SQ.ap[:, 0:N], lhsT=ones_f.ap[:, :], rhs=sq[c].ap[:, 0:N], start=(c == 0), stop=(c == 1)),
                  reads=[ones_f.b0, sq[c].b0], writes=[SQ.b0], inc=(c == 1))
        fw.op("dve", lambda: nc.vector.tensor_scalar(out=ms.ap[:, 0:N], in0=SQ.ap[:, 0:N], scalar1=1.0 / 256, scalar2=EPS, op0=ALU.mult, op1=ALU.add),
              reads=[SQ.b0], writes=[ms.b0])
        fw.op("act", lambda: nc.scalar.activation(out=rstd.ap[:, 0:N], in_=ms.ap[:, 0:N], func=AF.Ln), reads=[ms.b0], writes=[rstd.b0])
        fw.op("act", lambda: nc.scalar.activation(out=rstd.ap[:, 0:N], in_=rstd.ap[:, 0:N], func=AF.Exp, scale=-0.5), reads=[rstd.b0], writes=[rstd.b0])
        for c in range(2):
            fw.op("dve", lambda c=c: nc.vector.scalar_tensor_tensor(out=yn.ap[:, 0:N], in0=yb[c].ap[:, 0:N], scalar=cg.ap[:, c:c + 1], in1=rstd.ap[:, 0:N],
                                                                    op0=ALU.mult, op1=ALU.mult), reads=[yb[c].b0, cg.b0, rstd.b0], writes=[yn.b0])
            fw.op("act", lambda: nc.scalar.activation(out=et.ap[:, 0:N], in_=yn.ap[:, 0:N], func=AF.Exp, scale=-1.0), reads=[yn.b0], writes=[et.b0])
            fw.op("dve", lambda: nc.vector.tensor_scalar(out=et.ap[:, 0:N], in0=et.ap[:, 0:N], scalar1=1.0, scalar2=None, op0=ALU.add), reads=[et.b0], writes=[et.b0])
            fw.op("dve", lambda: nc.vector.reciprocal(out=et.ap[:, 0:N], in_=et.ap[:, 0:N]), reads=[et.b0], writes=[et.b0])
            tl = [XT.b[t] for t in range(c0 // 128, c0 // 128 + 4)] if prompt else [XT.b[NT]]
            fw.op("dve", lambda c=c: nc.vector.tensor_tensor(out=XT.ap[:, 3 + c, c0:c0 + N], in0=yn.ap[:, 0:N], in1=et.ap[:, 0:N], op=ALU.mult),
                  reads=[yn.b0, et.b0], writes=tl)


def phase_moba(kb, l, pes, XT, QKT, qkb, Vsb, negT, cst, ident_b, ones_b):
    nc, fw = kb.nc, kb.fw
    NT = kb.NT
    NB = NT // 4
    e48, cmask = cst["e48_b"], cst["cmask_b"]
    S = [kb.ps(pes, "S") for _ in range(3)]
    num = [kb.ps(pes, "num") for _ in range(2)]
    den = [kb.ps(pes, "den") for _ in range(2)]
    P = [kb.sb(pes, "P", [128, 512], BF16) for _ in range(3)]
    rd = [kb.sb(pes, "rd", [128, 512]) for _ in range(2)]
    qz = [[kb.sb(pes, "qz", [128, 512], BF16) for _ in range(2)] for _ in range(2)]
    for par in range(2):
        for k in range(2):
            fw.op("pool", lambda: nc.gpsimd.memset(qz[par][k].ap[:, :], 0.0), writes=[qz[par][k].b0])
    items = []
    i = 0
    it = 0
    for h in range(6):
        for b in range(NB):
            nk = 4 * b + 4
            for kj in range(nk):
                items.append(_moba_tile(kb, h, b, kj, nk, i, it, XT, QKT, qkb, Vsb, negT, e48, cmask, ident_b, ones_b, S, P, num, den, rd, qz))
                i += 1
            it += 1
    emit_pipelined(items, 1)


def _moba_tile(kb, h, b, kj, nk, i, it, XT, QKT, qkb, Vsb, negT, e48, cmask, ident_b, ones_b, S, P, num, den, rd, qz):
    nc, fw = kb.nc, kb.fw
    c, pb = h // 2, 64 * (h % 2)
    nm, dn = num[it % 2], den[it % 2]
    qcols = slice(b * 512, (b + 1) * 512)
    q = qz[h % 2][b % 2]
    Sb, Pb = S[i % 3], P[i % 3]
    diag = kj >= 4 * b

    def st0():
        if kj == 0:
            fw.op("pool", lambda: nc.gpsimd.tensor_copy(out=q.ap[pb:pb + 64, :], in_=QKT.ap[pb:pb + 64, 0 + c, qcols]),
                  reads=[qkb(0 + c, t) for t in range(4 * b, 4 * b + 4)], writes=[q.b0])
        fw.op("pe", lambda: nc.tensor.matmul(Sb.ap[:, :], lhsT=QKT.ap[:, 3 + c, kj * 128:(kj + 1) * 128], rhs=q.ap[:, :],
                                             start=True, stop=False), reads=[qkb(3 + c, kj), q.b0], writes=[Sb.b0], inc=False)
        r = h * 8 + kj // 2
        fw.op("pe", lambda: nc.tensor.matmul(Sb.ap[:, :], lhsT=e48.ap[:, r * 128:(r + 1) * 128], rhs=negT.ap[:, qcols], start=False, stop=not diag),
              reads=[e48.b0] + [negT.b[t] for t in range(4 * b, 4 * b + 4)], writes=[Sb.b0], inc=not diag)
        if diag:
            rr = 4 + kj - 4 * b
            fw.op("pe", lambda: nc.tensor.matmul(Sb.ap[:, :], lhsT=ident_b.ap[:, :], rhs=cmask.ap[:, rr * 512:(rr + 1) * 512], start=False, stop=True),
                  reads=[ident_b.b0, cmask.b0], writes=[Sb.b0])

    def st1():
        fw.op("act", lambda: nc.scalar.activation(out=Pb.ap[:, :], in_=Sb.ap[:, :], func=AF.Exp), reads=[Sb.b0], writes=[Pb.b0])

    def st2():
        fw.op("pe", lambda: nc.tensor.matmul(nm.ap[:, :], lhsT=Vsb.ap[:, kj, c * 128:(c + 1) * 128], rhs=Pb.ap[:, :], start=(kj == 0), stop=(kj == nk - 1)),
              reads=[Vsb.b[kj], Pb.b0], writes=[nm.b0], inc=False)
        fw.op("pe", lambda: nc.tensor.matmul(dn.ap[:, :], lhsT=ones_b.ap[:, :], rhs=Pb.ap[:, :], start=(kj == 0), stop=(kj == nk - 1)),
              reads=[ones_b.b0, Pb.b0], writes=[dn.b0], inc=True)
        if kj == nk - 1:
            rdb = rd[it % 2]
            fw.op("dve", lambda: nc.vector.reciprocal(out=rdb.ap[pb:pb + 64, :], in_=dn.ap[pb:pb + 64, :]), reads=[dn.b0], writes=[rdb.b0])
            fw.op("dve", lambda: nc.vector.tensor_tensor(out=XT.ap[pb:pb + 64, c, qcols], in0=nm.ap[pb:pb + 64, :], in1=rdb.ap[pb:pb + 64, :], op=ALU.mult),
                  reads=[nm.b0, rdb.b0], writes=[XT.b[t] for t in range(4 * b, 4 * b + 4)])
    return [st0, st1, st2]


def phase_sb(kb, l, pes, XT, QKT, qkb, Vsb, cst, ident_b, cvals):
    nc, fw = kb.nc, kb.fw
    NT = kb.NT
    NB = NT // 4
    cmask, trineg, selneg, selb, su16 = cst["cmask_b"], cst["trineg_b"], cst["selneg_b"], cst["selb_b"], cst["su16"]
    Z = [kb.ps(pes, "Z") for _ in range(3)]
    Rb = kb.ps(pes, "Rb"); Cb = kb.ps(pes, "Cb")
    oacc = [kb.ps(pes, "oacc") for _ in range(2)]
    E = [kb.sb(pes, "E", [128, 512]) for _ in range(NT)]
    SPt = [kb.sb(pes, "SPt", [128, 512], BF16) for _ in range(NT)]
    Ab = [kb.sb(pes, "Ab", [128, 512], BF16) for _ in range(3)]
    Rf = kb.sb(pes, "Rf", [16, 512]); chi = kb.sb(pes, "chi", [128, 512], BF16); clo = kb.sb(pes, "clo", [128, 512], BF16)
    fw.op("pool", lambda: nc.gpsimd.memset(chi.ap[:, :], 0.0), writes=[chi.b0])
    fw.op("pool", lambda: nc.gpsimd.memset(clo.ap[:, :], 0.0), writes=[clo.b0])
    qz = [[kb.sb(pes, "qz", [128, 512], BF16) for _ in range(2)] for _ in range(2)]
    for par in range(2):
        for k in range(2):
            fw.op("pool", lambda: nc.gpsimd.memset(qz[par][k].ap[:, :], 0.0), writes=[qz[par][k].b0])
    ctr = {"i": 0}
    hbs = [(h, b) for h in range(6) for b in range(NB)]

    def zmm(h, b, q, Zb, kj, last):
        c = h // 2
        diag = kj >= 4 * b
        fw.op("pe", lambda: nc.tensor.matmul(Zb.ap[:, :], lhsT=QKT.ap[:, 9 + c, kj * 128:(kj + 1) * 128], rhs=q.ap[:, :],
                                             start=True, stop=(last and not diag)), reads=[qkb(9 + c, kj), q.b0], writes=[Zb.b0], inc=(last and not diag))
        if diag:
            rr = kj - 4 * b
            fw.op("pe", lambda: nc.tensor.matmul(Zb.ap[:, :], lhsT=ident_b.ap[:, :], rhs=cmask.ap[:, rr * 512:(rr + 1) * 512], start=False, stop=last),
                  reads=[ident_b.b0, cmask.b0], writes=[Zb.b0], inc=last)

    def P1(n):
        h, b = hbs[n]
        c, pb = h // 2, 64 * (h % 2)
        nk = 4 * b + 4
        qcols = slice(b * 512, (b + 1) * 512)
        q = qz[h % 2][b % 2]
        fw.op("pool", lambda: nc.gpsimd.tensor_copy(out=q.ap[pb:pb + 64, :], in_=QKT.ap[pb:pb + 64, 6 + c, qcols]),
              reads=[qkb(6 + c, t) for t in range(4 * b, 4 * b + 4)], writes=[q.b0])
        for kj in range(nk):
            Zb = Z[ctr["i"] % 3]
            ctr["i"] += 1
            zmm(h, b, q, Zb, kj, True)
            fw.op("act", lambda: nc.scalar.activation(out=E[kj].ap[:, :], in_=Zb.ap[:, :], func=AF.Exp), reads=[Zb.b0], writes=[E[kj].b0])

    def P2(n):
        h, b = hbs[n]
        nk = 4 * b + 4
        for kj in range(nk):
            fw.op("act", lambda kj=kj: nc.scalar.activation(out=SPt[kj].ap[:, :], in_=E[kj].ap[:, :], func=AF.Ln, bias=cvals.ap[:, 0:1]),
                  reads=[E[kj].b0, cvals.b0], writes=[SPt[kj].b0])
        for kj in range(nk):
            fw.op("pe", lambda kj=kj: nc.tensor.matmul(Rb.ap[:, :], lhsT=selneg.ap[:, kj * 128:(kj + 1) * 128], rhs=SPt[kj].ap[:, :], start=(kj == 0), stop=(kj == nk - 1)),
                  reads=[selneg.b0, SPt[kj].b0], writes=[Rb.b0], inc=(kj == nk - 1))
        fw.op("act", lambda: nc.scalar.copy(out=Rf.ap[0:16, :], in_=Rb.ap[0:16, :]), reads=[Rb.b0], writes=[Rf.b0])
        fw.op("pe", lambda: nc.tensor.matmul(Cb.ap[0:16, :], lhsT=su16.ap[0:16, 0:16], rhs=Rf.ap[0:16, :], start=True, stop=True),
              reads=[su16.b0, Rf.b0], writes=[Cb.b0])
        fw.op("dve", lambda: nc.vector.tensor_copy(out=chi.ap[0:16, :], in_=Cb.ap[0:16, :]), reads=[Cb.b0], writes=[chi.b0])
        fw.op("dve", lambda: nc.vector.tensor_tensor(out=clo.ap[0:16, :], in0=Cb.ap[0:16, :], in1=chi.ap[0:16, :], op=ALU.subtract),
              reads=[Cb.b0, chi.b0], writes=[clo.b0])

    def P3(n):
        h, b = hbs[n]
        c, pb = h // 2, 64 * (h % 2)
        nk = 4 * b + 4
        qcols = slice(b * 512, (b + 1) * 512)
        q = qz[h % 2][b % 2]
        oa = oacc[n % 2]
        items = []
        for kj in range(nk):
            i = ctr["i"]
            ctr["i"] += 1
            items.append(_sb_tile(kb, h, b, q, kj, nk, Z[i % 3], Ab[i % 3], oa, zmm, XT, Vsb, SPt, trineg, selb, chi, clo, c, pb, qcols))
        emit_pipelined(items, 1)

    P1(0)
    for n in range(len(hbs)):
        P2(n)
        if n + 1 < len(hbs):
            P1(n + 1)
        P3(n)


def _sb_tile(kb, h, b, q, kj, nk, Zb, A, oa, zmm, XT, Vsb, SPt, trineg, selb, chi, clo, c, pb, qcols):
    nc, fw = kb.nc, kb.fw

    def st0():
        zmm(h, b, q, Zb, kj, False)
        fw.op("pe", lambda: nc.tensor.matmul(Zb.ap[:, :], lhsT=trineg.ap[:, :], rhs=SPt[kj].ap[:, :], start=False, stop=False),
              reads=[trineg.b0, SPt[kj].b0], writes=[Zb.b0], inc=False)
        fw.op("pe", lambda: nc.tensor.matmul(Zb.ap[:, :], lhsT=selb.ap[:, kj * 128:(kj + 1) * 128], rhs=chi.ap[:, :], start=False, stop=False),
              reads=[selb.b0, chi.b0], writes=[Zb.b0], inc=False)
        fw.op("pe", lambda: nc.tensor.matmul(Zb.ap[:, :], lhsT=selb.ap[:, kj * 128:(kj + 1) * 128], rhs=clo.ap[:, :], start=False, stop=True),
              reads=[selb.b0, clo.b0], writes=[Zb.b0], inc=True)

    def st1():
        fw.op("act", lambda: nc.scalar.activation(out=A.ap[:, :], in_=Zb.ap[:, :], func=AF.Exp), reads=[Zb.b0], writes=[A.b0])

    def st2():
        fw.op("pe", lambda: nc.tensor.matmul(oa.ap[:, :], lhsT=Vsb.ap[:, kj, 384 + c * 128:384 + (c + 1) * 128], rhs=A.ap[:, :],
                                             start=(kj == 0), stop=(kj == nk - 1)), reads=[Vsb.b[kj], A.b0], writes=[oa.b0], inc=True)
        if kj == nk - 1:
            fw.op("dve", lambda: nc.vector.tensor_copy(out=XT.ap[pb:pb + 64, 5 + c, qcols], in_=oa.ap[pb:pb + 64, :]), reads=[oa.b0],
                  writes=[XT.b[t] for t in range(4 * b, 4 * b + 4)])
    return [st0, st1, st2]


def phase_mix(kb, l, pes, XT, w_out, gvec, hsrc, hbuf, hb, ident_b, evac, norm_to_T, rstd_from_ssq):
    nc, fw = kb.nc, kb.fw
    NT, NS = kb.NT, kb.NS
    wo = kb.sb(pes, "wo", [128, 8, D], BF16)
    fw.dma("pool", wo.ap[:, :, :], w_out[l, :, :].rearrange("(k p) n -> p k n", p=128), writes=[wo.b0])
    gpost = kb.sb(pes, "gpost", [128, D]); gpre = kb.sb(pes, "gpre", [128, D])
    fw.dma("sp", gpost.ap[:], gvec["g_post_mix"][l, :].partition_broadcast(128), writes=[gpost.b0])
    fw.dma("sp", gpre.ap[:], gvec["g_pre_mlp"][l, :].partition_broadcast(128), writes=[gpre.b0])
    hr = [kb.sb(pes, "hr", [128, D]) for _ in range(2)]
    tmps = [(kb.sb(pes, "junk", [128, D], BF16), kb.sb(pes, "ssq", [128, 1]), kb.sb(pes, "tmp1", [128, 1]),
             kb.sb(pes, "rs", [128, 1]), kb.sb(pes, "abf", [128, D], BF16)) for _ in range(2)]
    s2 = [kb.sb(pes, "s2", [128, 2]) for _ in range(2)]
    stot = [kb.sb(pes, "stot", [128, 1]) for _ in range(2)]
    t1 = [kb.sb(pes, "t1", [128, 1]) for _ in range(2)]
    rs2 = [kb.sb(pes, "rs2", [128, 1]) for _ in range(2)]
    dlt = [kb.sb(pes, "dlt", [128, D]) for _ in range(2)]
    mixb = [[kb.ps(pes, "mix") for _ in range(2)] for _ in range(2)]
    trbs = [kb.ps(pes, "trb", [128, 1024], BF16) for _ in range(2)]
    for t in range(NT + 1):
        R = kb.rows(t)
        k2 = t % 2
        hT = hr[k2]
        fw.dma("sp", hT.ap[0:R, :], hsrc(l, t), reads=[hb[t]], writes=[hT.b0])
        junk = tmps[k2][0]
        for half in range(2):
            bank = mixb[k2][half]
            for c in range(8):
                fw.op("pe", lambda c=c: nc.tensor.matmul(bank.ap[0:R, :], lhsT=XT.ap[:, c, t * 128:t * 128 + R], rhs=wo.ap[:, c, half * 512:(half + 1) * 512],
                                                         start=(c == 0), stop=(c == 7)), reads=[XT.b[t], wo.b0], writes=[bank.b0], inc=(c == 7))
            fw.op("act", lambda: nc.scalar.activation(out=junk.ap[0:R, 0:512], in_=bank.ap[0:R, :], func=AF.Square, accum_out=s2[k2].ap[0:R, half:half + 1]),
                  reads=[bank.b0], writes=[junk.b0, s2[k2].b0])
        fw.op("dve", lambda: nc.vector.tensor_tensor(out=stot[k2].ap[0:R, :], in0=s2[k2].ap[0:R, 0:1], in1=s2[k2].ap[0:R, 1:2], op=ALU.add),
              reads=[s2[k2].b0], writes=[stot[k2].b0])
        rstd_from_ssq(stot[k2].ap[0:R, 0:1], stot[k2].b0, R, D, t1[k2], rs2[k2])
        for half in range(2):
            bank = mixb[k2][half]
            hs = slice(half * 512, (half + 1) * 512)
            fw.op("dve", lambda: nc.vector.scalar_tensor_tensor(out=dlt[k2].ap[0:R, hs], in0=bank.ap[0:R, :], scalar=rs2[k2].ap[0:R, 0:1], in1=gpost.ap[0:R, hs],
                                                                op0=ALU.mult, op1=ALU.mult), reads=[bank.b0, rs2[k2].b0, gpost.b0], writes=[dlt[k2].b0])
        fw.op("pool", lambda: nc.gpsimd.tensor_tensor(out=hT.ap[0:R, :], in0=hT.ap[0:R, :], in1=dlt[k2].ap[0:R, :], op=ALU.add),
              reads=[hT.b0, dlt[k2].b0], writes=[hT.b0])
        fw.dma("sp", hbuf[t * 128:t * 128 + R, :], hT.ap[0:R, :], reads=[hT.b0], writes=[hb[t]])
        norm_to_T(l, t, hT, gpre, XT, trbs[k2], tmps[k2])


def phase_mlp(kb, l, pes, XT, w_up, w_down, gvec, hbuf, hb, hdst, outb, final, evac, rstd_from_ssq):
    nc, fw = kb.nc, kb.fw
    NT, NS = kb.NT, kb.NS
    SP = NT * 128
    NB = NT // 4
    NG, GF = 8, 4
    facc = kb.sb(pes, "facc", [128, NT + 1, D], F32, nb=NT + 1)
    wu = [kb.sb(pes, "wu", [128, 8, 512], BF16) for _ in range(2)]
    wd = [kb.sb(pes, "wd", [128, GF, D], BF16) for _ in range(2)]
    actT = [kb.sb(pes, "actT", [128, GF, 512], BF16) for _ in range(2)]
    rt = [kb.sb(pes, "rt", [128, 512]) for _ in range(2)]
    U = [kb.ps(pes, "U") for _ in range(3)]
    Dn = [kb.ps(pes, "Dn") for _ in range(4)]
    blocks = [(tb * 512, 512, [4 * tb + i for i in range(4)]) for tb in range(NB)] + [(SP, NS, [NT])]
    ctr = {"iu": 0, "idn": 0}

    def make_blk(g, bi, c0, N, tiles):
        wug, wdg = wu[g % 2], wd[g % 2]
        aT = actT[(g * len(blocks) + bi) % 2]

        def st0():
            if (bi == 0 and g == 0) or bi == 1:
                for g2 in ([0] if (bi == 0 and g == 0) else [g + 1]):
                    if g2 < NG:
                        fw.dma("pool", wu[g2 % 2].ap[:, :, :], w_up[l, :, g2 * 512:(g2 + 1) * 512].rearrange("(k p) n -> p k n", p=128), writes=[wu[g2 % 2].b0])
                        fw.dma("pool", wd[g2 % 2].ap[:, :, :], w_down[l, g2 * 512:(g2 + 1) * 512, :].rearrange("(f p) n -> p f n", p=128), writes=[wd[g2 % 2].b0])
            for fc in range(GF):
                Ub = U[ctr["iu"] % 3]
                rtb = rt[ctr["iu"] % 2]
                ctr["iu"] += 1
                for k in range(8):
                    fw.op("pe", lambda k=k: nc.tensor.matmul(Ub.ap[:, 0:N], lhsT=wug.ap[:, k, fc * 128:(fc + 1) * 128], rhs=XT.ap[:, k, c0:c0 + N],
                                                             start=(k == 0), stop=(k == 7)), reads=[wug.b0] + [XT.b[t] for t in tiles], writes=[Ub.b0], inc=(k == 7))
                fw.op("act", lambda: nc.scalar.activation(out=rtb.ap[:, 0:N], in_=Ub.ap[:, 0:N], func=AF.Relu), reads=[Ub.b0], writes=[rtb.b0])
                fw.op("act", lambda: nc.scalar.activation(out=aT.ap[:, fc, 0:N], in_=rtb.ap[:, 0:N], func=AF.Square),
                      reads=[rtb.b0], writes=[aT.b0])

        def st1():
            for ti, t in enumerate(tiles):
                R = kb.rows(t)
                for half in range(2):
                    Db = Dn[ctr["idn"] % 4]
                    ctr["idn"] += 1
                    hs = slice(half * 512, (half + 1) * 512)
                    for fc in range(GF):
                        fw.op("pe", lambda fc=fc: nc.tensor.matmul(Db.ap[0:R, :], lhsT=aT.ap[:, fc, ti * 128:ti * 128 + R], rhs=wdg.ap[:, fc, hs],
                                                                   start=(fc == 0), stop=(fc == GF - 1)), reads=[aT.b0, wdg.b0], writes=[Db.b0], inc=(fc == GF - 1))
                    if g == 0:
                        fw.op("act", lambda: nc.scalar.copy(out=facc.ap[0:R, t, hs], in_=Db.ap[0:R, :]), reads=[Db.b0], writes=[facc.b[t]])
                    else:
                        fw.op("dve", lambda: nc.vector.tensor_tensor(out=facc.ap[0:R, t, hs], in0=Db.ap[0:R, :], in1=facc.ap[0:R, t, hs], op=ALU.add),
                              reads=[Db.b0, facc.b[t]], writes=[facc.b[t]])
        return [st0, st1]

    items = []
    for g in range(NG):
        for bi, (c0, N, tiles) in enumerate(blocks):
            items.append(make_blk(g, bi, c0, N, tiles))
    emit_pipelined(items, 1)
    gpost = kb.sb(pes, "gpm", [128, D])
    fw.dma("sp", gpost.ap[:], gvec["g_post_mlp"][l, :].partition_broadcast(128), writes=[gpost.b0])
    hr = [kb.sb(pes, "hr", [128, D]) for _ in range(2)]
    junk = [kb.sb(pes, "junk", [128, D], BF16) for _ in range(2)]
    ssq = [kb.sb(pes, "ssq", [128, 1]) for _ in range(2)]
    t1 = [kb.sb(pes, "t1", [128, 1]) for _ in range(2)]
    rs = [kb.sb(pes, "rs", [128, 1]) for _ in range(2)]
    for t in range(NT + 1):
        R = kb.rows(t)
        k2 = t % 2
        hT = hr[k2]
        fw.dma("sp", hT.ap[0:R, :], hbuf[t * 128:t * 128 + R, :], reads=[hb[t]], writes=[hT.b0])
        fw.op("act", lambda: nc.scalar.activation(out=junk[k2].ap[0:R, :], in_=facc.ap[0:R, t, :], func=AF.Square, accum_out=ssq[k2].ap[0:R, 0:1]),
              reads=[facc.b[t]], writes=[junk[k2].b0, ssq[k2].b0])
        rstd_from_ssq(ssq[k2].ap[0:R, 0:1], ssq[k2].b0, R, D, t1[k2], rs[k2])
        fw.op("dve", lambda: nc.vector.scalar_tensor_tensor(out=facc.ap[0:R, t, :], in0=facc.ap[0:R, t, :], scalar=rs[k2].ap[0:R, 0:1], in1=gpost.ap[0:R, :],
                                                            op0=ALU.mult, op1=ALU.mult), reads=[facc.b[t], rs[k2].b0, gpost.b0], writes=[facc.b[t]])
        fw.op("pool", lambda: nc.gpsimd.tensor_tensor(out=hT.ap[0:R, :], in0=hT.ap[0:R, :], in1=facc.ap[0:R, t, :], op=ALU.add),
              reads=[hT.b0, facc.b[t]], writes=[hT.b0])
        fw.dma("sp", hdst(l, t, final), hT.ap[0:R, :], reads=[hT.b0], writes=[outb if final else hb[t]])


def phase_sample(kb, l, pes, XT, QS, VN, ck, cv, idx, cst, ident_b, ones_b, ones_f, cvals, evac):
    nc, fw = kb.nc, kb.fw
    NT, NS, NPG = kb.NT, kb.NS, kb.NPG
    SP = NT * 128
    NBK = NPG // 2
    H6 = 6 * NPG
    trineg, hsel = cst["trineg_b"], cst["hsel"]
    Vs = [kb.sb(pes, "Vs", [128, NPG, 768], BF16, nb=NPG) for _ in range(3)]
    kT = [kb.sb(pes, "kT", [128, 6, 128], BF16) for _ in range(3)]
    trb = [kb.ps(pes, "trb", [128, 1024], BF16) for _ in range(2)]
    Zs = [kb.ps(pes, "Zs") for _ in range(2)]
    misc = [kb.ps(pes, "misc") for _ in range(2)]
    Oall = kb.ps(pes, "Oall")
    Qblk = kb.sb(pes, "Qblk", [128, NS, 6, 2], BF16)
    fw.op("dve", lambda: nc.vector.memset(Qblk.ap[:], 0.0), writes=[Qblk.b0])
    for e in range(2):
        pb = 64 * e
        for (d0, s0) in ((0, 0), (3, 6)):
            fw.op("dve", lambda: nc.vector.tensor_copy(out=Qblk.ap[pb:pb + 64, :, d0:d0 + 3, e].rearrange("p s c -> p c s"), in_=QS.ap[pb:pb + 64, s0:s0 + 3, :]),
                  reads=[QS.b0], writes=[Qblk.b0])
    vnT = kb.sb(pes, "vnT", [128, 3, NS]); prod = kb.sb(pes, "prod", [128, 3, NS]); pself = kb.sb(pes, "pself", [128, 3, NS])
    for c in range(3):
        fw.op("pe", lambda c=c: nc.tensor.transpose(out=trb[0].ap[:, c * NS:(c + 1) * NS], in_=VN.ap[0:NS, c * 128:(c + 1) * 128], identity=ident_b.ap[0:NS, 0:NS]),
              reads=[VN.b0, ident_b.b0], writes=[trb[0].b0], inc=(c == 2))
    evac(vnT.ap[:, :, :], trb[0].ap[:, 0:3 * NS].rearrange("p (c s) -> p c s", c=3), [trb[0].b0], [vnT.b0])
    fw.op("dve", lambda: nc.vector.tensor_tensor(out=prod.ap[:, :, :], in0=QS.ap[:, 0:3, :], in1=QS.ap[:, 3:6, :], op=ALU.mult), reads=[QS.b0], writes=[prod.b0])
    fw.op("pe", lambda: nc.tensor.matmul(misc[0].ap[:, 0:3 * NS], lhsT=hsel.ap[:, :], rhs=prod.ap[:, :, :].rearrange("p c s -> p (c s)"), start=True, stop=True),
          reads=[hsel.b0, prod.b0], writes=[misc[0].b0])
    fw.op("act", lambda: nc.scalar.activation(out=pself.ap[:, :, :].rearrange("p c s -> p (c s)"), in_=misc[0].ap[:, 0:3 * NS], func=AF.Exp),
          reads=[misc[0].b0], writes=[pself.b0])
    segg = kb.sb(pes, "segg", [128, H6])
    fw.op("dve", lambda: nc.vector.memset(segg.ap[:], 1.0), writes=[segg.b0])
    fw.op("dve", lambda: nc.vector.memset(segg.ap[:, :].rearrange("p (h j) -> p h j", j=NPG)[:, :, 0:1], 0.0), writes=[segg.b0])

    def tmp(name, dt=F32, n=H6):
        return [kb.sb(pes, name, [128, n], dt) for _ in range(2)]
    Ee, SPf, SPb, Csb, Incl, Wa, Wb, Asb = tmp("Ee"), tmp("SPf"), tmp("SPb", BF16), tmp("Csb"), tmp("Incl"), tmp("Wa"), tmp("Wb"), tmp("Asb", BF16)
    Zc, G8, Mx, Ng, Wm, Pm = tmp("Zc"), tmp("G8", F32, 48), tmp("Mx", F32, 48), tmp("Ng", F32, 48), tmp("Wm"), tmp("Pm", BF16)
    tristr = cst["tristr_b"]
    kpg = [kb.sb(pes, "kpg2", [128, 2, 768], BF16) for _ in range(4)]
    Cs2, Inc2, Car = tmp("Cs2", F32, 6 * NBK), tmp("Inc2", F32, 6 * NBK), tmp("Car", F32, 6 * NBK)
    segg8 = kb.sb(pes, "segg8", [128, 6 * NBK])
    fw.op("dve", lambda: nc.vector.memset(segg8.ap[:], 1.0), writes=[segg8.b0])
    fw.op("dve", lambda: nc.vector.memset(segg8.ap[:, :].rearrange("p (h n) -> p h n", n=NBK)[:, :, 0:1], 0.0), writes=[segg8.b0])
    ckv = ck.rearrange("(r two) c -> r (two c)", two=2)
    cvv = cv.rearrange("(r two) c -> r (two c)", two=2)
    eo = l * kb.NPOOL * 128 * 768
    ctr = {"gi": 0, "ti": 0}
    items = []
    for s_ in range(NS):
        items.append(_sample_item(locals(), s_))
    emit_pipelined(items, 1)
    num = kb.sb(pes, "fnum", [128, 3, NS]); den = kb.sb(pes, "fden", [128, 3, NS])
    for c3 in range(3):
        for e in range(2):
            pb = 64 * e
            sbv = Oall.ap[pb:pb + 64, 0:6 * NS].rearrange("p (s x) -> p s x", x=6)[:, :, 2 * c3 + e]
            fw.op("dve", lambda: nc.vector.tensor_copy(out=XT.ap[pb:pb + 64, 5 + c3, SP:SP + NS], in_=sbv), reads=[Oall.b0], writes=[XT.b[NT]])
            mov = Oall.ap[pb:pb + 64, 6 * NS:12 * NS].rearrange("p (s x) -> p s x", x=6)[:, :, 2 * c3 + e]
            dnv = Oall.ap[pb:pb + 64, 12 * NS:18 * NS].rearrange("p (s x) -> p s x", x=6)[:, :, 2 * c3 + e]
            fw.op("dve", lambda: nc.vector.tensor_tensor(out=num.ap[pb:pb + 64, c3, :], in0=pself.ap[pb:pb + 64, c3, :], in1=vnT.ap[pb:pb + 64, c3, :], op=ALU.mult),
                  reads=[pself.b0, vnT.b0], writes=[num.b0])
            fw.op("dve", lambda: nc.vector.tensor_tensor(out=num.ap[pb:pb + 64, c3, :], in0=mov, in1=num.ap[pb:pb + 64, c3, :], op=ALU.add),
                  reads=[Oall.b0, num.b0], writes=[num.b0])
            fw.op("dve", lambda: nc.vector.tensor_tensor(out=den.ap[pb:pb + 64, c3, :], in0=dnv, in1=pself.ap[pb:pb + 64, c3, :], op=ALU.add),
                  reads=[Oall.b0, pself.b0], writes=[den.b0])
    fw.op("dve", lambda: nc.vector.reciprocal(out=den.ap[:, :, :], in_=den.ap[:, :, :]), reads=[den.b0], writes=[den.b0])
    fw.op("dve", lambda: nc.vector.tensor_tensor(out=XT.ap[:, 0:3, SP:SP + NS], in0=num.ap[:, :, :], in1=den.ap[:, :, :], op=ALU.mult), reads=[num.b0, den.b0], writes=[XT.b[NT]])


def _sample_item(env, s):
    g = env
    kb, fw, nc = g["kb"], g["fw"], g["nc"]
    names = ["NS", "NPG", "NBK", "H6", "Vs", "Zs", "misc", "Oall", "Qblk", "kpg", "kT", "trb", "idx", "ckv", "cvv", "eo", "evac", "ident_b", "ones_b", "ones_f", "cvals",
             "trineg", "tristr", "segg8", "Ee", "SPf", "SPb", "Csb", "Cs2", "Inc2", "Car", "Wa", "Asb", "Zc", "G8", "Mx", "Ng", "Wm", "Pm", "ctr"]
    NS, NPG, NBK, H6, Vs, Zs, misc, Oall, Qblk, kpg, kT, trb, idx, ckv, cvv, eo, evac, ident_b, ones_b, ones_f, cvals, \
        trineg, tristr, segg8, Ee, SPf, SPb, Csb, Cs2, Inc2, Car, Wa, Asb, Zc, G8, Mx, Ng, Wm, Pm, ctr = [g[n] for n in names]
    k2 = s % 2
    V = Vs[s % 3]
    Zb = Zs[k2]

    def st0():
        Zb = Zs[k2]
        for n in range(NBK):
            kp = kpg[ctr["gi"] % 4]
            ctr["gi"] += 1
            col = s * NBK + n
            fw.gather(kp.ap[:, :, :].rearrange("p a c -> p (a c)"), ckv, idx.ap[:, col:col + 1], reads=[idx.b0], writes=[kp.b0], element_offset=eo)
            fw.gather(V.ap[:, 2 * n:2 * n + 2, :].rearrange("p a c -> p (a c)"), cvv, idx.ap[:, col:col + 1], reads=[idx.b0], writes=[V.b[2 * n], V.b[2 * n + 1]], element_offset=eo)
            for e in range(2):
                j = 2 * n + e
                tb = trb[ctr["ti"] % 2]
                kt = kT[ctr["ti"] % 3]
                ctr["ti"] += 1
                for cc in range(6):
                    fw.op("pe", lambda cc=cc: nc.tensor.transpose(out=tb.ap[:, cc * 128:(cc + 1) * 128], in_=kp.ap[:, e, cc * 128:(cc + 1) * 128], identity=ident_b.ap[:, :]),
                          reads=[kp.b0, ident_b.b0], writes=[tb.b0], inc=(cc == 5))
                evac(kt.ap[:, :, :], tb.ap[:, 0:768].rearrange("p (c r) -> p c r", c=6), [tb.b0], [kt.b0])
                for cc in range(6):
                    if cc < 3:
                        o = Zb.ap[:, 0:H6].rearrange("p (h j) -> p h j", j=NPG)[:, 2 * cc:2 * cc + 2, j]
                    else:
                        o = Zb.ap[:, H6:2 * H6].rearrange("p (h j) -> p h j", j=NPG)[:, 2 * (cc - 3):2 * (cc - 3) + 2, 2 * (NBK - 1 - n) + e]
                    fw.op("pe", lambda cc=cc, o=o: nc.tensor.matmul(o, lhsT=kt.ap[:, cc, :], rhs=Qblk.ap[:, s, cc, :], start=True, stop=True),
                          reads=[kt.b0, Qblk.b0], writes=[Zb.b0], inc=(cc == 5))

    def st1():
        m0, m1 = misc[0], misc[1]
        fw.op("act", lambda: nc.scalar.activation(out=Ee[k2].ap[:, :], in_=Zb.ap[:, H6:2 * H6], func=AF.Exp), reads=[Zb.b0], writes=[Ee[k2].b0])
        fw.op("act", lambda: nc.scalar.activation(out=SPf[k2].ap[:, :], in_=Ee[k2].ap[:, :], func=AF.Ln, bias=cvals.ap[:, 0:1]), reads=[Ee[k2].b0, cvals.b0], writes=[SPf[k2].b0])
        fw.op("dve", lambda: nc.vector.tensor_copy(out=SPb[k2].ap[:, :], in_=SPf[k2].ap[:, :]), reads=[SPf[k2].b0], writes=[SPb[k2].b0])
        spv = SPb[k2].ap[:, :].rearrange("p (q two) -> p q two", two=2)
        m0v = m0.ap[:, 0:H6].rearrange("p (q two) -> p q two", two=2)
        fw.op("pe", lambda: nc.tensor.matmul(m0v[:, :, 0], lhsT=trineg.ap[:, :], rhs=spv[:, :, 0], start=True, stop=False), reads=[trineg.b0, SPb[k2].b0], writes=[m0.b0], inc=False)
        fw.op("pe", lambda: nc.tensor.matmul(m0v[:, :, 0], lhsT=trineg.ap[:, :], rhs=spv[:, :, 1], start=False, stop=True), reads=[trineg.b0, SPb[k2].b0], writes=[m0.b0], inc=False)
        fw.op("pe", lambda: nc.tensor.matmul(m0v[:, :, 1], lhsT=tristr.ap[:, :], rhs=spv[:, :, 0], start=True, stop=False), reads=[tristr.b0, SPb[k2].b0], writes=[m0.b0], inc=False)
        fw.op("pe", lambda: nc.tensor.matmul(m0v[:, :, 1], lhsT=trineg.ap[:, :], rhs=spv[:, :, 1], start=False, stop=True), reads=[trineg.b0, SPb[k2].b0], writes=[m0.b0], inc=True)
        fw.op("pe", lambda: nc.tensor.matmul(m1.ap[:, 0:H6], lhsT=ones_f.ap[:, :], rhs=SPf[k2].ap[:, :], start=True, stop=True), reads=[ones_f.b0, SPf[k2].b0], writes=[m1.b0])
        fw.op("act", lambda: nc.scalar.copy(out=Csb[k2].ap[:, :], in_=m1.ap[:, 0:H6]), reads=[m1.b0], writes=[Csb[k2].b0])
        csv = Csb[k2].ap[:, :].rearrange("p (q two) -> p q two", two=2)
        fw.op("dve", lambda: nc.vector.tensor_tensor(out=Cs2[k2].ap[:, :].unsqueeze(2), in0=csv[:, :, 0:1], in1=csv[:, :, 1:2], op=ALU.add), reads=[Csb[k2].b0], writes=[Cs2[k2].b0])
        fw.op("dve", lambda: nc.vector.tensor_tensor_scan(out=Inc2[k2].ap[:, :], data0=segg8.ap[:, :], data1=Cs2[k2].ap[:, :], initial=0.0, op0=ALU.mult, op1=ALU.add),
              reads=[segg8.b0, Cs2[k2].b0], writes=[Inc2[k2].b0])
        fw.op("dve", lambda: nc.vector.tensor_tensor(out=Car[k2].ap[:, :], in0=Inc2[k2].ap[:, :], in1=Cs2[k2].ap[:, :], op=ALU.subtract), reads=[Inc2[k2].b0, Cs2[k2].b0], writes=[Car[k2].b0])
        fw.op("dve", lambda: nc.vector.tensor_tensor(out=Wa[k2].ap[:, :].rearrange("p (q two) -> p q two", two=2), in0=Zb.ap[:, H6:2 * H6].rearrange("p (q two) -> p q two", two=2),
                                                     in1=Car[k2].ap[:, :].unsqueeze(2).to_broadcast([128, 6 * NBK, 2]), op=ALU.subtract), reads=[Zb.b0, Car[k2].b0], writes=[Wa[k2].b0])
        fw.op("dve", lambda: nc.vector.tensor_tensor(out=Wa[k2].ap[:, :], in0=m0.ap[:, 0:H6], in1=Wa[k2].ap[:, :], op=ALU.add), reads=[m0.b0, Wa[k2].b0], writes=[Wa[k2].b0])
        fw.op("act", lambda: nc.scalar.activation(out=Asb[k2].ap[:, :], in_=Wa[k2].ap[:, :], func=AF.Exp), reads=[Wa[k2].b0], writes=[Asb[k2].b0])
        fw.op("act", lambda: nc.scalar.copy(out=Zc[k2].ap[:, :], in_=Zb.ap[:, 0:H6]), reads=[Zb.b0], writes=[Zc[k2].b0])
        fw.op("pe", lambda: nc.tensor.matmul(m1.ap[:, 256:256 + H6], lhsT=ones_f.ap[:, :], rhs=Zc[k2].ap[:, :], start=True, stop=True), reads=[ones_f.b0, Zc[k2].b0], writes=[m1.b0])
        fw.op("dve", lambda: nc.vector.memset(G8[k2].ap[:, :], -1e30), writes=[G8[k2].b0])
        gv = m1.ap[:, 256:256 + H6].rearrange("p (h n two) -> p h n two", h=6, two=2)
        fw.op("dve", lambda: nc.vector.tensor_copy(out=G8[k2].ap[:, :].rearrange("p (h n) -> p h n", h=6)[:, :, 0:NBK].unsqueeze(3), in_=gv[:, :, :, 0:1]), reads=[m1.b0], writes=[G8[k2].b0])
        fw.op("dve", lambda: nc.vector.tensor_tensor(out=G8[k2].ap[:, :].rearrange("p (h n) -> p h n", h=6)[:, :, 0:NBK].unsqueeze(3),
                                                     in0=gv[:, :, :, 1:2], in1=G8[k2].ap[:, :].rearrange("p (h n) -> p h n", h=6)[:, :, 0:NBK].unsqueeze(3), op=ALU.add),
              reads=[m1.b0, G8[k2].b0], writes=[G8[k2].b0])
        for h in range(6):
            fw.op("dve", lambda h=h: nc.vector.max(out=Mx[k2].ap[:, h * 8:(h + 1) * 8], in_=G8[k2].ap[:, h * 8:(h + 1) * 8]), reads=[G8[k2].b0], writes=[Mx[k2].b0])
        for h in range(6):
            fw.op("dve", lambda h=h: nc.vector.tensor_scalar(out=Ng[k2].ap[:, h * 8:(h + 1) * 8], in0=G8[k2].ap[:, h * 8:(h + 1) * 8], scalar1=Mx[k2].ap[:, h * 8 + 2:h * 8 + 3],
                                                             scalar2=NEG, op0=ALU.is_lt, op1=ALU.mult), reads=[G8[k2].b0, Mx[k2].b0], writes=[Ng[k2].b0])
        fw.op("dve", lambda: nc.vector.tensor_tensor(out=Wm[k2].ap[:, :].rearrange("p (h n two) -> p h n two", h=6, two=2),
                                                     in0=Zb.ap[:, 0:H6].rearrange("p (h n two) -> p h n two", h=6, two=2),
                                                     in1=Ng[k2].ap[:, :].rearrange("p (h n) -> p h n", h=6)[:, :, 0:NBK].unsqueeze(3).to_broadcast([128, 6, NBK, 2]), op=ALU.add),
              reads=[Zb.b0, Ng[k2].b0], writes=[Wm[k2].b0])
        fw.op("act", lambda: nc.scalar.activation(out=Pm[k2].ap[:, :], in_=Wm[k2].ap[:, :], func=AF.Exp), reads=[Wm[k2].b0], writes=[Pm[k2].b0])

    def st2():
        Av = Asb[k2].ap[:, :].rearrange("p (h j) -> p h j", j=NPG)
        for c3 in range(3):
            for j in range(NPG):
                n_, e_ = j // 2, j % 2
                fw.op("pe", lambda: nc.tensor.matmul(Oall.ap[:, s * 6 + 2 * c3:s * 6 + 2 * c3 + 2], lhsT=V.ap[:, j, 384 + c3 * 128:384 + (c3 + 1) * 128],
                                                     rhs=Av[:, 2 * c3:2 * c3 + 2, 2 * (NBK - 1 - n_) + e_], start=(j == 0), stop=(j == NPG - 1)),
                      reads=[V.b[j], Asb[k2].b0], writes=[Oall.b0], inc=(j == NPG - 1))
        Pv = Pm[k2].ap[:, :].rearrange("p (h j) -> p h j", j=NPG)
        for c3 in range(3):
            for j in range(NPG):
                fw.op("pe", lambda: nc.tensor.matmul(Oall.ap[:, 6 * NS + s * 6 + 2 * c3:6 * NS + s * 6 + 2 * c3 + 2], lhsT=V.ap[:, j, c3 * 128:(c3 + 1) * 128],
                                                     rhs=Pv[:, 2 * c3:2 * c3 + 2, j], start=(j == 0), stop=(j == NPG - 1)),
                      reads=[V.b[j], Pm[k2].b0], writes=[Oall.b0], inc=(j == NPG - 1))
        for j in range(NPG):
            fw.op("pe", lambda: nc.tensor.matmul(Oall.ap[:, 12 * NS + s * 6:12 * NS + s * 6 + 6], lhsT=ones_b.ap[:, :], rhs=Pv[:, 0:6, j], start=(j == 0), stop=(j == NPG - 1)),
                  reads=[ones_b.b0, Pm[k2].b0], writes=[Oall.b0], inc=(j == NPG - 1))

    return [st0, st1, st2]


def core_in_map(inp, c, NT, NS, NPG, NPOOL, consts, depth=2):
    f32 = lambda a: np.ascontiguousarray(np.asarray(a, dtype=np.float32))
    m = {}
    m["x_p"] = f32(inp["x_prompt"][c]).reshape(NT * 128, D)
    m["x_s"] = f32(inp["x_sample"][c * NS:(c + 1) * NS]).reshape(NS, D)
    m["ck"] = inp["_ck"]
    m["cv"] = inp["_cv"]
    m["sconv"] = f32(inp["state_conv"][:, c * NS:(c + 1) * NS]).reshape(depth, NS * 30, 256)
    m["ptab"] = np.ascontiguousarray(np.asarray(inp["page_table"][c * NS:(c + 1) * NS], dtype=np.int32)).reshape(-1)
    for n in ("w_in", "w_out", "w_up", "w_down", "conv_w", "conv_b", "conv_g", "g_pre_mix", "g_post_mix", "g_pre_mlp", "g_post_mlp"):
        m[n] = inp["_" + n]
    for n, v in consts.items():
        m["c_" + n] = v
    return m


_NC_CACHE = {}


def run(inp, NT, NS, NPG, NPOOL, n_cores, do_sample=True, depth=2):
    key = (NT, NS, NPG, NPOOL, do_sample)
    if key not in _NC_CACHE:
        _NC_CACHE[key] = build(NT, NS, NPG, NPOOL, depth, do_sample)
    nc = _NC_CACHE[key]
    consts = make_consts(NT, NPG * 128)
    inp = dict(inp)
    f32 = lambda a: np.ascontiguousarray(np.asarray(a, dtype=np.float32))
    inp["_ck"] = f32(inp["cache_k"]).reshape(depth * NPOOL * 128, 768)
    inp["_cv"] = f32(inp["cache_v"]).reshape(depth * NPOOL * 128, 768)
    for n in ("w_in", "w_out", "w_up", "w_down", "conv_w", "conv_b", "conv_g", "g_pre_mix", "g_post_mix", "g_pre_mlp", "g_post_mlp"):
        inp["_" + n] = f32(inp[n])
    in_maps = [core_in_map(inp, c, NT, NS, NPG, NPOOL, consts, depth) for c in range(n_cores)]
    res = run_bass_kernel_spmd(nc, in_maps, core_ids=list(range(n_cores)))
    rs = res.results
    SP = NT * 128
    y_p = np.stack([r["y_p"] for r in rs]).reshape(n_cores, SP, D)
    y_s = np.concatenate([r["y_s"] for r in rs]).reshape(n_cores * NS, 1, D)
    kr_p = np.stack([r["kr_p"] for r in rs], axis=1).reshape(depth, n_cores, SP, 12, 64)
    vr_p = np.stack([r["vr_p"] for r in rs], axis=1).reshape(depth, n_cores, SP, 12, 64)
    cv_p = np.stack([r["cv_p"] for r in rs], axis=1).reshape(depth, n_cores, 30, 256)
    kr_s = np.concatenate([r["kr_s"] for r in rs], axis=1).reshape(depth, n_cores * NS, 1, 12, 64)
    vr_s = np.concatenate([r["vr_s"] for r in rs], axis=1).reshape(depth, n_cores * NS, 1, 12, 64)
    cv_s = np.concatenate([r["cv_s"].reshape(depth, NS, 30, 256) for r in rs], axis=1)
    return (y_p, y_s, kr_p, vr_p, cv_p, kr_s, vr_s, cv_s)


def kernel(x_prompt, x_sample, cache_k, cache_v, state_conv, page_table, w_in, w_out, conv_w, conv_b, conv_g,
           w_up, w_down, g_pre_mix, g_post_mix, g_pre_mlp, g_post_mlp):
    inp = dict(x_prompt=x_prompt, x_sample=x_sample, cache_k=cache_k, cache_v=cache_v, state_conv=state_conv,
               page_table=page_table, w_in=w_in, w_out=w_out, conv_w=conv_w, conv_b=conv_b, conv_g=conv_g,
               w_up=w_up, w_down=w_down, g_pre_mix=g_pre_mix, g_post_mix=g_post_mix, g_pre_mlp=g_pre_mlp, g_post_mlp=g_post_mlp)
    n_cores = 8
    NT = x_prompt.shape[1] // 128
    NS = x_sample.shape[0] // n_cores
    NPG = page_table.shape[1]
    NPOOL = cache_k.shape[1]
    return run(inp, NT, NS, NPG, NPOOL, n_cores)
```
